# Optimizing a Trainium2 kernel written in Bass

```python
import math
import jax, jax.numpy as jnp
from jax import lax
import numpy as np

D_MODEL = 1024
BATCH = 2
SEQ = 16384
DEPTH = 2

D_MIX = D_MODEL
GLA_HEADS = 4
GLA_DK = D_MODEL // 8
GLA_DV = 3 * D_MODEL // 16
GLA_RANK = 16
GLA_TAU = 16.0
GLA_CHUNK = 64
S5_WIDTH = D_MIX - GLA_HEADS * GLA_DV
S5_GROUP = 16
S5_GROUPS = S5_WIDTH // S5_GROUP
S5_STATE = 64
S5_DT_MIN = 1e-3
S5_DT_MAX = 1e-1
RWKV_HEAD = 64
RWKV_WIDTH = D_MIX // 2
RWKV_HEADS = RWKV_WIDTH // RWKV_HEAD
RWKV_DECAY_RANK = 64
RWKV_A_RANK = 64
RWKV_GATE_RANK = 128
RWKV_GN_EPS = 64e-5
MB_WIDTH = D_MIX - RWKV_WIDTH
MB_HEADDIM = 64
MB_HEADS = MB_WIDTH // MB_HEADDIM
MB_GROUPS = 2
MB_STATE = 128
MB_CONV = 4
MB_CHUNK = 128
D_FF = 2816
FFN_CONV = 3
EPS = 1e-6

GLA_COLS = 2 * GLA_HEADS * GLA_DK + 2 * GLA_HEADS * GLA_DV + GLA_RANK
EVEN_COLS = GLA_COLS + S5_WIDTH
RWKV_COLS = 3 * RWKV_WIDTH + RWKV_DECAY_RANK + RWKV_A_RANK + RWKV_GATE_RANK
MB_CONV_DIM = MB_WIDTH + 2 * MB_GROUPS * MB_STATE
MB_COLS = MB_WIDTH + MB_CONV_DIM + MB_HEADS
ODD_COLS = RWKV_COLS + MB_COLS

kernel_name = 'hybrid_gla_s5_rwkv7_mamba2_trunk'


def rms_norm(x, w):
    x32 = x.astype(jnp.float32)
    return (x32 * lax.rsqrt(jnp.mean(x32 * x32, -1, keepdims=True) + EPS)).astype(x.dtype) * w


def token_shift(p):
    return jnp.pad(p, ((0, 0), (1, 0), (0, 0)))[:, :-1]


def causal_dwconv(x, w, b):
    K, C = w.shape
    y = lax.conv_general_dilated(x, w.astype(x.dtype)[:, None, :], window_strides=(1,),
                                 padding=[(K - 1, 0)], dimension_numbers=('NWC', 'WIO', 'NWC'),
                                 feature_group_count=C)
    return y + b


def gla_chunked(q, k, v, log_a):
    Bs, L, H, DK = q.shape
    DV = v.shape[-1]
    nc = L // GLA_CHUNK

    def chunks(t):
        return t.reshape(Bs, nc, GLA_CHUNK, H, t.shape[-1]).transpose(1, 0, 3, 2, 4)

    causal = jnp.tril(jnp.ones((GLA_CHUNK, GLA_CHUNK), dtype=bool))[:, :, None]

    def step(S, inp):
        qc, kc, vc, gc = inp
        b = jnp.cumsum(gc, axis=2)
        diff = jnp.where(causal, b[:, :, :, None, :] - b[:, :, None, :, :], -jnp.inf)
        att = jnp.einsum('bhtk,bhsk,bhtsk->bhts', qc, kc, jnp.exp(diff))
        o = jnp.einsum('bhts,bhsv->bhtv', att, vc) + jnp.einsum('bhtk,bhkv->bhtv', qc * jnp.exp(b), S)
        b_last = b[:, :, -1, :]
        S = jnp.exp(b_last)[..., None] * S + jnp.einsum(
            'bhsk,bhsv->bhkv', kc * jnp.exp(b_last[:, :, None, :] - b), vc)
        return S, o

    S0 = jnp.zeros((Bs, H, DK, DV), jnp.float32)
    _, o = lax.scan(step, S0, (chunks(q), chunks(k), chunks(v), chunks(log_a)))
    return o.transpose(1, 0, 3, 2, 4).reshape(Bs, L, H, DV)


def gla_mixer(p, wa2, ba, norm_w):
    Bs, L, _ = p.shape
    f32 = jnp.float32
    nk = GLA_HEADS * GLA_DK
    nv = GLA_HEADS * GLA_DV
    q = p[..., :nk]
    k = p[..., nk:2 * nk]
    v = p[..., 2 * nk:2 * nk + nv]
    g = p[..., 2 * nk + nv:2 * nk + 2 * nv]
    alo = p[..., 2 * nk + 2 * nv:]
    log_a = jax.nn.log_sigmoid((alo @ wa2 + ba).astype(f32)) / GLA_TAU

    def heads(t):
        return t.astype(f32).reshape(Bs, L, GLA_HEADS, -1)

    o = gla_chunked(heads(q) * GLA_DK ** -0.5, heads(k), heads(v), heads(log_a))
    o = o * lax.rsqrt(jnp.mean(o * o, -1, keepdims=True) + EPS) * norm_w
    o = o * jax.nn.silu(heads(g))
    return o.reshape(Bs, L, nv).astype(p.dtype)


def s5_mixer(u, lam_re, lam_im, log_dt, b_re, b_im, c_re, c_im, d, w_glu, b_glu):
    Bs, L, _ = u.shape
    f32 = jnp.float32
    u32 = u.astype(f32)
    ug = u32.reshape(Bs, L, S5_GROUPS, S5_GROUP)
    lam_re = lam_re.astype(f32)
    lam_im = lam_im.astype(f32)
    dt = jnp.exp(log_dt.astype(f32))[:, None]
    mag = jnp.exp(lam_re * dt)
    ang = lam_im * dt
    lb_re = mag * jnp.cos(ang)
    lb_im = mag * jnp.sin(ang)
    den = lam_re * lam_re + lam_im * lam_im
    nr = lb_re - 1.0
    f_re = (nr * lam_re + lb_im * lam_im) / den
    f_im = (lb_im * lam_re - nr * lam_im) / den
    b_re = b_re.astype(f32)
    b_im = b_im.astype(f32)
    bb_re = f_re[..., None] * b_re - f_im[..., None] * b_im
    bb_im = f_re[..., None] * b_im + f_im[..., None] * b_re
    bu_re = jnp.einsum('blgc,gpc->lbgp', ug, bb_re)
    bu_im = jnp.einsum('blgc,gpc->lbgp', ug, bb_im)
    a_re = jnp.broadcast_to(lb_re[None, None], bu_re.shape)
    a_im = jnp.broadcast_to(lb_im[None, None], bu_re.shape)

    def combine(e1, e2):
        a1r, a1i, b1r, b1i = e1
        a2r, a2i, b2r, b2i = e2
        return (a2r * a1r - a2i * a1i,
                a2r * a1i + a2i * a1r,
                a2r * b1r - a2i * b1i + b2r,
                a2r * b1i + a2i * b1r + b2i)

    _, _, s_re, s_im = lax.associative_scan(combine, (a_re, a_im, bu_re, bu_im), axis=0)
    y = (jnp.einsum('lbgp,gcp->blgc', s_re, c_re.astype(f32))
         - jnp.einsum('lbgp,gcp->blgc', s_im, c_im.astype(f32)))
    y = y.reshape(Bs, L, S5_WIDTH) + d * u32
    z = jax.nn.gelu(y)
    out = z * jax.nn.sigmoid(z @ w_glu + b_glu)
    return out.astype(u.dtype)


def wkv7(r, w, k, v, a, b):
    Bs, L, H, N = r.shape

    def step(S, inp):
        rt, wt, kt, vt, at, bt = inp
        sa = jnp.einsum('bhij,bhj->bhi', S, at)
        S = S * wt[:, :, None, :] + sa[..., None] * bt[:, :, None, :] + vt[..., None] * kt[:, :, None, :]
        return S, jnp.einsum('bhij,bhj->bhi', S, rt)

    S0 = jnp.zeros((Bs, H, N, N), jnp.float32)
    seq = tuple(jnp.moveaxis(t, 1, 0) for t in (r, w, k, v, a, b))
    _, y = lax.scan(step, S0, seq)
    return jnp.moveaxis(y, 0, 1)


def rwkv7_mixer(p, mu, w0, w2, a0, a2, g2, k_k, k_a, r_k, ln_w, ln_b):
    Bs, L, _ = p.shape
    dtype = p.dtype
    f32 = jnp.float32
    p = p.astype(f32)
    p = p + (token_shift(p) - p) * mu
    W = RWKV_WIDTH
    o = 3 * W
    r = p[..., :W]
    k = p[..., W:2 * W]
    v = p[..., 2 * W:3 * W]
    xw = p[..., o:o + RWKV_DECAY_RANK]
    xa = p[..., o + RWKV_DECAY_RANK:o + RWKV_DECAY_RANK + RWKV_A_RANK]
    xg = p[..., o + RWKV_DECAY_RANK + RWKV_A_RANK:]
    w = -jax.nn.softplus(-(w0 + jnp.tanh(xw) @ w2)) - 0.5
    decay = jnp.exp(-jnp.exp(w))
    a = jax.nn.sigmoid(a0 + xa @ a2)
    g = jax.nn.sigmoid(xg) @ g2

    def heads(t):
        return t.reshape(Bs, L, RWKV_HEADS, RWKV_HEAD)

    kk = heads(k * k_k)
    kk = kk / jnp.maximum(jnp.sqrt(jnp.sum(kk * kk, -1, keepdims=True)), 1e-12)
    k = k * (1.0 + (a - 1.0) * k_a)
    r, k, v, decay, a = heads(r), heads(k), heads(v), heads(decay), heads(a)
    y = wkv7(r, decay, k, v, -kk, kk * a)
    mean = jnp.mean(y, -1, keepdims=True)
    var = jnp.mean(jnp.square(y - mean), -1, keepdims=True)
    y = (y - mean) * lax.rsqrt(var + RWKV_GN_EPS)
    y = y * ln_w.reshape(RWKV_HEADS, RWKV_HEAD) + ln_b.reshape(RWKV_HEADS, RWKV_HEAD)
    y = y + jnp.sum(r * k * r_k, -1, keepdims=True) * v
    return (y.reshape(Bs, L, W) * g).astype(dtype)


def ssd_chunked(x, a, bm, cm):
    Bs, L, H, P = x.shape
    G, N = bm.shape[-2], bm.shape[-1]
    E = H // G
    nc = L // MB_CHUNK
    x = x.reshape(Bs, nc, MB_CHUNK, G, E, P)
    a = a.reshape(Bs, nc, MB_CHUNK, G, E).transpose(0, 1, 3, 4, 2)
    bm = bm.reshape(Bs, nc, MB_CHUNK, G, N)
    cm = cm.reshape(Bs, nc, MB_CHUNK, G, N)
    acs = jnp.cumsum(a, -1)
    causal = jnp.tril(jnp.ones((MB_CHUNK, MB_CHUNK), dtype=bool))
    lmat = jnp.exp(jnp.where(causal, acs[..., :, None] - acs[..., None, :], -jnp.inf))
    cb = jnp.einsum('bclgn,bcsgn->bcgls', cm, bm)
    y_diag = jnp.einsum('bcgls,bcgels,bcsgep->bclgep', cb, lmat, x)
    decay_states = jnp.exp(acs[..., -1:] - acs)
    states = jnp.einsum('bclgn,bcgel,bclgep->bcgepn', bm, decay_states, x)
    chunk_decay = jnp.exp(acs[..., -1])

    def step(h, inp):
        st, dec = inp
        return dec[..., None, None] * h + st, h

    h0 = jnp.zeros((Bs, G, E, P, N), jnp.float32)
    _, prev = lax.scan(step, h0, (jnp.moveaxis(states, 1, 0), jnp.moveaxis(chunk_decay, 1, 0)))
    prev = jnp.moveaxis(prev, 0, 1)
    y_off = jnp.einsum('bclgn,bcgepn,bcgel->bclgep', cm, prev, jnp.exp(acs))
    return (y_diag + y_off).reshape(Bs, L, H, P)


def mamba2_mixer(p, conv_w, conv_b, dt_bias, a_log, d_skip, norm_w):
    Bs, L, _ = p.shape
    f32 = jnp.float32
    z = p[..., :MB_WIDTH].astype(f32)
    xbc = jax.nn.silu(causal_dwconv(p[..., MB_WIDTH:MB_WIDTH + MB_CONV_DIM], conv_w, conv_b)).astype(f32)
    dt = jax.nn.softplus(p[..., MB_WIDTH + MB_CONV_DIM:].astype(f32) + dt_bias)
    xs = xbc[..., :MB_WIDTH].reshape(Bs, L, MB_HEADS, MB_HEADDIM)
    nb = MB_GROUPS * MB_STATE
    bm = xbc[..., MB_WIDTH:MB_WIDTH + nb].reshape(Bs, L, MB_GROUPS, MB_STATE)
    cm = xbc[..., MB_WIDTH + nb:].reshape(Bs, L, MB_GROUPS, MB_STATE)
    a = -jnp.exp(a_log.astype(f32))
    y = ssd_chunked(xs * dt[..., None], dt * a, bm, cm) + xs * d_skip[:, None]
    y = y.reshape(Bs, L, MB_WIDTH) * jax.nn.silu(z)
    yg = y.reshape(Bs, L, MB_GROUPS, -1)
    yg = yg * lax.rsqrt(jnp.mean(yg * yg, -1, keepdims=True) + EPS)
    return (yg.reshape(Bs, L, MB_WIDTH) * norm_w).astype(p.dtype)


def conv_glu_ffn(h, w_up, conv_w, conv_b, w_down):
    u = h @ w_up
    gate = causal_dwconv(u[..., :D_FF], conv_w, conv_b)
    return (jax.nn.silu(gate) * u[..., D_FF:]) @ w_down


def setup_inputs(seed: int = 0) -> dict:
    key = jax.random.key(seed)
    ks = iter(jax.random.split(key, 64))
    f32 = jnp.float32
    NE = (DEPTH + 1) // 2
    NO = DEPTH // 2

    def nrm(shape, scale):
        return jax.random.normal(next(ks), shape, f32) * scale

    def uni(shape, lo, hi):
        return jax.random.uniform(next(ks), shape, f32, lo, hi)

    def gain(shape):
        return 1.0 + nrm(shape, 0.01)

    inp = {}
    inp['x'] = nrm((BATCH, SEQ, D_MODEL), 1.0)
    inp['norm_mix'] = gain((DEPTH, D_MODEL))
    inp['norm_ffn'] = gain((DEPTH, D_MODEL))
    inp['norm_final'] = gain((D_MODEL,))
    inp['e_w_in'] = nrm((NE, D_MODEL, EVEN_COLS), D_MODEL ** -0.5)
    inp['e_gla_wa2'] = nrm((NE, GLA_RANK, GLA_HEADS * GLA_DK), GLA_RANK ** -0.5)
    inp['e_gla_ba'] = nrm((NE, GLA_HEADS * GLA_DK), 0.1)
    inp['e_gla_norm'] = gain((NE, GLA_DV))
    n_idx = jnp.arange(S5_STATE, dtype=f32)
    inp['e_s5_lambda_re'] = -0.5 + nrm((NE, S5_GROUPS, S5_STATE), 0.01)
    inp['e_s5_lambda_im'] = jnp.pi * n_idx + nrm((NE, S5_GROUPS, S5_STATE), 0.01)
    inp['e_s5_log_dt'] = uni((NE, S5_GROUPS), math.log(S5_DT_MIN), math.log(S5_DT_MAX))
    inp['e_s5_b_re'] = nrm((NE, S5_GROUPS, S5_STATE, S5_GROUP), (2 * S5_GROUP) ** -0.5)
    inp['e_s5_b_im'] = nrm((NE, S5_GROUPS, S5_STATE, S5_GROUP), (2 * S5_GROUP) ** -0.5)
    inp['e_s5_c_re'] = nrm((NE, S5_GROUPS, S5_GROUP, S5_STATE), S5_STATE ** -0.5)
    inp['e_s5_c_im'] = nrm((NE, S5_GROUPS, S5_GROUP, S5_STATE), S5_STATE ** -0.5)
    inp['e_s5_d'] = nrm((NE, S5_WIDTH), 0.5)
    inp['e_s5_w_glu'] = nrm((NE, S5_WIDTH, S5_WIDTH), S5_WIDTH ** -0.5)
    inp['e_s5_b_glu'] = nrm((NE, S5_WIDTH), 0.01)
    inp['e_w_out'] = nrm((NE, D_MIX, D_MODEL), D_MIX ** -0.5)
    inp['o_w_in'] = nrm((NO, D_MODEL, ODD_COLS), D_MODEL ** -0.5)
    inp['o_rw_mu'] = uni((NO, RWKV_COLS), 0.0, 1.0)
    inp['o_rw_w0'] = uni((NO, RWKV_WIDTH), -6.0, -1.0)
    inp['o_rw_w2'] = nrm((NO, RWKV_DECAY_RANK, RWKV_WIDTH), 0.1)
    inp['o_rw_a0'] = nrm((NO, RWKV_WIDTH), 0.1)
    inp['o_rw_a2'] = nrm((NO, RWKV_A_RANK, RWKV_WIDTH), 0.1)
    inp['o_rw_g2'] = nrm((NO, RWKV_GATE_RANK, RWKV_WIDTH), RWKV_GATE_RANK ** -0.5)
    inp['o_rw_k_k'] = 0.85 + nrm((NO, RWKV_WIDTH), 0.02)
    inp['o_rw_k_a'] = 1.0 + nrm((NO, RWKV_WIDTH), 0.02)
    inp['o_rw_r_k'] = nrm((NO, RWKV_HEADS, RWKV_HEAD), 0.1)
    inp['o_rw_ln_w'] = gain((NO, RWKV_WIDTH))
    inp['o_rw_ln_b'] = nrm((NO, RWKV_WIDTH), 0.01)
    inp['o_mb_conv_w'] = nrm((NO, MB_CONV, MB_CONV_DIM), 0.5)
    inp['o_mb_conv_b'] = nrm((NO, MB_CONV_DIM), 0.01)
    dt0 = jnp.exp(uni((NO, MB_HEADS), math.log(1e-3), math.log(1e-1)))
    inp['o_mb_dt_bias'] = dt0 + jnp.log(-jnp.expm1(-dt0))
    inp['o_mb_a_log'] = jnp.log(uni((NO, MB_HEADS), 1.0, 16.0))
    inp['o_mb_d'] = gain((NO, MB_HEADS))
    inp['o_mb_norm'] = gain((NO, MB_WIDTH))
    inp['o_w_out'] = nrm((NO, D_MIX, D_MODEL), D_MIX ** -0.5)
    inp['ffn_w_up'] = nrm((DEPTH, D_MODEL, 2 * D_FF), D_MODEL ** -0.5)
    inp['ffn_conv_w'] = nrm((DEPTH, FFN_CONV, D_FF), FFN_CONV ** -0.5)
    inp['ffn_conv_b'] = nrm((DEPTH, D_FF), 0.01)
    inp['ffn_w_down'] = nrm((DEPTH, D_FF, D_MODEL), D_FF ** -0.5)
    return inp


def reference(x, norm_mix, norm_ffn, norm_final,
              e_w_in, e_gla_wa2, e_gla_ba, e_gla_norm,
              e_s5_lambda_re, e_s5_lambda_im, e_s5_log_dt, e_s5_b_re, e_s5_b_im,
              e_s5_c_re, e_s5_c_im, e_s5_d, e_s5_w_glu, e_s5_b_glu, e_w_out,
              o_w_in, o_rw_mu, o_rw_w0, o_rw_w2, o_rw_a0, o_rw_a2, o_rw_g2,
              o_rw_k_k, o_rw_k_a, o_rw_r_k, o_rw_ln_w, o_rw_ln_b,
              o_mb_conv_w, o_mb_conv_b, o_mb_dt_bias, o_mb_a_log, o_mb_d, o_mb_norm, o_w_out,
              ffn_w_up, ffn_conv_w, ffn_conv_b, ffn_w_down):
    for i in range(DEPTH):
        j = i // 2
        h = rms_norm(x, norm_mix[i])
        if i % 2 == 0:
            p = h @ e_w_in[j]
            ya = gla_mixer(p[..., :GLA_COLS], e_gla_wa2[j], e_gla_ba[j], e_gla_norm[j])
            yb = s5_mixer(p[..., GLA_COLS:], e_s5_lambda_re[j], e_s5_lambda_im[j], e_s5_log_dt[j],
                          e_s5_b_re[j], e_s5_b_im[j], e_s5_c_re[j], e_s5_c_im[j], e_s5_d[j],
                          e_s5_w_glu[j], e_s5_b_glu[j])
            x = x + jnp.concatenate([ya, yb], axis=-1) @ e_w_out[j]
        else:
            p = h @ o_w_in[j]
            yc = rwkv7_mixer(p[..., :RWKV_COLS], o_rw_mu[j], o_rw_w0[j], o_rw_w2[j], o_rw_a0[j],
                             o_rw_a2[j], o_rw_g2[j], o_rw_k_k[j], o_rw_k_a[j], o_rw_r_k[j],
                             o_rw_ln_w[j], o_rw_ln_b[j])
            yd = mamba2_mixer(p[..., RWKV_COLS:], o_mb_conv_w[j], o_mb_conv_b[j], o_mb_dt_bias[j],
                              o_mb_a_log[j], o_mb_d[j], o_mb_norm[j])
            x = x + jnp.concatenate([yc, yd], axis=-1) @ o_w_out[j]
        h = rms_norm(x, norm_ffn[i])
        x = x + conv_glu_ffn(h, ffn_w_up[i], ffn_conv_w[i], ffn_conv_b[i], ffn_w_down[i])
    return rms_norm(x, norm_final)
```

```python
import contextlib
import numpy as np
import concourse.bass as bass
import concourse.mybir as mybir
from concourse.bass_utils import run_bass_kernel_spmd

F32 = mybir.dt.float32
BF16 = mybir.dt.bfloat16
ALU = mybir.AluOpType
AF = mybir.ActivationFunctionType
AX = mybir.AxisListType


class Lane:
    def __init__(self, nc, name, inc):
        self.sem = nc.alloc_semaphore(name)
        self.name = name
        self.inc = inc
        self.count = 0


class Buf:
    def __init__(self, h, name, psum=False):
        self.h = h
        self.name = name
        self.psum = psum
        self.last_w = None
        self.readers = {}

    def __getitem__(self, key):
        return V(self, self.h[key])


class V:
    def __init__(self, buf, ap):
        self.buf = buf
        self.ap = ap

    def __getitem__(self, key):
        return V(self.buf, self.ap[key])

    def rearrange(self, s, **kw):
        return V(self.buf, self.ap.rearrange(s, **kw))

    def bcast(self, shape):
        return V(self.buf, self.ap.to_broadcast(shape))

    def bitcast(self, dt):
        return V(self.buf, self.ap.bitcast(dt))


class KB:
    def __init__(self, nc):
        self.nc = nc
        self.eng = {"pe": nc.tensor, "act": nc.scalar, "dve": nc.vector, "pool": nc.gpsimd, "sp": nc.sync}
        self.lanes = {n: Lane(nc, "L" + n, 1) for n in ("pe", "act", "dve", "pool")}
        self.waited = {}
        self.rings = {}
        self.stack = contextlib.ExitStack()
        self.cclane = Lane(nc, "Lcc", 1)
        self.pid = {}
        self.rec = None
        self.nbuf = 0
        self.ninst = 0

    def sb(self, shape, dt=F32, name=None):
        self.nbuf += 1
        name = (name or "t") + "_%d" % self.nbuf
        return Buf(self.stack.enter_context(self.nc.sbuf_tensor(name, list(shape), dt)), name)

    def ps(self, shape, dt=F32, name=None):
        self.nbuf += 1
        name = (name or "p") + "_%d" % self.nbuf
        return Buf(self.stack.enter_context(self.nc.psum_tensor(name, list(shape), dt)), name, psum=True)

    def dram(self, name, shape, dt=F32, kind="Internal"):
        h = self.nc.dram_tensor(name, list(shape), dt, kind=kind)
        return Buf(h.ap(), name)

    RING = 8

    def dma_lane(self, name, ename):
        if name not in self.rings:
            self.rings[name] = [[Lane(self.nc, "D%s_%d" % (name, i), 16) for i in range(self.RING)], 0]
        ring = self.rings[name]
        lane = ring[0][ring[1] % self.RING]
        ring[1] += 1
        if lane.count > 0:
            wk = (ename, lane.name)
            if self.waited.get(wk, 0) < lane.count:
                self.waited[wk] = lane.count
                self.eng[ename].wait_ge(lane.sem, lane.count * lane.inc)
                self.ninst += 1
        return lane

    def _need(self, ename, deps, lane, cnt):
        if cnt <= 0:
            return
        key = lane.name
        if deps.get(key, (None, 0))[1] < cnt:
            deps[key] = (lane, cnt)

    def record(self):
        self.rec = []
        return self.rec

    def stop_record(self):
        self.rec = None

    def emit_merged(self, A, B):
        la, lb = len(A), len(B)
        ia = ib = 0
        while ia < la or ib < lb:
            if ib >= lb or (ia < la and ia * lb <= ib * la):
                self.op(*A[ia]); ia += 1
            else:
                self.op(*B[ib]); ib += 1

    def op(self, ename, fn, reads=(), writes=(), lane=None, pe_acc=False):
        if self.rec is not None:
            self.rec.append((ename, fn, list(reads), list(writes), lane, pe_acc))
            return None
        e = self.eng[ename]
        if lane is None:
            lane = self.lanes[ename]
        elif isinstance(lane, str):
            lane = self.dma_lane(lane, ename)
        deps = {}
        rb = []
        for r in reads:
            b = r.buf if isinstance(r, V) else r
            if b is None:
                continue
            rb.append(b)
            if b.last_w is not None:
                self._need(ename, deps, *b.last_w)
            if b.psum:
                for ln, (l, c) in b.readers.items():
                    if l is not lane:
                        self._need(ename, deps, l, c)
        wb = []
        for w in writes:
            b = w.buf if isinstance(w, V) else w
            wb.append(b)
            if b.last_w is not None:
                if not (pe_acc and b.last_w[0] is lane):
                    self._need(ename, deps, *b.last_w)
            for ln, (l, c) in b.readers.items():
                self._need(ename, deps, l, c)
        for key, (l, c) in deps.items():
            wk = (ename, key)
            if self.waited.get(wk, 0) >= c:
                continue
            self.waited[wk] = c
            e.wait_ge(l.sem, c * l.inc)
            self.ninst += 1
        ins = fn(e)
        lane.count += 1
        ins.then_inc(lane.sem, lane.inc)
        self.ninst += 1
        for b in wb:
            b.last_w = (lane, lane.count)
            b.readers = {}
        for b in rb:
            if b in wb:
                continue
            b.readers[lane.name] = (lane, lane.count)
        return ins

    def dma(self, out, in_, q="sp", lane="ld", **kw):
        return self.op(q, lambda e: e.dma_start(out=out.ap, in_=in_.ap, **kw), reads=[in_], writes=[out], lane=lane)

    def mm(self, out, lhsT, rhs, start=True, stop=True, **kw):
        return self.op("pe", lambda e: e.matmul(out.ap, lhsT.ap, rhs.ap, start=start, stop=stop, **kw),
                       reads=[lhsT, rhs], writes=[out], pe_acc=not start)

    def transpose(self, out, in_, ident):
        return self.op("pe", lambda e: e.transpose(out.ap, in_.ap, ident.ap), reads=[in_, ident], writes=[out])

    def act(self, out, in_, func, bias=None, scale=None, accum=None, eng="act"):
        kw = {}
        reads = [in_]
        if bias is not None:
            if isinstance(bias, V):
                kw["bias"] = bias.ap
                reads.append(bias)
            else:
                kw["bias"] = bias
        if scale is not None:
            if isinstance(scale, V):
                kw["scale"] = scale.ap
                reads.append(scale)
            else:
                kw["scale"] = scale
        writes = [out]
        if accum is not None:
            kw["accum_out"] = accum.ap
            writes.append(accum)
        return self.op(eng, lambda e: e.activation(out.ap, in_.ap, func, **kw), reads=reads, writes=writes)

    def tt(self, out, in0, in1, op, eng="dve"):
        return self.op(eng, lambda e: e.tensor_tensor(out.ap, in0.ap, in1.ap, op), reads=[in0, in1], writes=[out])

    def ts(self, out, in0, s1, op0, s2=None, op1=None, eng="dve", accum=None):
        reads = [in0]
        a1 = s1
        a2 = s2
        if isinstance(s1, V):
            reads.append(s1)
            a1 = s1.ap
        if isinstance(s2, V):
            reads.append(s2)
            a2 = s2.ap
        kw = {}
        writes = [out]
        if accum is not None:
            kw["accum_out"] = accum.ap
            writes.append(accum)
        if op1 is None:
            return self.op(eng, lambda e: e.tensor_scalar(out.ap, in0.ap, a1, None, op0, **kw), reads=reads, writes=writes)
        return self.op(eng, lambda e: e.tensor_scalar(out.ap, in0.ap, a1, a2, op0, op1, **kw), reads=reads, writes=writes)

    def stt(self, out, in0, scalar, in1, op0, op1, eng="dve"):
        reads = [in0, in1]
        a = scalar
        if isinstance(scalar, V):
            reads.append(scalar)
            a = scalar.ap
        return self.op(eng, lambda e: e.scalar_tensor_tensor(out.ap, in0.ap, a, in1.ap, op0, op1), reads=reads, writes=[out])

    def scan(self, out, d0, d1, init, op0=ALU.mult, op1=ALU.add, eng="dve"):
        reads = [d0, d1]
        a = init
        if isinstance(init, V):
            reads.append(init)
            a = init.ap
        return self.op(eng, lambda e: e.tensor_tensor_scan(out.ap, d0.ap, d1.ap, a, op0, op1), reads=reads, writes=[out])

    def copy(self, out, in_, eng="dve"):
        if eng == "act":
            return self.op("act", lambda e: e.copy(out.ap, in_.ap), reads=[in_], writes=[out])
        return self.op(eng, lambda e: e.tensor_copy(out.ap, in_.ap), reads=[in_], writes=[out])

    def memset(self, out, val, eng="pool"):
        return self.op(eng, lambda e: e.memset(out.ap, val), reads=[], writes=[out])

    def all_lanes(self):
        ls = list(self.lanes.values()) + [self.cclane]
        for name, (lanes, _) in self.rings.items():
            ls.extend(lanes)
        return ls

    def end_phase(self):
        for ename, e in self.eng.items():
            for l in self.all_lanes():
                if l.count > 0 and self.waited.get((ename, l.name), 0) < l.count:
                    self.waited[(ename, l.name)] = l.count
                    e.wait_ge(l.sem, l.count * l.inc)
                    self.ninst += 1
        self.stack.close()
        self.stack = contextlib.ExitStack()

    def allgather(self, dst, src, groups):
        return self.op("pool", lambda e: e.collective_compute("AllGather", ALU.bypass, replica_groups=groups,
                                                              ins=[src[:].ap.opt()], outs=[dst[:].ap.opt()]),
                       reads=[src], writes=[dst], lane=self.cclane)

    def dyn_dma(self, out, buf, row0, nrows, col_static, n, qscale, ename="sp", lane="ld"):
        e = self.eng[ename]
        key = (ename, qscale)
        if key not in self.pid:
            qreg = e.to_reg((e.partition_id() % 4) * qscale)
            self.pid[key] = (qreg, e.alloc_register("dynoff_%s_%d" % (ename, qscale)))
        qreg, r = self.pid[key]
        tens = buf.h.tensor
        rowlen = buf.h.shape[1]

        def fn(e_):
            e_.reg_add(r, qreg, row0 * rowlen + col_static)
            return e_.dma_start(out=out.ap, in_=bass.AP(tens, r, [[rowlen, nrows], [1, n]]))
        return self.op(ename, fn, reads=[buf], writes=[out], lane=lane)

    def dyn_dma3(self, out_ap, out_buf, buf, qscale, static_off, pattern, ename="sp", lane="dyn"):
        e = self.eng[ename]
        key = (ename, qscale)
        if key not in self.pid:
            qreg = e.to_reg((e.partition_id() % 4) * qscale)
            self.pid[key] = (qreg, e.alloc_register("dynoff_%s_%d" % (ename, qscale)))
        qreg, r = self.pid[key]
        tens = buf.h.tensor

        def fn(e_):
            e_.reg_add(r, qreg, static_off)
            return e_.dma_start(out=out_ap, in_=bass.AP(tens, r, pattern))
        return self.op(ename, fn, reads=[buf], writes=[out_buf], lane=lane)

    def wait_ring(self, ename, ring):
        if ring not in self.rings:
            return
        e = self.eng[ename]
        for l in self.rings[ring][0]:
            if l.count > 0 and self.waited.get((ename, l.name), 0) < l.count:
                self.waited[(ename, l.name)] = l.count
                e.wait_ge(l.sem, l.count * l.inc)
                self.ninst += 1

    def core_q(self, ename, mod):
        key = (ename, mod)
        if key not in self.pid:
            self.pid[key] = self.eng[ename].partition_id() % mod
        return self.pid[key]

    def finish(self, bufs):
        self.end_phase()

import math

D = 1024
EPS = 1e-6
TWO_PI = 2.0 * math.pi
MAGIC = 12582912.0


def cast_w(k, dst, src, ncols, nmix, stage, cnt=[0]):
    engs = ("act", "dve", "pool")
    piece = stage[0].h.shape[1]
    for kc in range(8):
        for c0 in range(0, ncols, piece):
            n = min(piece, ncols - c0)
            st = stage[cnt[0] % len(stage)]
            eng = engs[cnt[0] % 3]
            cnt[0] += 1
            k.dma(st[:, 0:n], src[kc * 128:(kc + 1) * 128, c0:c0 + n], q="sp", lane="wld")
            if eng == "act":
                k.op("act", lambda e: e.mul(dst[kc][:, c0:c0 + n].ap, st[:, 0:n].ap, nmix[:, kc:kc + 1].ap), reads=[st, nmix], writes=[dst[kc]])
            else:
                k.ts(dst[kc][:, c0:c0 + n], st[:, 0:n], nmix[:, kc:kc + 1], ALU.mult, eng=eng)


def make_consts(k):
    ident = k.sb([128, 128], F32, "ident"); k.memset(ident[:], 0.0)
    k.op("pool", lambda e: e.affine_select(out=ident[:].ap, in_=ident[:].ap, compare_op=ALU.not_equal, fill=1.0, base=0,
                                           pattern=[[-1, 128]], channel_multiplier=1), reads=[ident], writes=[ident])
    identb = k.sb([128, 128], BF16, "identb"); k.copy(identb[:], ident[:])
    maskT = k.sb([128, 128], F32, "maskT"); k.memset(maskT[:], 1.0)
    k.op("pool", lambda e: e.affine_select(out=maskT[:].ap, in_=maskT[:].ap, compare_op=ALU.is_ge, fill=0.0, base=0,
                                           pattern=[[1, 128]], channel_multiplier=-1), reads=[maskT], writes=[maskT])
    ones = k.sb([128, 128], F32, "ones"); k.memset(ones[:], 1.0)
    onesb = k.sb([128, 128], BF16, "onesb"); k.memset(onesb[:], 1.0)
    return ident, identb, maskT, ones, onesb


def rsqrt_small(k, out, in_, mul, add):
    k.ts(out, in_, mul, ALU.mult, add, ALU.add)
    k.act(out, out, AF.Ln)
    k.act(out, out, AF.Exp, scale=-0.5)


def recip_1p(k, t):
    k.ts(t, t, 1.0, ALU.add)
    k.op("dve", lambda e: e.reciprocal(t.ap, t.ap), reads=[t], writes=[t])


def emit_A0(k, L, io, TB=512, PAD=4):
    skip = ()
    NB = L // TB
    xT = io["xT"]; w_fm = io["w_fm"]; w_tm = io["w_tm"]; nmix_d = io["nmix"]; wa2_d = io["wa2"]; nba_d = io["nba"]
    gn_d = io["gnorm"]; s5col_d = io["s5col"]; s5b_d = io["s5b"]; s5c_d = io["s5c"]; s5d_d = io["s5d"]
    P_o = io["P"]; PT = io["PT"]

    ident, identb, maskT, ones, onesb = make_consts(k)
    yaTa = [k.sb([128, TB], F32, "yaTa") for _ in range(2)]
    yaTb = [k.sb([64, TB], F32, "yaTb") for _ in range(2)]
    nmix = k.sb([128, 8]); k.dma(nmix[:], nmix_d[:], lane="wld")
    wa2 = k.sb([16, 128]); k.dma(wa2[:], wa2_d[:], lane="wld")
    nba = k.sb([128, 1]); k.dma(nba[:], nba_d[:], lane="wld")
    k.ts(nba[:], nba[:], -1.0, ALU.mult)
    gn = k.sb([128, 192]); k.dma(gn[:], gn_d[:], lane="wld")
    s5col = k.sb([128, 6]); k.dma(s5col[:], s5col_d[:], lane="wld")
    s5b = k.sb([128, 256]); k.dma(s5b[:], s5b_d[:], lane="wld")
    s5c = k.sb([128, 256]); k.dma(s5c[:], s5c_d[:], lane="wld")
    s5d = k.sb([64, 1]); k.dma(s5d[:], s5d_d[:], lane="wld")

    stage = [k.sb([128, 384], F32, "stage") for _ in range(3)]
    wfm = [k.sb([128, 336], BF16, "wfm") for _ in range(8)]
    wtm = [k.sb([128, 384], BF16, "wtm") for _ in range(8)]
    cast_w(k, wfm, w_fm, 336, nmix, stage)
    cast_w(k, wtm, w_tm, 384, nmix, stage)

    PS = [k.ps([128, 512], F32, "ps") for _ in range(7)]
    PSB = k.ps([128, 1024], BF16, "psb")
    psi = [0]

    psel = [None]
    psa = [0]; psb = [0]
    NA = len(PS) - 3

    def nps():
        if psel[0] == "A":
            p = PS[psa[0] % NA]; psa[0] += 1
        elif psel[0] == "B":
            p = PS[NA + psb[0] % 3]; psb[0] += 1
        else:
            p = PS[psi[0] % len(PS)]; psi[0] += 1
        return p

    cosT = []; sinT = []; rho = []; BbT = []; Cbd = []
    sm = k.sb([128, 32], F32, "s5small")
    for st in (range(2) if 's5setup' not in skip else []):
        lre = s5col[:, st * 3 + 0:st * 3 + 1]; lim = s5col[:, st * 3 + 1:st * 3 + 2]; ldt = s5col[:, st * 3 + 2:st * 3 + 3]
        c = lambda i: sm[:, i:i + 1]
        dt, a, ang, mag, nrd, angr, sn, cs_, tmp, lbr, lbi, nr, den, fre, fim, t2 = [c(i) for i in range(16)]
        k.act(dt, ldt, AF.Exp)
        k.tt(a, lre, dt, ALU.mult)
        k.tt(ang, lim, dt, ALU.mult)
        k.act(mag, a, AF.Exp)
        k.ts(nrd, ang, 1.0 / TWO_PI, ALU.mult)
        k.ts(nrd, nrd, MAGIC, ALU.add)
        k.ts(nrd, nrd, -MAGIC, ALU.add)
        k.stt(angr, nrd, -TWO_PI, ang, ALU.mult, ALU.add)
        k.ts(angr, angr, math.pi, ALU.min, -math.pi, ALU.max)
        k.act(sn, angr, AF.Sin)
        k.ts(tmp, angr, -1.0, ALU.mult); k.tt(tmp, tmp, angr, ALU.max)
        k.ts(tmp, tmp, -1.0, ALU.mult, math.pi / 2, ALU.add)
        k.act(cs_, tmp, AF.Sin)
        k.tt(lbr, mag, cs_, ALU.mult)
        k.tt(lbi, mag, sn, ALU.mult)
        k.ts(nr, lbr, -1.0, ALU.add)
        k.tt(den, lre, lre, ALU.mult)
        k.stt(den, lim, lim, den, ALU.mult, ALU.add)
        k.op("dve", lambda e: e.reciprocal(den.ap, den.ap), reads=[den], writes=[den])
        k.tt(fre, nr, lre, ALU.mult)
        k.stt(fre, lbi, lim, fre, ALU.mult, ALU.add)
        k.tt(fre, fre, den, ALU.mult)
        k.tt(fim, lbi, lre, ALU.mult)
        k.tt(t2, nr, lim, ALU.mult)
        k.tt(fim, fim, t2, ALU.subtract)
        k.tt(fim, fim, den, ALU.mult)
        r_ = k.sb([128, TB], F32, "rho"); k.ts(r_[:], ones[:, 0:1].bcast([128, TB]), mag, ALU.mult)
        rho.append(r_)
        ct = k.sb([128, TB], F32, "cosT"); stb = k.sb([128, TB], F32, "sinT")
        k.copy(ct[:, 0:1], cs_); k.copy(stb[:, 0:1], sn)
        n = 1
        tA = k.sb([128, TB // 2], F32, "tA")
        while n < TB:
            cr = ct[:, n - 1:n]; si = stb[:, n - 1:n]
            k.ts(tA[:, 0:n], stb[:, 0:n], si, ALU.mult)
            k.stt(ct[:, n:2 * n], ct[:, 0:n], cr, tA[:, 0:n], ALU.mult, ALU.subtract)
            k.ts(tA[:, 0:n], stb[:, 0:n], cr, ALU.mult)
            k.stt(stb[:, n:2 * n], ct[:, 0:n], si, tA[:, 0:n], ALU.mult, ALU.add)
            n *= 2
        cosT.append(ct); sinT.append(stb)
        bre = s5b[:, st * 128:st * 128 + 64]; bim = s5b[:, st * 128 + 64:st * 128 + 128]
        bb = k.sb([128, 128], F32, "bb")
        k.ts(bb[:, 0:64], bim, fim, ALU.mult)
        k.stt(bb[:, 0:64], bre, fre, bb[:, 0:64], ALU.mult, ALU.subtract)
        k.ts(bb[:, 64:128], bre, fim, ALU.mult)
        k.stt(bb[:, 64:128], bim, fre, bb[:, 64:128], ALU.mult, ALU.add)
        bt = k.sb([64, 256], F32, "BbT")
        for ri in range(2):
            p = nps()
            k.transpose(p[0:64, 0:128], bb[:, ri * 64:(ri + 1) * 64], ident[:])
            k.copy(bt[:, ri * 128:(ri + 1) * 128], p[0:64, 0:128])
        BbT.append(bt)
        cb_ = k.sb([128, 128], F32, "Cbd")
        k.copy(cb_[:, 0:64], s5c[:, st * 128:st * 128 + 64])
        k.ts(cb_[:, 64:128], s5c[:, st * 128 + 64:st * 128 + 128], -1.0, ALU.mult)
        Cbd.append(cb_)
    s5carry = [[k.sb([128, 1], F32, "s5carry") for _ in range(2)] for _ in range(2)]
    for st in range(2):
        for ri in range(2):
            k.memset(s5carry[st][ri][:], 0.0)

    S = k.sb([128, 192], F32, "S"); k.memset(S[:], 0.0)
    Sb = k.sb([128, 192], BF16, "Sb"); k.memset(Sb[:], 0.0)
    cmask = k.sb([128, TB], F32, "cmask"); k.memset(cmask[:], 1.0)
    for c in range(TB // 128):
        k.memset(cmask[:, c * 128:c * 128 + 1], 0.0)

    xs = [k.sb([128, TB], F32, "xs") for _ in range(8)]
    sqb = [k.sb([128, TB], BF16, "sqb") for _ in range(2)]
    hb = [k.sb([128, TB], BF16, "hb") for _ in range(8)]
    rstd = k.sb([128, TB], F32, "rstd")
    alo = k.sb([16, TB], F32, "alo")
    lsp = k.sb([128, TB], F32, "lsp")
    cs = k.sb([128, TB], F32, "cs")
    eb = k.sb([128, TB], F32, "eb")
    enb = k.sb([128, TB], F32, "enb")
    qt = k.sb([128, TB], BF16, "qt")
    kt = k.sb([128, TB], BF16, "kt")
    ksb = k.sb([128, TB], F32, "ksb")
    ncl = k.sb([128, 4], F32, "ncl")
    vb = [k.sb([128, 192], BF16, "vb") for _ in range(4)]
    gsl = [k.sb([128, 192], F32, "gsl") for _ in range(4)]
    ekh = [k.sb([128, 128], F32, "ekh") for _ in range(2)]
    khT = [k.sb([128, 128], BF16, "khT") for _ in range(2)]
    kh = [k.sb([128, 128], BF16, "kh") for _ in range(2)]
    att = [k.sb([128, 128], BF16, "att") for _ in range(2)]
    osq = k.sb([128, 192], F32, "osq")
    ssq = [k.sb([128, 1], F32, "ssq") for _ in range(2)]
    yo = [k.sb([128, 192], F32, "yo") for _ in range(2)]
    us = k.sb([64, TB], F32, "us")
    burs = [[k.sb([128, TB], F32, "bur") for _ in range(2)] for _ in range(2)]
    xrs = [[k.sb([128, TB], F32, "xr") for _ in range(2)] for _ in range(2)]
    t1p = [k.sb([128, TB], F32, "t1p") for _ in range(2)]
    t1d = [k.sb([128, TB], F32, "t1d") for _ in range(2)]
    wscs = [[k.sb([128, TB], F32, "wsc") for _ in range(2)] for _ in range(2)]
    sre = [[k.sb([128, TB], F32, "sre") for _ in range(2)] for _ in range(2)]
    yz = k.sb([64, TB], F32, "yz")
    gz = [k.sb([64, TB], F32, "gz") for _ in range(2)]
    zo = [k.sb([64, TB], F32, "zo") for _ in range(2)]

    for bi in range(NB):
        t0 = bi * TB
        pss = nps()
        for kc in range(8):
            k.dma(xs[kc][:], xT[kc * 128:(kc + 1) * 128, t0:t0 + TB], q="sp", lane="xld")
            k.act(sqb[kc % 2][:], xs[kc][:], AF.Square)
            k.mm(pss[:], onesb[:], sqb[kc % 2][:], start=(kc == 0), stop=(kc == 7))
        rsqrt_small(k, rstd[:], pss[:], 1.0 / D, EPS)
        for kc in range(8):
            k.tt(hb[kc][:], xs[kc][:], rstd[:], ALU.mult, eng=("pool" if kc % 2 else "dve"))
        pq = nps(); pk = nps(); pu = nps(); pa = nps()
        for (p, c0, m) in ((pq, 0, 128), (pk, 128, 128), (pu, 256, 64), (pa, 320, 16)):
            for kc in range(8):
                k.mm(p[0:m, :], wfm[kc][:, c0:c0 + m], hb[kc][:], start=(kc == 0), stop=(kc == 7))
        if 'gates' in skip:
            continue
        k.copy(alo[:], pa[0:16, :], eng="act")
        if 'g1' in skip:
            continue
        pl = nps()
        k.mm(pl[:], wa2[:], alo[:])
        k.act(lsp[:], pl[:], AF.Exp, scale=-1.0, bias=nba[:, 0:1])
        k.act(lsp[:], lsp[:], AF.Ln, bias=1.0)
        if 'g2' in skip:
            continue
        k.scan(cs[:], cmask[:], lsp[:], 0.0)
        k.act(eb[:], cs[:], AF.Exp, scale=-1.0 / 16.0)
        k.act(enb[:], cs[:], AF.Exp, scale=1.0 / 16.0)
        k.stt(qt[:], pq[:], 128.0 ** -0.5, eb[:], ALU.mult, ALU.mult)
        k.tt(kt[:], pk[:], enb[:], ALU.mult)
        k.copy(ksb[:], pk[:], eng="act")
        for c in range(4):
            k.ts(ncl[:, c:c + 1], cs[:, c * 128 + 127:c * 128 + 128], -1.0 / 16.0, ALU.mult)
        if 'g3' in skip:
            continue
        k.copy(us[:], pu[0:64, :], eng="act")
        if 'g4' in skip:
            continue
        for s in range(4):
            pv = nps()
            for kc in range(8):
                k.mm(pv[:, 0:384], hb[kc][:, s * 128:(s + 1) * 128], wtm[kc][:], start=(kc == 0), stop=(kc == 7))
            k.copy(vb[s][:], pv[:, 0:192], eng="act")
            k.act(gsl[s][:], pv[:, 192:384], AF.Exp, scale=-1.0)
            recip_1p(k, gsl[s][:])
            k.tt(gsl[s][:], gsl[s][:], gn[:], ALU.mult, eng="pool")
            k.tt(gsl[s][:], gsl[s][:], pv[:, 192:384], ALU.mult)
        recA = k.record(); psel[0] = "A"
        for c in (range(4) if 'gla' not in skip else []):
            sl = slice(c * 128, (c + 1) * 128)
            i2 = c % 2
            k.act(ekh[i2][:], cs[:, sl], AF.Exp, scale=1.0 / 16.0, bias=ncl[:, c:c + 1])
            k.tt(khT[i2][:], ksb[:, sl], ekh[i2][:], ALU.mult, eng="pool")
            k.transpose(PSB[:, i2 * 128:(i2 + 1) * 128], khT[i2][:], identb[:])
            k.copy(kh[i2][:], PSB[:, i2 * 128:(i2 + 1) * 128], eng="act")
            pa_ = nps()
            k.mm(pa_[:, 0:128], kt[:, sl], qt[:, sl])
            k.tt(att[i2][:], pa_[:, 0:128], maskT[:], ALU.mult)
            po = nps()
            k.mm(po[:, 0:192], att[i2][:], vb[c][:], start=True, stop=False)
            k.mm(po[:, 0:192], qt[:, sl], Sb[:], start=False, stop=True)
            pst = nps()
            k.mm(pst[:, 0:192], kh[i2][:], vb[c][:])
            k.stt(S[:], S[:], eb[:, c * 128 + 127:c * 128 + 128], pst[:, 0:192], ALU.mult, ALU.add)
            k.copy(Sb[:], S[:], eng="act")
            k.act(osq[:], po[:, 0:192], AF.Square, accum=ssq[i2][:])
            rsqrt_small(k, ssq[i2][:], ssq[i2][:], 1.0 / 192.0, EPS)
            k.stt(yo[i2][:], po[:, 0:192], ssq[i2][:, 0:1], gsl[c][:], ALU.mult, ALU.mult)
            pt1 = nps(); k.transpose(pt1[:, 0:128], yo[i2][:, 0:128], ident[:])
            k.copy(yaTa[bi % 2][:, sl], pt1[:, 0:128], eng="act")
            pt2 = nps(); k.transpose(pt2[0:64, 0:128], yo[i2][:, 128:192], ident[:])
            k.copy(yaTb[bi % 2][:, sl], pt2[0:64, 0:128], eng="act")
            if c == 3:
                pr0 = (t0 // PT) * 256; pc0 = t0 % PT
                k.dma(P_o[pr0:pr0 + 128, pc0:pc0 + TB], yaTa[bi % 2][:], q="pool", lane="st")
                k.dma(P_o[pr0 + 128:pr0 + 192, pc0:pc0 + TB], yaTb[bi % 2][:], q="pool", lane="st")
        recB = k.record(); psel[0] = "B"
        for st in range(2):
            pbr = nps(); pbi = nps()
            k.mm(pbr[:], BbT[st][:, 0:128], us[:])
            k.mm(pbi[:], BbT[st][:, 128:256], us[:])
            bur = burs[st]; xr = xrs[st]; wsc = wscs[st]
            k.copy(bur[0][:], pbr[:], eng="act")
            k.copy(bur[1][:], pbi[:], eng="act")
            k.tt(t1p[0][:], bur[0][:], cosT[st][:], ALU.mult, eng="pool")
            k.tt(t1p[1][:], bur[1][:], sinT[st][:], ALU.mult, eng="pool")
            k.tt(xr[0][:], t1p[0][:], t1p[1][:], ALU.add, eng="pool")
            k.tt(t1d[0][:], bur[1][:], cosT[st][:], ALU.mult)
            k.tt(t1d[1][:], bur[0][:], sinT[st][:], ALU.mult)
            k.tt(xr[1][:], t1d[0][:], t1d[1][:], ALU.subtract)
            for ri in range(2):
                k.scan(wsc[ri][:], rho[st][:], xr[ri][:], s5carry[st][ri][:, 0:1])
            k.tt(t1p[0][:], wsc[0][:], cosT[st][:], ALU.mult, eng="pool")
            k.tt(t1p[1][:], wsc[1][:], sinT[st][:], ALU.mult, eng="pool")
            k.tt(sre[st][0][:], t1p[0][:], t1p[1][:], ALU.subtract, eng="pool")
            k.tt(t1d[0][:], wsc[0][:], sinT[st][:], ALU.mult)
            k.tt(t1d[1][:], wsc[1][:], cosT[st][:], ALU.mult)
            k.tt(sre[st][1][:], t1d[0][:], t1d[1][:], ALU.add)
            for ri in range(2):
                k.copy(s5carry[st][ri][:], sre[st][ri][:, TB - 1:TB], eng="act")
        py = nps()
        for st in range(2):
            for ri in range(2):
                k.mm(py[0:64, :], Cbd[st][:, ri * 64:(ri + 1) * 64], sre[st][ri][:], start=(st == 0 and ri == 0), stop=(st == 1 and ri == 1))
        k.stt(yz[:], us[:], s5d[:, 0:1], py[0:64, :], ALU.mult, ALU.add)
        g0 = gz[0]; g1 = gz[1]
        k.act(g0[:], yz[:], AF.Square)
        k.ts(g0[:], g0[:], 0.044715 * 0.7978845608028654, ALU.mult, 0.7978845608028654, ALU.add)
        k.tt(g0[:], g0[:], yz[:], ALU.mult)
        k.act(g1[:], g0[:], AF.Exp, scale=-2.0)
        recip_1p(k, g1[:])
        zz = zo[bi % 2]
        k.tt(zz[:], g1[:], yz[:], ALU.mult)
        pr0 = (t0 // PT) * 256; pc0 = t0 % PT
        k.dma(P_o[pr0 + 192:pr0 + 256, pc0:pc0 + TB], zz[:], q="pool", lane="st")
        k.stop_record(); psel[0] = None
        k.emit_merged(recA, recB)
        io["hook"](bi)

import math

D = 1024
EPS = 1e-6
GN_EPS = 64e-5
CW = 64
EM05 = math.exp(-0.5)


def b3(v):
    return V(v.buf, v.ap.unsqueeze(1).to_broadcast([128, 2, 64]))


def r3(v):
    return v.rearrange("p (a b) -> p a b", a=2)


def emit_A1(k, L, io, Tc, TB=512, PAD=2):
    skip = ()
    NB = L // TB
    NFM = 8 * 128
    x1G = io["x1G"]; w_fm = io["w_fm"]; w_tm = io["w_tm"]; nmix_d = io["nmix"]; rwp_d = io["rwp"]; w2_d = io["w2p"]; a2_d = io["a2p"]
    g2_d = io["g2c"]; lnw_d = io["lnw"]; lnb_d = io["lnb"]; mcw_d = io["mcw"]; mcb_d = io["mcb"]; mh_d = io["mh"]
    P_o = io["P"]; PT = io["PT"]; XB = io["XB"]

    ident, identb, maskT, ones, onesb = make_consts(k)
    bmask = k.sb([128, 128], F32, "bmask"); k.memset(bmask[:], 0.0); k.memset(bmask[0:64, 0:64], 1.0); k.memset(bmask[64:128, 64:128], 1.0)
    mI = k.sb([128, 128], F32, "mI"); k.tt(mI[:], maskT[:], bmask[:], ALU.mult)
    mS = k.sb([128, 128], F32, "mS"); k.tt(mS[:], mI[:], ident[:], ALU.subtract)
    mSl = k.sb([128, 128], F32, "mSl")
    pt_ = k.ps([128, 512], F32, "ptmp")
    k.transpose(pt_[:, 0:128], mS[:], ident[:]); k.copy(mSl[:], pt_[:, 0:128])
    E = k.sb([128, 64], F32, "E"); k.memset(E[:], 0.0)
    k.op("pool", lambda e: e.affine_select(out=E[:].ap, in_=E[:].ap, compare_op=ALU.not_equal, fill=1.0, base=0, pattern=[[-1, 64]], channel_multiplier=1), reads=[E], writes=[E])
    k.op("pool", lambda e: e.affine_select(out=E[:].ap, in_=E[:].ap, compare_op=ALU.not_equal, fill=1.0, base=-64, pattern=[[-1, 64]], channel_multiplier=1), reads=[E], writes=[E])

    ycT = [k.sb([64, 2 * TB], F32, "ycT") for _ in range(2)]
    ydT = [k.sb([128, TB], F32, "ydT") for _ in range(2)]
    nmix = k.sb([128, 8]); k.dma(nmix[:], nmix_d[:], lane="wld")
    rwp = k.sb([128, 16]); k.dma(rwp[:], rwp_d[:], lane="wld")
    w2p = k.sb([128, 128]); k.dma(w2p[:], w2_d[:], lane="wld")
    a2p = k.sb([128, 128]); k.dma(a2p[:], a2_d[:], lane="wld")
    g2c = k.sb([128, 128]); k.dma(g2c[:], g2_d[:], lane="wld")
    lnw = k.sb([128, 64]); k.dma(lnw[:], lnw_d[:], lane="wld")
    lnb = k.sb([128, 64]); k.dma(lnb[:], lnb_d[:], lane="wld")
    mcw = k.sb([128, 12]); k.dma(mcw[:], mcw_d[:], lane="wld")
    mcb = k.sb([128, 3]); k.dma(mcb[:], mcb_d[:], lane="wld")
    mh = k.sb([128, 6]); k.dma(mh[:], mh_d[:], lane="wld")
    omka = k.sb([128, 1]); k.ts(omka[:], rwp[:, 8:9], -1.0, ALU.mult, 1.0, ALU.add)
    nrwp = k.sb([128, 16]); k.ts(nrwp[:], rwp[:], -1.0, ALU.mult)
    negA = k.sb([128, 2]); k.act(negA[:], mh[:, 2:4], AF.Exp); k.ts(negA[:], negA[:], -1.0, ALU.mult)

    stage = [k.sb([128, 512], F32, "stage") for _ in range(2)]
    wfm = [k.sb([128, NFM], BF16, "wfm") for _ in range(8)]
    wtm = [k.sb([128, 130], BF16, "wtm") for _ in range(8)]
    cast_w(k, wfm, w_fm, NFM, nmix, stage)
    cast_w(k, wtm, w_tm, 130, nmix, stage)

    PS = [pt_] + [k.ps([128, 512], F32, "ps") for _ in range(7)]
    psi = [0]

    psel = [None]
    psa = [0]; psb = [0]
    NA = len(PS) - 3

    def nps():
        if psel[0] == "A":
            p = PS[psa[0] % NA]; psa[0] += 1
        elif psel[0] == "B":
            p = PS[NA + psb[0] % 3]; psb[0] += 1
        else:
            p = PS[psi[0] % len(PS)]; psi[0] += 1
        return p

    def T(shape, name, dt=F32):
        return k.sb(shape, dt, name)

    Hpk = T([128, 64], "Hpk"); k.memset(Hpk[:], 0.0)
    HT = T([128, 128], "HT"); k.memset(HT[:], 0.0)
    cmask = T([128, TB], "cmask"); k.memset(cmask[:], 1.0)
    for c in range(TB // CW):
        k.memset(cmask[:, c * CW:c * CW + 1], 0.0)
    xs = [T([128, TB], "xs") for _ in range(8)]
    sqb = [T([128, TB], "sqb", BF16) for _ in range(2)]
    hb = [T([128, TB], "hb", BF16) for _ in range(8)]
    rstd = T([128, TB], "rstd")
    Psb = [T([128, TB + 1], "Psb") for _ in range(5)]
    for t_ in Psb:
        k.memset(t_[:, 0:1], 0.0)
    PM = [T([128, TB], "PM") for _ in range(5)]
    dtmp = T([128, TB], "dtmp")
    Xc = [T([128, TB + 3], "Xc") for _ in range(3)]
    for t_ in Xc:
        k.memset(t_[:, 0:3], 0.0)
    cacc = T([128, TB], "cacc")
    XBC = [T([128, TB], "XBC") for _ in range(3)]
    th = T([128, TB], "th"); nlw = T([128, TB], "nlw"); asig = T([128, TB], "asig"); sgx = T([128, TB], "sgx")
    sgxd = T([128, 2 * TB], "sgxd")
    kk = T([128, TB], "kk"); kkn = T([128, TB], "kkn"); kmod = T([128, TB], "kmod"); ftmp = T([128, TB], "ftmp")
    cumn = T([128, TB], "cumn"); Ep = T([128, TB], "Ep"); En = T([128, TB], "En"); Eex = T([128, TB], "Eex")
    rt = T([128, TB], "rt"); kt = T([128, TB], "kt"); bt = T([128, TB], "bt"); at = T([128, TB], "at"); prod = T([128, TB], "prod")
    ztm = [T([128, 128], "ztm") for _ in range(4)]
    dtv = [T([128, 2], "dtv") for _ in range(4)]

    def blk(out, src_v, eng="pool"):
        k.tt(r3(out[:]), b3(src_v), r3(bmask[:]), ALU.mult, eng=eng)

    for bi in range(NB):
        t0 = bi * TB
        pss = nps()
        for kc in range(8):
            rr = t0 // Tc; i0 = (t0 % Tc) // XB
            for pi in range(TB // XB):
                gr = (i0 + pi) * 4096 + rr * 1024 + kc * 128
                k.dma(xs[kc][:, pi * XB:(pi + 1) * XB], x1G[gr:gr + 128, :], q="sp", lane="xld")
            k.act(sqb[kc % 2][:], xs[kc][:], AF.Square)
            k.mm(pss[:], onesb[:], sqb[kc % 2][:], start=(kc == 0), stop=(kc == 7))
        rsqrt_small(k, rstd[:], pss[:], 1.0 / D, EPS)
        for kc in range(8):
            k.tt(hb[kc][:], xs[kc][:], rstd[:], ALU.mult, eng=("pool" if kc % 2 else "dve"))
        for ti in range(8):
            p = nps()
            for kc in range(8):
                k.mm(p[:], wfm[kc][:, ti * 128:(ti + 1) * 128], hb[kc][:], start=(kc == 0), stop=(kc == 7))
            if ti < 5:
                P = Psb[ti]
                k.copy(P[:, 1:TB + 1], p[:], eng="act")
                k.tt(dtmp[:], P[:, 0:TB], P[:, 1:TB + 1], ALU.subtract)
                k.stt(PM[ti][:], dtmp[:], rwp[:, ti:ti + 1], P[:, 1:TB + 1], ALU.mult, ALU.add)
                k.copy(P[:, 0:1], P[:, TB:TB + 1], eng="pool")
            else:
                mi = ti - 5
                X = Xc[mi]
                k.copy(X[:, 3:TB + 3], p[:], eng="act")
                k.act(cacc[:], X[:, 0:TB], AF.Identity, scale=mcw[:, mi * 4:mi * 4 + 1])
                for j in range(1, 4):
                    k.stt(cacc[:], X[:, j:TB + j], mcw[:, mi * 4 + j:mi * 4 + j + 1], cacc[:], ALU.mult, ALU.add)
                k.ts(cacc[:], cacc[:], mcb[:, mi:mi + 1], ALU.add)
                k.act(XBC[mi][:], cacc[:], AF.Exp, scale=-1.0)
                recip_1p(k, XBC[mi][:])
                k.tt(XBC[mi][:], XBC[mi][:], cacc[:], ALU.mult, eng="pool")
                k.copy(X[:, 0:3], X[:, TB:TB + 3], eng="pool")
        for s in range(4):
            pz = nps()
            for kc in range(8):
                k.mm(pz[:, 0:130], hb[kc][:, s * 128:(s + 1) * 128], wtm[kc][:], start=(kc == 0), stop=(kc == 7))
            k.act(ztm[s][:], pz[:, 0:128], AF.Exp, scale=-1.0)
            recip_1p(k, ztm[s][:])
            k.tt(ztm[s][:], ztm[s][:], pz[:, 0:128], ALU.mult)
            k.tt(dtv[s][:], pz[:, 128:130], mh[:, 0:2], ALU.add)
            k.act(dtv[s][:], dtv[s][:], AF.Exp)
            k.act(dtv[s][:], dtv[s][:], AF.Ln, bias=1.0)
        recA = k.record(); psel[0] = "A"
        if 'rwkv' not in skip:
            k.act(th[0:64, :], PM[4][0:64, :], AF.Exp, scale=-2.0)
            recip_1p(k, th[0:64, :])
            k.ts(th[0:64, :], th[0:64, :], 2.0, ALU.mult, -1.0, ALU.add)
            k.copy(th[64:128, :], PM[4][64:128, :], eng="act")
            pw = nps(); k.mm(pw[:], w2p[:], th[:])
            k.act(nlw[:], pw[:], AF.Exp, scale=-1.0, bias=nrwp[:, 5:6])
            recip_1p(k, nlw[:])
            k.ts(nlw[:], nlw[:], EM05, ALU.mult)
            pa = nps(); k.mm(pa[:], a2p[:], th[:])
            k.act(asig[:], pa[:], AF.Exp, scale=-1.0, bias=nrwp[:, 6:7])
            recip_1p(k, asig[:])
            k.act(sgx[:], PM[3][:], AF.Exp, scale=-1.0)
            recip_1p(k, sgx[:])
            k.copy(V(sgxd, sgxd[:].ap.rearrange("p (c a t) -> p c a t", a=2, t=CW)),
                   V(sgx, sgx[:].ap.rearrange("p (c t) -> p c t", t=CW).unsqueeze(2).to_broadcast([128, TB // CW, 2, CW])), eng="pool")
            k.ts(kk[:], PM[1][:], rwp[:, 7:8], ALU.mult)
            k.tt(ftmp[:], kk[:], kk[:], ALU.mult, eng="pool")
            pn = nps(); k.mm(pn[:], bmask[:], ftmp[:])
            k.ts(ftmp[:], pn[:], 1e-24, ALU.max)
            k.act(ftmp[:], ftmp[:], AF.Ln)
            k.act(ftmp[:], ftmp[:], AF.Exp, scale=-0.5)
            k.tt(kkn[:], kk[:], ftmp[:], ALU.mult)
            k.ts(ftmp[:], asig[:], rwp[:, 8:9], ALU.mult, omka[:, 0:1], ALU.add)
            k.tt(kmod[:], PM[1][:], ftmp[:], ALU.mult)
            k.scan(cumn[:], cmask[:], nlw[:], 0.0)
            k.act(Ep[:], cumn[:], AF.Exp, scale=-1.0)
            k.act(En[:], cumn[:], AF.Exp)
            k.tt(Eex[:], nlw[:], cumn[:], ALU.subtract, eng="pool")
            k.act(Eex[:], Eex[:], AF.Exp)
            k.tt(rt[:], PM[0][:], Ep[:], ALU.mult)
            k.tt(kt[:], kmod[:], En[:], ALU.mult, eng="pool")
            k.tt(bt[:], kkn[:], asig[:], ALU.mult, eng="pool")
            k.tt(bt[:], bt[:], En[:], ALU.mult, eng="pool")
            k.stt(at[:], kkn[:], -1.0, Eex[:], ALU.mult, ALU.mult)
            k.stt(prod[:], PM[0][:], rwp[:, 9:10], kmod[:], ALU.mult, ALU.mult)
            GR = 4
            if bi == 0:
                CB = []
                for _g in range(GR):
                    d = {n: T([128, 128], n) for n in ("rB", "kB", "bB", "aB", "bH", "kH", "vB", "pB", "AakT", "ArbT", "ArkT", "BH", "KH", "yf")}
                    d.update({n: T([128, 128], n, BF16) for n in ("M", "N", "X", "XT", "X2", "XT2", "R", "R2")})
                    d["Xpkb"] = T([128, 64], "Xpkb", BF16)
                    d.update({n: T([128, 64], n) for n in ("Vpk", "Xpk", "Upk", "yc", "ysq", "yn", "yo")})
                    d.update({n: T([128, 1], n) for n in ("sum", "ssq", "rk")})
                    CB.append(d)
            for g0 in range(0, TB // CW, GR):
                grp = [(g0 + u, CB[u]) for u in range(GR)]
                for c, b_ in grp:
                    sl = slice(c * CW, (c + 1) * CW)
                    gC = Ep[:, c * CW + CW - 1:c * CW + CW]
                    blk(b_["rB"], rt[:, sl]); blk(b_["kB"], kt[:, sl]); blk(b_["bB"], bt[:, sl], eng="dve"); blk(b_["aB"], at[:, sl], eng="dve")
                    blk(b_["vB"], PM[2][:, sl]); blk(b_["pB"], prod[:, sl])
                    k.stt(r3(b_["bH"][:]), b3(bt[:, sl]), gC, r3(bmask[:]), ALU.mult, ALU.mult)
                    k.stt(r3(b_["kH"][:]), b3(kt[:, sl]), gC, r3(bmask[:]), ALU.mult, ALU.mult)
                for (la, ra, dst, msk) in (("bB", "aB", "M", mS), ("aB", "bB", "N", mSl), ("kB", "aB", "AakT", mS),
                                           ("bB", "rB", "ArbT", mI), ("kB", "rB", "ArkT", mI)):
                    for c, b_ in grp:
                        p = nps(); k.mm(p[:, 0:128], b_[la][:], b_[ra][:]); k.tt(b_[dst][:], p[:, 0:128], msk[:], ALU.mult)
                cur = {}
                for c, b_ in grp:
                    k.tt(b_["R"][:], b_["M"][:], identb[:], ALU.add, eng="pool")
                    b_["Rcur"] = b_["R"]
                    cur[c] = (b_["M"], b_["N"])
                for lev in range(1, 6):
                    for c, b_ in grp:
                        Xc_, XTc = cur[c]
                        Xn, XTn = ((b_["X"], b_["XT"]), (b_["X2"], b_["XT2"]))[lev % 2]
                        p1 = nps(); k.mm(p1[:, 0:128], Xc_[:], XTc[:]); k.copy(XTn[:], p1[:, 0:128], eng="act")
                        if lev < 5:
                            p2 = nps(); k.mm(p2[:, 0:128], XTc[:], Xc_[:]); k.copy(Xn[:], p2[:, 0:128], eng="act")
                        cur[c] = (Xn, XTn)
                    for c, b_ in grp:
                        Rc = b_["Rcur"]; Rn = b_["R2"] if Rc is b_["R"] else b_["R"]
                        p3 = nps()
                        k.mm(p3[:, 0:128], identb[:], Rc[:], start=True, stop=False)
                        k.mm(p3[:, 0:128], cur[c][1][:], Rc[:], start=False, stop=True)
                        k.copy(Rn[:], p3[:, 0:128], eng="act")
                        b_["Rcur"] = Rn
                for c, b_ in grp:
                    p = nps(); k.mm(p[:, 0:64], b_["vB"][:], E[:]); k.copy(b_["Vpk"][:], p[:, 0:64], eng="act")
                for c, b_ in grp:
                    p = nps(); k.transpose(p[:, 0:128], b_["bH"][:], ident[:]); k.copy(b_["BH"][:], p[:, 0:128], eng="act")
                for c, b_ in grp:
                    p = nps(); k.transpose(p[:, 0:128], b_["kH"][:], ident[:]); k.copy(b_["KH"][:], p[:, 0:128], eng="act")
                pys = {}
                for c, b_ in grp:
                    gC = Ep[:, c * CW + CW - 1:c * CW + CW]
                    p = nps()
                    k.mm(p[:, 0:64], b_["aB"][:], Hpk[:], start=True, stop=False)
                    k.mm(p[:, 0:64], b_["AakT"][:], b_["Vpk"][:], start=False, stop=True)
                    k.copy(b_["Xpkb"][:], p[:, 0:64])
                    p = nps(); k.mm(p[:, 0:64], b_["Rcur"][:], b_["Xpkb"][:]); k.copy(b_["Upk"][:], p[:, 0:64])
                    py = nps()
                    k.mm(py[:, 0:64], b_["rB"][:], Hpk[:], start=True, stop=False)
                    k.mm(py[:, 0:64], b_["ArbT"][:], b_["Upk"][:], start=False, stop=False)
                    k.mm(py[:, 0:64], b_["ArkT"][:], b_["Vpk"][:], start=False, stop=True)
                    ph = nps()
                    k.mm(ph[:, 0:64], b_["BH"][:], b_["Upk"][:], start=True, stop=False)
                    k.mm(ph[:, 0:64], b_["KH"][:], b_["Vpk"][:], start=False, stop=True)
                    k.stt(Hpk[:], Hpk[:], gC, ph[:, 0:64], ALU.mult, ALU.add)
                    k.act(b_["yc"][:], py[:, 0:64], AF.Identity, accum=b_["sum"][:])
                for c, b_ in grp:
                    k.ts(b_["sum"][:], b_["sum"][:], -1.0 / 64.0, ALU.mult)
                for c, b_ in grp:
                    k.act(b_["yc"][:], b_["yc"][:], AF.Identity, bias=b_["sum"][:, 0:1])
                for c, b_ in grp:
                    k.act(b_["ysq"][:], b_["yc"][:], AF.Square, accum=b_["ssq"][:])
                for c, b_ in grp:
                    k.ts(b_["ssq"][:], b_["ssq"][:], 1.0 / 64.0, ALU.mult, GN_EPS, ALU.add)
                for c, b_ in grp:
                    k.act(b_["ssq"][:], b_["ssq"][:], AF.Ln)
                for c, b_ in grp:
                    k.act(b_["ssq"][:], b_["ssq"][:], AF.Exp, scale=-0.5)
                for c, b_ in grp:
                    k.stt(b_["yn"][:], b_["yc"][:], b_["ssq"][:, 0:1], lnw[:], ALU.mult, ALU.mult)
                for c, b_ in grp:
                    k.tt(b_["yn"][:], b_["yn"][:], lnb[:], ALU.add, eng="pool")
                for c, b_ in grp:
                    p = nps(); k.mm(p[:, 0:2], b_["pB"][:], ones[:, 0:2]); k.copy(b_["rk"][:], p[:, 0:1], eng="act")
                for c, b_ in grp:
                    k.stt(b_["yn"][:], b_["Vpk"][:], b_["rk"][:, 0:1], b_["yn"][:], ALU.mult, ALU.add)
                for c, b_ in grp:
                    pg = nps(); k.mm(pg[:, 0:128], sgxd[:, c * 128:(c + 1) * 128], g2c[:])
                    k.tt(r3(b_["yf"][:]), b3(b_["yn"][:]), r3(pg[:, 0:128]), ALU.mult)
                for c, b_ in grp:
                    k.tt(r3(b_["yf"][:]), r3(b_["yf"][:]), r3(bmask[:]), ALU.mult, eng="pool")
                for c, b_ in grp:
                    k.tt(b_["yo"][:], b_["yf"][:, 0:64], b_["yf"][:, 64:128], ALU.add, eng="pool")
                for c, b_ in grp:
                    pT = nps(); k.transpose(pT[0:64, 0:128], b_["yo"][:], ident[:])
                    k.copy(V(ycT[bi % 2], ycT[bi % 2][:].ap.rearrange("p (h t) -> p h t", h=2)[:, :, c * CW:(c + 1) * CW]),
                           V(pT, pT[0:64, 0:128].ap.rearrange("p (h t) -> p h t", h=2)), eng="act")
            for hh in range(2):
                pr0 = (t0 // PT) * 256; pc0 = t0 % PT
                k.dma(P_o[pr0 + hh * 64:pr0 + (hh + 1) * 64, pc0:pc0 + TB], ycT[bi % 2][:, hh * TB:(hh + 1) * TB], q="pool", lane="st")
        recB = k.record(); psel[0] = "B"
        for c in range(4):
            sl = slice(c * 128, (c + 1) * 128)
            if bi == 0 and c == 0:
                mb = {n: T([128, 128], n) for n in ("xtm", "xdt", "xdd", "Btm", "abc0", "abc1", "df", "GTm", "WT0", "WT1", "ydg", "yy")}
                mb.update({n: T([128, 2], n) for n in ("a", "acs", "nacs", "tot", "eacs", "dec", "etot")})
            a = mb["a"]
            k.tt(a[:], dtv[c][:], negA[:], ALU.mult)
            pc = nps()
            k.mm(pc[:, 0:2], maskT[:], a[:])
            k.mm(pc[:, 2:4], ones[:], a[:])
            k.copy(mb["acs"][:], pc[:, 0:2], eng="act")
            k.copy(mb["tot"][:], pc[:, 2:4], eng="act")
            k.ts(mb["nacs"][:], mb["acs"][:], -1.0, ALU.mult)
            k.act(mb["eacs"][:], mb["acs"][:], AF.Exp)
            k.act(mb["etot"][:], mb["tot"][:], AF.Exp)
            k.tt(mb["dec"][:], mb["tot"][:], mb["acs"][:], ALU.subtract)
            k.act(mb["dec"][:], mb["dec"][:], AF.Exp)
            p = nps(); k.transpose(p[:, 0:128], XBC[0][:, sl], ident[:]); k.copy(mb["xtm"][:], p[:, 0:128], eng="act")
            p = nps(); k.transpose(p[:, 0:128], XBC[1][:, sl], ident[:]); k.copy(mb["Btm"][:], p[:, 0:128], eng="act")
            for h in range(2):
                hs = slice(h * 64, (h + 1) * 64)
                k.ts(mb["xdt"][:, hs], mb["xtm"][:, hs], dtv[c][:, h:h + 1], ALU.mult)
                k.ts(mb["xdd"][:, hs], mb["xdt"][:, hs], mb["dec"][:, h:h + 1], ALU.mult)
            pg = nps(); k.mm(pg[:, 0:128], XBC[1][:, sl], XBC[2][:, sl]); k.tt(mb["GTm"][:], pg[:, 0:128], maskT[:], ALU.mult)
            pyd = nps()
            for h in range(2):
                hs = slice(h * 64, (h + 1) * 64)
                abc = mb["abc%d" % h]
                k.ts(abc[:], ones[:], a[:, h:h + 1], ALU.mult)
                pr = nps(); k.mm(pr[:, 0:128], abc[:], maskT[:])
                k.ts(mb["df"][:], pr[:, 0:128], mb["nacs"][:, h:h + 1], ALU.add, 0.0, ALU.min)
                k.act(mb["df"][:], mb["df"][:], AF.Exp)
                WT = mb["WT%d" % h]
                k.tt(WT[:], mb["df"][:], mb["GTm"][:], ALU.mult)
                k.mm(pyd[:, hs], WT[:], mb["xdt"][:, hs])
            k.copy(mb["ydg"][:], pyd[:, 0:128], eng="act")
            po = nps(); k.mm(po[:, 0:128], XBC[2][:, sl], HT[:])
            for h in range(2):
                hs = slice(h * 64, (h + 1) * 64)
                k.stt(mb["yy"][:, hs], po[:, hs], mb["eacs"][:, h:h + 1], mb["ydg"][:, hs], ALU.mult, ALU.add)
                k.stt(mb["yy"][:, hs], mb["xtm"][:, hs], mh[:, 4 + h:5 + h], mb["yy"][:, hs], ALU.mult, ALU.add)
            k.tt(mb["yy"][:], mb["yy"][:], ztm[c][:], ALU.mult, eng="pool")
            pT = nps(); k.transpose(pT[:, 0:128], mb["yy"][:], ident[:])
            k.copy(ydT[bi % 2][:, sl], pT[:, 0:128], eng="act")
            if c == 3:
                pr0 = (t0 // PT) * 256; pc0 = t0 % PT
                k.dma(P_o[pr0 + 128:pr0 + 256, pc0:pc0 + TB], ydT[bi % 2][:], q="pool", lane="st")
            pst = nps(); k.mm(pst[:, 0:128], mb["Btm"][:], mb["xdd"][:])
            for h in range(2):
                hs = slice(h * 64, (h + 1) * 64)
                k.stt(HT[:, hs], HT[:, hs], mb["etot"][:, h:h + 1], pst[:, hs], ALU.mult, ALU.add)
        k.stop_record(); psel[0] = None
        k.emit_merged(recA, recB)
        io["hook"](bi)


D = 1024
DFF = 2816
NJ = DFF // 128
EPS = 1e-6


def load_cast_weight(k, dst_chunks, src, rows_kc, ncols, scale_cols=None, piece=704, engs=("dve",), stage=None, cnt=[0]):
    for kc in range(rows_kc):
        for c0 in range(0, ncols, piece):
            n = min(piece, ncols - c0)
            st = stage[cnt[0] % len(stage)]
            k.dma(st[:, 0:n], src[kc * 128:(kc + 1) * 128, c0:c0 + n], q="sp", lane="wld")
            eng = engs[cnt[0] % len(engs)]
            cnt[0] += 1
            if scale_cols is not None:
                if eng == "act":
                    k.op("act", lambda e: e.mul(dst_chunks[kc][:, c0:c0 + n].ap, st[:, 0:n].ap, scale_cols[:, kc:kc + 1].ap),
                         reads=[st, scale_cols], writes=[dst_chunks[kc]])
                else:
                    k.ts(dst_chunks[kc][:, c0:c0 + n], st[:, 0:n], scale_cols[:, kc:kc + 1], ALU.mult, eng=eng)
            else:
                k.copy(dst_chunks[kc][:, c0:c0 + n], st[:, 0:n], eng=eng)


def emit_B(k, layer, Tc, final, io, TB=256):
    H = 4 if layer == 0 else 2
    w_out = io["w_out"]; w_up = io["w_up"]; w_dn = io["w_dn"]; cw_d = io["cw"]; cb_d = io["cb"]; nf_d = io["nffn"]
    if final:
        nfin_d = io["nfin"]
    yloc = io["yloc"]
    G = io["G"]; PT = io["PT"]; NPc = Tc // PT; W = 4 + Tc
    XB = TB
    if layer == 0:
        wglu_d = io["wglu"]; bglu_d = io["bglu"]
        parts = [(0, 192, 0, 192), (192, 64, 768, 64)]
    else:
        mbn_d = io["mbn"]
        parts = [(0, 128, 0, 128), (128, 128, 512, 128)]
    dq = "act"; dqi = [0]
    for i in range(-1, NPc):
        for (sr, nr, dr, dstride) in parts:
            if i < 0:
                soff = 0 * 1024 * PT + sr * PT + (PT - H); ncol = H; dcol = 0
            else:
                soff = (1 + i) * 1024 * PT + sr * PT; ncol = PT; dcol = H + i * PT
            dst_ap = bass.AP(yloc.h.tensor, dr * W + dcol, [[dstride * W, 4], [W, nr], [1, ncol]])
            k.dyn_dma3(dst_ap, yloc, G, NPc * 1024 * PT, soff, [[256 * PT, 4], [PT, nr], [1, ncol]], ename=("act", "sp")[dqi[0] % 2], lane="yloc")
            dqi[0] += 1
    k.wait_ring("act", "yloc")

    cw = k.sb([128, NJ * 3]); k.dma(cw[:], cw_d[:], lane="wld")
    cb = k.sb([128, NJ]); k.dma(cb[:], cb_d[:], lane="wld")
    nf = k.sb([128, 8]); k.dma(nf[:], nf_d[:], lane="wld")
    if final:
        nfin = k.sb([128, 8]); k.dma(nfin[:], nfin_d[:], lane="wld")
    ones = k.sb([128, 128]); k.memset(ones[:], 1.0)
    if layer == 0:
        bglu = k.sb([128, 2]); k.dma(bglu[:], bglu_d[:], lane="wld")
        hbglu = k.sb([128, 2]); k.ts(hbglu[:], bglu[:], 0.5, ALU.mult)
    else:
        mbn = k.sb([128, 4]); k.dma(mbn[:], mbn_d[:], lane="wld")

    stage = [k.sb([128, 704], F32, "stage") for _ in range(3)]
    wo = [k.sb([128, D], BF16, "wo") for _ in range(8)]
    wu = [k.sb([128, 2 * DFF], BF16, "wu") for _ in range(8)]
    wd = [k.sb([128, D], BF16, "wd") for _ in range(NJ)]
    load_cast_weight(k, wo, w_out, 8, D, stage=stage, piece=512)
    if layer == 0:
        wg = [k.sb([128, 256], BF16, "wg") for _ in range(2)]
        load_cast_weight(k, wg, wglu_d, 2, 256, stage=stage, piece=256)
    load_cast_weight(k, wu, w_up, 8, 2 * DFF, scale_cols=nf, stage=stage)
    load_cast_weight(k, wd, w_dn, NJ, D, stage=stage, piece=512)

    xs = [[k.sb([128, TB], F32, "xs") for _ in range(8)] for _ in range(1)]
    ys = [k.sb([128, TB], F32, "ys") for _ in range(3)]
    yb = [k.sb([128, TB], BF16, "yb") for _ in range(8)]
    xnb = [k.sb([128, TB], BF16, "xnb") for _ in range(8)]
    actb = [k.sb([128, TB], BF16, "actb") for _ in range(NJ)]
    G = [k.sb([128, TB + 2], F32, "G") for _ in range(2)]
    cv = [k.sb([128, TB], F32, "cv") for _ in range(2)]
    sg = [k.sb([128, TB], F32, "sg") for _ in range(2)]
    carry = [k.sb([128, 2], F32, "carry") for _ in range(NJ)]
    sq = [k.sb([128, TB], F32, "sq") for _ in range(2)]
    rstd = k.sb([128, TB], F32, "rstd")
    zs = [k.sb([128, TB], F32, "zs") for _ in range(4)]
    PS = [k.ps([128, 512], F32, "ps") for _ in range(8)]
    psi = [0]

    def nps():
        p = PS[psi[0] % 8]
        psi[0] += 1
        return p

    def rstd_from_sumsq(ps, n, inv_n, eps):
        k.ts(rstd[:, 0:n], ps[:, 0:n], inv_n, ALU.mult, eps, ALU.add)
        k.act(rstd[:, 0:n], rstd[:, 0:n], AF.Sqrt)
        k.op("dve", lambda e: e.reciprocal(rstd[:, 0:n].ap, rstd[:, 0:n].ap), reads=[rstd], writes=[rstd])

    if layer == 0:
        blocks = [(0, 2, True), (2, 2, False)] + [(4 + i * TB, TB, False) for i in range(Tc // TB)]
    else:
        blocks = [(0, 2, True)] + [(2 + i * TB, TB, False) for i in range(Tc // TB)]
    for bi, (c0, n, halo) in enumerate(blocks):
        X = xs[0]
        for dc in range(8):
            if layer == 0:
                xsrc = io["xTb"][dc * 128:(dc + 1) * 128, c0:c0 + n]
            elif c0 < 2:
                xsrc = io["x1halo"][dc * 128:(dc + 1) * 128, c0:c0 + n]
            else:
                pi_ = (c0 - 2) // XB
                xsrc = io["x1p"][pi_ * 1024 + dc * 128:pi_ * 1024 + (dc + 1) * 128, :]
            k.dma(X[dc][:, 0:n], xsrc, q="sp", lane="xld")
        for dc in range(8):
            raw = (layer == 0 and dc >= 6) or (layer == 1 and dc >= 4)
            if raw:
                st = zs[dc - 4] if layer == 1 else zs[dc - 6]
            else:
                st = ys[dc % 3]
            k.dma(st[:, 0:n], yloc[dc * 128:(dc + 1) * 128, c0:c0 + n], q="act", lane="yld")
            if not raw:
                k.copy(yb[dc][:, 0:n], st[:, 0:n], eng="pool")
        if layer == 0:
            zb = [xnb[0], xnb[1]]
            for i in range(2):
                k.copy(zb[i][:, 0:n], zs[i][:, 0:n], eng="pool")
            for co in range(2):
                p = nps()
                for ci in range(2):
                    k.mm(p[:, 0:n], wg[ci][:, co * 128:(co + 1) * 128], zb[ci][:, 0:n], start=(ci == 0), stop=(ci == 1))
                k.act(sg[0][:, 0:n], p[:, 0:n], AF.Tanh, scale=0.5, bias=hbglu[:, co:co + 1])
                k.ts(sg[0][:, 0:n], sg[0][:, 0:n], 0.5, ALU.mult, 0.5, ALU.add)
                k.tt(yb[6 + co][:, 0:n], zs[co][:, 0:n], sg[0][:, 0:n], ALU.mult)
        else:
            for g in range(2):
                p = nps()
                for i in range(2):
                    k.act(sq[i][:, 0:n], zs[2 * g + i][:, 0:n], AF.Square)
                    k.mm(p[:, 0:n], ones[:], sq[i][:, 0:n], start=(i == 0), stop=(i == 1))
                rstd_from_sumsq(p, n, 1.0 / 256.0, EPS)
                for i in range(2):
                    k.stt(yb[4 + 2 * g + i][:, 0:n], zs[2 * g + i][:, 0:n], mbn[:, 2 * g + i:2 * g + i + 1], rstd[:, 0:n], ALU.mult, ALU.mult)
        for dc in range(8):
            p = nps()
            for kc in range(8):
                k.mm(p[:, 0:n], wo[kc][:, dc * 128:(dc + 1) * 128], yb[kc][:, 0:n], start=(kc == 0), stop=(kc == 7))
            k.tt(X[dc][:, 0:n], X[dc][:, 0:n], p[:, 0:n], ALU.add)
        p = nps()
        for dc in range(8):
            k.act(sq[dc % 2][:, 0:n], X[dc][:, 0:n], AF.Square)
            k.mm(p[:, 0:n], ones[:], sq[dc % 2][:, 0:n], start=(dc == 0), stop=(dc == 7))
        rstd_from_sumsq(p, n, 1.0 / D, EPS)
        for dc in range(8):
            k.tt(xnb[dc][:, 0:n], X[dc][:, 0:n], rstd[:, 0:n], ALU.mult, eng=("pool" if dc % 2 else "dve"))
        def stage_a(j):
            pg = nps()
            pv = nps()
            for kc in range(8):
                k.mm(pg[:, 0:n], wu[kc][:, j * 128:(j + 1) * 128], xnb[kc][:, 0:n], start=(kc == 0), stop=(kc == 7))
            for kc in range(8):
                k.mm(pv[:, 0:n], wu[kc][:, DFF + j * 128:DFF + (j + 1) * 128], xnb[kc][:, 0:n], start=(kc == 0), stop=(kc == 7))
            Gj = G[j % 2]
            k.copy(Gj[:, 2:2 + n], pg[:, 0:n], eng="act")
            if halo:
                k.copy(carry[j][:, 0:2], Gj[:, 2:4], eng="pool")
            else:
                k.copy(Gj[:, 0:2], carry[j][:, 0:2], eng="pool")
            return (j, pv)

        def stage_b(j):
            Gj = G[j % 2]
            c = cv[j % 2]
            k.act(c[:, 0:n], Gj[:, 0:n], AF.Identity, scale=cw[:, 3 * j:3 * j + 1])
            k.stt(c[:, 0:n], Gj[:, 1:1 + n], cw[:, 3 * j + 1:3 * j + 2], c[:, 0:n], ALU.mult, ALU.add)
            k.stt(c[:, 0:n], Gj[:, 2:2 + n], cw[:, 3 * j + 2:3 * j + 3], c[:, 0:n], ALU.mult, ALU.add)
            k.copy(carry[j][:, 0:2], Gj[:, n:n + 2], eng="pool")

        def stage_c(jj, pvv):
            s_ = sg[jj % 2]
            k.act(s_[:, 0:n], cv[jj % 2][:, 0:n], AF.Silu, bias=cb[:, jj:jj + 1])
            k.tt(actb[jj][:, 0:n], s_[:, 0:n], pvv[:, 0:n], ALU.mult)

        cur_a = stage_a(0)
        prev_a = None
        for j in range(NJ):
            nxt_a = stage_a(j + 1) if j + 1 < NJ else None
            if not halo:
                stage_b(j)
                if prev_a is not None:
                    stage_c(*prev_a)
            prev_a = cur_a
            cur_a = nxt_a
        if not halo:
            stage_c(*prev_a)
        if halo:
            continue
        for dc in range(8):
            p = nps()
            for j in range(NJ):
                k.mm(p[:, 0:n], wd[j][:, dc * 128:(dc + 1) * 128], actb[j][:, 0:n], start=(j == 0), stop=(j == NJ - 1))
            k.tt(X[dc][:, 0:n], X[dc][:, 0:n], p[:, 0:n], ALU.add)
        if final:
            p = nps()
            for dc in range(8):
                k.act(sq[dc % 2][:, 0:n], X[dc][:, 0:n], AF.Square)
                k.mm(p[:, 0:n], ones[:], sq[dc % 2][:, 0:n], start=(dc == 0), stop=(dc == 7))
            rstd_from_sumsq(p, n, 1.0 / D, EPS)
            for dc in range(8):
                k.stt(X[dc][:, 0:n], X[dc][:, 0:n], nfin[:, dc:dc + 1], rstd[:, 0:n], ALU.mult, ALU.mult)
        for dc in range(8):
            if layer == 1:
                dst = io["outT"][dc * 128:(dc + 1) * 128, c0 - 2:c0 - 2 + n]
            elif c0 < 4:
                dst = io["x1halo"][dc * 128:(dc + 1) * 128, 0:2]
            else:
                pi_ = (c0 - 4) // XB
                dst = io["x1p"][pi_ * 1024 + dc * 128:pi_ * 1024 + (dc + 1) * 128, :]
            k.dma(dst, X[dc][:, 0:n], q="pool", lane="st")
        if layer == 0 and c0 >= 4:
            io["hook"]((c0 - 4) // XB)


def colmajor(v, nch):
    return np.ascontiguousarray(np.asarray(v, np.float32).reshape(nch, 128).T)


def colmajor(v, nch):
    return np.ascontiguousarray(np.asarray(v, np.float32).reshape(nch, 128).T)

def prep_A0(inp, xT_b, h):
    W = inp["e_w_in"][0]
    nk = 512; nv = 768
    q = W[:, h * 128:(h + 1) * 128]
    kk = W[:, nk + h * 128: nk + (h + 1) * 128]
    v = W[:, 2 * nk + h * 192: 2 * nk + (h + 1) * 192]
    g = W[:, 2 * nk + nv + h * 192: 2 * nk + nv + (h + 1) * 192]
    alo = W[:, 2 * nk + 2 * nv: 2 * nk + 2 * nv + 16]
    GC = 2 * nk + 2 * nv + 16
    u = W[:, GC + h * 64: GC + (h + 1) * 64]
    m = {"xT": xT_b,
         "w_fm": np.ascontiguousarray(np.concatenate([q, kk, u, alo], 1)),
         "w_tm": np.ascontiguousarray(np.concatenate([v, g], 1)),
         "nmix": colmajor(inp["norm_mix"][0], 8),
         "wa2": np.ascontiguousarray(inp["e_gla_wa2"][0][:, h * 128:(h + 1) * 128]),
         "nba": np.ascontiguousarray(inp["e_gla_ba"][0][h * 128:(h + 1) * 128].reshape(128, 1)),
         "gnorm": np.ascontiguousarray(np.broadcast_to(inp["e_gla_norm"][0][None, :], (128, 192))),
         "s5d": np.ascontiguousarray(inp["e_s5_d"][0][h * 64:(h + 1) * 64].reshape(64, 1))}
    s5col = np.zeros((128, 6), np.float32)
    s5b = np.zeros((128, 256), np.float32)
    s5c = np.zeros((128, 256), np.float32)
    for st in range(2):
        for gi in range(2):
            gl = 2 * st + gi
            g_ = 4 * h + gl
            rows = slice(gi * 64, (gi + 1) * 64)
            s5col[rows, st * 3 + 0] = inp["e_s5_lambda_re"][0][g_]
            s5col[rows, st * 3 + 1] = inp["e_s5_lambda_im"][0][g_]
            s5col[rows, st * 3 + 2] = inp["e_s5_log_dt"][0][g_]
            s5b[rows, st * 128 + gl * 16: st * 128 + gl * 16 + 16] = inp["e_s5_b_re"][0][g_]
            s5b[rows, st * 128 + 64 + gl * 16: st * 128 + 64 + gl * 16 + 16] = inp["e_s5_b_im"][0][g_]
            s5c[rows, st * 128 + gl * 16: st * 128 + gl * 16 + 16] = inp["e_s5_c_re"][0][g_].T
            s5c[rows, st * 128 + 64 + gl * 16: st * 128 + 64 + gl * 16 + 16] = inp["e_s5_c_im"][0][g_].T
    m["s5col"] = s5col; m["s5b"] = s5b; m["s5c"] = s5c
    return m


def prep_A1(inp, xT_b, j):
    W = inp["o_w_in"][0]
    RW = 512
    cs = slice(j * 128, (j + 1) * 128)
    def col(base):
        return W[:, base + j * 128: base + (j + 1) * 128]
    r = col(0); kk = col(RW); v = col(2 * RW)
    o = 3 * RW
    xw = W[:, o:o + 64]; xa = W[:, o + 64:o + 128]; xg = W[:, o + 128:o + 256]
    RC = 3 * RW + 256
    z = W[:, RC + j * 128: RC + (j + 1) * 128]
    xb = RC + 512
    x = W[:, xb + j * 128: xb + (j + 1) * 128]
    g = j // 2
    Bm = W[:, xb + 512 + g * 128: xb + 512 + (g + 1) * 128]
    Cm = W[:, xb + 768 + g * 128: xb + 768 + (g + 1) * 128]
    dt = W[:, xb + 1024 + 2 * j: xb + 1024 + 2 * j + 2]
    mu = inp["o_rw_mu"][0]
    rwp = np.zeros((128, 16), np.float32)
    rwp[:, 0] = mu[0 + j * 128: 0 + (j + 1) * 128]
    rwp[:, 1] = mu[RW + j * 128: RW + (j + 1) * 128]
    rwp[:, 2] = mu[2 * RW + j * 128: 2 * RW + (j + 1) * 128]
    rwp[:, 3] = mu[o + 128:o + 256]
    rwp[:, 4] = mu[o:o + 128]
    rwp[:, 5] = inp["o_rw_w0"][0][cs]
    rwp[:, 6] = inp["o_rw_a0"][0][cs]
    rwp[:, 7] = inp["o_rw_k_k"][0][cs]
    rwp[:, 8] = inp["o_rw_k_a"][0][cs]
    rwp[:, 9] = inp["o_rw_r_k"][0].reshape(-1)[cs]
    w2p = np.zeros((128, 128), np.float32); w2p[0:64] = inp["o_rw_w2"][0][:, cs]
    a2p = np.zeros((128, 128), np.float32); a2p[64:128] = inp["o_rw_a2"][0][:, cs]
    lnw = np.repeat(inp["o_rw_ln_w"][0][cs].reshape(2, 1, 64), 64, axis=1).reshape(128, 64)
    lnb = np.repeat(inp["o_rw_ln_b"][0][cs].reshape(2, 1, 64), 64, axis=1).reshape(128, 64)
    cw = inp["o_mb_conv_w"][0]; cb = inp["o_mb_conv_b"][0]
    mcw = np.zeros((128, 12), np.float32); mcb = np.zeros((128, 3), np.float32)
    for ti, sl in enumerate((slice(j * 128, (j + 1) * 128), slice(512 + g * 128, 512 + (g + 1) * 128), slice(768 + g * 128, 768 + (g + 1) * 128))):
        mcw[:, ti * 4:(ti + 1) * 4] = cw[:, sl].T
        mcb[:, ti] = cb[sl]
    mh = np.zeros((128, 6), np.float32)
    mh[:, 0:2] = inp["o_mb_dt_bias"][0][2 * j:2 * j + 2]
    mh[:, 2:4] = inp["o_mb_a_log"][0][2 * j:2 * j + 2]
    mh[:, 4:6] = inp["o_mb_d"][0][2 * j:2 * j + 2]
    return {"xT": xT_b,
            "w_fm": np.ascontiguousarray(np.concatenate([r, kk, v, xg, xw, xa, x, Bm, Cm], 1)),
            "w_tm": np.ascontiguousarray(np.concatenate([z, dt], 1)),
            "nmix": colmajor(inp["norm_mix"][1], 8), "rwp": rwp, "w2p": w2p, "a2p": a2p,
            "g2c": np.ascontiguousarray(inp["o_rw_g2"][0][:, cs]),
            "lnw": np.ascontiguousarray(lnw), "lnb": np.ascontiguousarray(lnb), "mcw": mcw, "mcb": mcb, "mh": mh}

NCORE = 8
GROUPS = [[0, 1, 2, 3], [4, 5, 6, 7]]

A0_IN = {"w_fm": [1024, 336], "w_tm": [1024, 384], "nmix": [128, 8], "wa2": [16, 128], "nba": [128, 1], "gnorm": [128, 192],
         "s5col": [128, 6], "s5b": [128, 256], "s5c": [128, 256], "s5d": [64, 1]}
A1_IN = {"w_fm": [1024, 1024], "w_tm": [1024, 130], "nmix": [128, 8], "rwp": [128, 16], "w2p": [128, 128], "a2p": [128, 128],
         "g2c": [128, 128], "lnw": [128, 64], "lnb": [128, 64], "mcw": [128, 12], "mcb": [128, 3], "mh": [128, 6]}
B_IN = {"w_out": [1024, 1024], "w_up": [1024, 5632], "w_dn": [2816, 1024], "cw": [128, 66], "cb": [128, 22], "nffn": [128, 8]}


def build_fused(L, Bsz=2):
    nc = bass.Bass("TRN2", target_bir_lowering=False)
    k = KB(nc)
    Tc = (Bsz * L) // NCORE
    PT = min(1024, Tc)
    NP = L // PT
    XB = 256
    NXB = Tc // XB

    def ext(prefix, spec):
        return {n: k.dram(prefix + n, shp, F32, kind="ExternalInput") for n, shp in spec.items()}
    a0 = ext("a0_", A0_IN)
    a0["xT"] = k.dram("a0_xT", [1024, L], F32, kind="ExternalInput")
    P0 = k.dram("P0", [NP * 256, PT]); G0 = k.dram("G0", [(NP + 1) * 1024, PT])
    P1 = k.dram("P1", [NP * 256, PT]); G1 = k.dram("G1", [(NP + 1) * 1024, PT])
    x1p = k.dram("x1p", [NXB * 1024, XB]); x1G = k.dram("x1G", [NXB * 4096, XB])
    x1halo = k.dram("x1halo", [1024, 2]); yloc = k.dram("yloc", [1024, 4 + Tc])
    b0 = ext("b0_", dict(B_IN, wglu=[256, 256], bglu=[128, 2]))
    b0["xTb"] = k.dram("b0_xTb", [1024, 4 + Tc], F32, kind="ExternalInput")
    a1 = ext("a1_", A1_IN)
    b1 = ext("b1_", dict(B_IN, nfin=[128, 8], mbn=[128, 4]))
    b1["outT"] = k.dram("outT", [1024, Tc], F32, kind="ExternalOutput")

    def piece_hook(P, G):
        def hook(bi):
            t1 = (bi + 1) * 512
            if t1 % PT == 0:
                p = t1 // PT - 1
                k.wait_ring("pool", "st")
                k.allgather(V(G, G.h[(p + 1) * 1024:(p + 2) * 1024, :]), V(P, P.h[p * 256:(p + 1) * 256, :]), GROUPS)
        return hook

    def zero_pad_piece(G):
        zt = k.sb([128, 4], F32, "zt"); k.memset(zt[:], 0.0)
        for r0 in range(0, 1024, 128):
            k.dma(G[r0:r0 + 128, PT - 4:PT], zt[:], q="pool", lane="st")

    zero_pad_piece(G0)
    a0.update(P=P0, PT=PT, hook=piece_hook(P0, G0))
    emit_A0(k, L, a0)
    k.end_phase()
    def x1_hook(i):
        k.wait_ring("pool", "st")
        k.allgather(V(x1G, x1G.h[i * 4096:(i + 1) * 4096, :]), V(x1p, x1p.h[i * 1024:(i + 1) * 1024, :]), GROUPS)
    b0.update(G=G0, PT=PT, yloc=yloc, x1halo=x1halo, x1p=x1p, hook=x1_hook)
    emit_B(k, 0, Tc, False, b0)
    k.end_phase()
    zero_pad_piece(G1)
    a1.update(x1G=x1G, P=P1, PT=PT, XB=XB, hook=piece_hook(P1, G1))
    emit_A1(k, L, a1, Tc)
    k.end_phase()
    b1.update(G=G1, PT=PT, yloc=yloc, x1halo=x1halo, x1p=x1p)
    emit_B(k, 1, Tc, True, b1)
    k.end_phase()
    print("fused program instructions:", k.ninst)
    return nc


def _b_params(inp, layer, final):
    i = layer
    m = {"w_out": inp["e_w_out"][0] if layer == 0 else inp["o_w_out"][0],
         "w_up": inp["ffn_w_up"][i], "w_dn": inp["ffn_w_down"][i],
         "cw": np.ascontiguousarray(inp["ffn_conv_w"][i].reshape(3, 22, 128).transpose(2, 1, 0).reshape(128, 66)),
         "cb": colmajor(inp["ffn_conv_b"][i], 22), "nffn": colmajor(inp["norm_ffn"][i], 8)}
    if final:
        m["nfin"] = colmajor(inp["norm_final"], 8)
    if layer == 0:
        m["wglu"] = inp["e_s5_w_glu"][0]
        m["bglu"] = colmajor(inp["e_s5_b_glu"][0], 2)
    else:
        m["mbn"] = colmajor(inp["o_mb_norm"][0], 4)
    return m


_NC_CACHE = {}


def kernel(**inputs):
    inp = {k_: np.ascontiguousarray(np.asarray(v, dtype=np.float32)) for k_, v in inputs.items()}
    x = inp["x"]
    Bsz, L = x.shape[0], x.shape[1]
    Tc = (Bsz * L) // NCORE
    per_b = NCORE // Bsz
    if L not in _NC_CACHE:
        _NC_CACHE[L] = build_fused(L, Bsz)
    nc = _NC_CACHE[L]
    xT = [np.ascontiguousarray(x[b].T) for b in range(Bsz)]
    pb0 = _b_params(inp, 0, False)
    pb1 = _b_params(inp, 1, True)
    maps = []
    for c in range(NCORE):
        b, q = c // per_b, c % per_b
        m = {}
        a0 = prep_A0(inp, xT[b], q)
        m.update({"a0_" + n: v for n, v in a0.items()})
        a1 = prep_A1(inp, None, q)
        m.update({"a1_" + n: v for n, v in a1.items() if n != "xT"})
        m.update({"b0_" + n: v for n, v in pb0.items()})
        m.update({"b1_" + n: v for n, v in pb1.items()})
        t0 = q * Tc
        if t0 == 0:
            xb = np.concatenate([np.zeros((1024, 4), np.float32), xT[b][:, 0:Tc]], 1)
        else:
            xb = xT[b][:, t0 - 4:t0 + Tc]
        m["b0_xTb"] = np.ascontiguousarray(xb)
        maps.append(m)
    res = run_bass_kernel_spmd(nc, maps, core_ids=list(range(NCORE))).results
    out = np.empty((Bsz, L, 1024), np.float32)
    for c in range(NCORE):
        b, q = c // per_b, c % per_b
        out[b, q * Tc:(q + 1) * Tc, :] = res[c]["outT"].T
    return out
```

```python
import contextlib
import numpy as np
import concourse.bass as bass
import concourse.mybir as mybir
from concourse.bass_utils import run_bass_kernel_spmd

F32 = mybir.dt.float32
BF16 = mybir.dt.bfloat16
ALU = mybir.AluOpType
AF = mybir.ActivationFunctionType
AX = mybir.AxisListType


class Lane:
    def __init__(self, nc, name, inc):
        self.sem = nc.alloc_semaphore(name)
        self.name = name
        self.inc = inc
        self.count = 0


class Buf:
    def __init__(self, h, name, psum=False):
        self.h = h
        self.name = name
        self.psum = psum
        self.last_w = None
        self.readers = {}

    def __getitem__(self, key):
        return V(self, self.h[key])


class V:
    def __init__(self, buf, ap):
        self.buf = buf
        self.ap = ap

    def __getitem__(self, key):
        return V(self.buf, self.ap[key])

    def rearrange(self, s, **kw):
        return V(self.buf, self.ap.rearrange(s, **kw))

    def bcast(self, shape):
        return V(self.buf, self.ap.to_broadcast(shape))

    def bitcast(self, dt):
        return V(self.buf, self.ap.bitcast(dt))


class KB:
    def __init__(self, nc):
        self.nc = nc
        self.eng = {"pe": nc.tensor, "act": nc.scalar, "dve": nc.vector, "pool": nc.gpsimd, "sp": nc.sync}
        self.lanes = {n: Lane(nc, "L" + n, 1) for n in ("pe", "act", "dve", "pool")}
        self.waited = {}
        self.rings = {}
        self.stack = contextlib.ExitStack()
        self.cclane = Lane(nc, "Lcc", 1)
        self.pid = {}
        self.rec = None
        self.nbuf = 0
        self.ninst = 0

    def sb(self, shape, dt=F32, name=None):
        self.nbuf += 1
        name = (name or "t") + "_%d" % self.nbuf
        return Buf(self.stack.enter_context(self.nc.sbuf_tensor(name, list(shape), dt)), name)

    def ps(self, shape, dt=F32, name=None):
        self.nbuf += 1
        name = (name or "p") + "_%d" % self.nbuf
        return Buf(self.stack.enter_context(self.nc.psum_tensor(name, list(shape), dt)), name, psum=True)

    def dram(self, name, shape, dt=F32, kind="Internal"):
        h = self.nc.dram_tensor(name, list(shape), dt, kind=kind)
        return Buf(h.ap(), name)

    RING = 8

    def dma_lane(self, name, ename):
        if name not in self.rings:
            self.rings[name] = [[Lane(self.nc, "D%s_%d" % (name, i), 16) for i in range(self.RING)], 0]
        ring = self.rings[name]
        lane = ring[0][ring[1] % self.RING]
        ring[1] += 1
        if lane.count > 0:
            wk = (ename, lane.name)
            if self.waited.get(wk, 0) < lane.count:
                self.waited[wk] = lane.count
                self.eng[ename].wait_ge(lane.sem, lane.count * lane.inc)
                self.ninst += 1
        return lane

    def _need(self, ename, deps, lane, cnt):
        if cnt <= 0:
            return
        key = lane.name
        if deps.get(key, (None, 0))[1] < cnt:
            deps[key] = (lane, cnt)

    def record(self):
        self.rec = []
        return self.rec

    def stop_record(self):
        self.rec = None

    def emit_merged(self, A, B):
        la, lb = len(A), len(B)
        ia = ib = 0
        while ia < la or ib < lb:
            if ib >= lb or (ia < la and ia * lb <= ib * la):
                self.op(*A[ia]); ia += 1
            else:
                self.op(*B[ib]); ib += 1

    def op(self, ename, fn, reads=(), writes=(), lane=None, pe_acc=False):
        if self.rec is not None:
            self.rec.append((ename, fn, list(reads), list(writes), lane, pe_acc))
            return None
        e = self.eng[ename]
        if lane is None:
            lane = self.lanes[ename]
        elif isinstance(lane, str):
            lane = self.dma_lane(lane, ename)
        deps = {}
        rb = []
        for r in reads:
            b = r.buf if isinstance(r, V) else r
            if b is None:
                continue
            rb.append(b)
            if b.last_w is not None:
                self._need(ename, deps, *b.last_w)
            if b.psum:
                for ln, (l, c) in b.readers.items():
                    if l is not lane:
                        self._need(ename, deps, l, c)
        wb = []
        for w in writes:
            b = w.buf if isinstance(w, V) else w
            wb.append(b)
            if b.last_w is not None:
                if not (pe_acc and b.last_w[0] is lane):
                    self._need(ename, deps, *b.last_w)
            for ln, (l, c) in b.readers.items():
                self._need(ename, deps, l, c)
        for key, (l, c) in deps.items():
            wk = (ename, key)
            if self.waited.get(wk, 0) >= c:
                continue
            self.waited[wk] = c
            e.wait_ge(l.sem, c * l.inc)
            self.ninst += 1
        ins = fn(e)
        lane.count += 1
        ins.then_inc(lane.sem, lane.inc)
        self.ninst += 1
        for b in wb:
            b.last_w = (lane, lane.count)
            b.readers = {}
        for b in rb:
            if b in wb:
                continue
            b.readers[lane.name] = (lane, lane.count)
        return ins

    def dma(self, out, in_, q="sp", lane="ld", **kw):
        return self.op(q, lambda e: e.dma_start(out=out.ap, in_=in_.ap, **kw), reads=[in_], writes=[out], lane=lane)

    def mm(self, out, lhsT, rhs, start=True, stop=True, **kw):
        return self.op("pe", lambda e: e.matmul(out.ap, lhsT.ap, rhs.ap, start=start, stop=stop, **kw),
                       reads=[lhsT, rhs], writes=[out], pe_acc=not start)

    def transpose(self, out, in_, ident):
        return self.op("pe", lambda e: e.transpose(out.ap, in_.ap, ident.ap), reads=[in_, ident], writes=[out])

    def act(self, out, in_, func, bias=None, scale=None, accum=None, eng="act"):
        kw = {}
        reads = [in_]
        if bias is not None:
            if isinstance(bias, V):
                kw["bias"] = bias.ap
                reads.append(bias)
            else:
                kw["bias"] = bias
        if scale is not None:
            if isinstance(scale, V):
                kw["scale"] = scale.ap
                reads.append(scale)
            else:
                kw["scale"] = scale
        writes = [out]
        if accum is not None:
            kw["accum_out"] = accum.ap
            writes.append(accum)
        return self.op(eng, lambda e: e.activation(out.ap, in_.ap, func, **kw), reads=reads, writes=writes)

    def tt(self, out, in0, in1, op, eng="dve"):
        return self.op(eng, lambda e: e.tensor_tensor(out.ap, in0.ap, in1.ap, op), reads=[in0, in1], writes=[out])

    def ts(self, out, in0, s1, op0, s2=None, op1=None, eng="dve", accum=None):
        reads = [in0]
        a1 = s1
        a2 = s2
        if isinstance(s1, V):
            reads.append(s1)
            a1 = s1.ap
        if isinstance(s2, V):
            reads.append(s2)
            a2 = s2.ap
        kw = {}
        writes = [out]
        if accum is not None:
            kw["accum_out"] = accum.ap
            writes.append(accum)
        if op1 is None:
            return self.op(eng, lambda e: e.tensor_scalar(out.ap, in0.ap, a1, None, op0, **kw), reads=reads, writes=writes)
        return self.op(eng, lambda e: e.tensor_scalar(out.ap, in0.ap, a1, a2, op0, op1, **kw), reads=reads, writes=writes)

    def stt(self, out, in0, scalar, in1, op0, op1, eng="dve"):
        reads = [in0, in1]
        a = scalar
        if isinstance(scalar, V):
            reads.append(scalar)
            a = scalar.ap
        return self.op(eng, lambda e: e.scalar_tensor_tensor(out.ap, in0.ap, a, in1.ap, op0, op1), reads=reads, writes=[out])

    def scan(self, out, d0, d1, init, op0=ALU.mult, op1=ALU.add, eng="dve"):
        reads = [d0, d1]
        a = init
        if isinstance(init, V):
            reads.append(init)
            a = init.ap
        return self.op(eng, lambda e: e.tensor_tensor_scan(out.ap, d0.ap, d1.ap, a, op0, op1), reads=reads, writes=[out])

    def copy(self, out, in_, eng="dve"):
        if eng == "act":
            return self.op("act", lambda e: e.copy(out.ap, in_.ap), reads=[in_], writes=[out])
        return self.op(eng, lambda e: e.tensor_copy(out.ap, in_.ap), reads=[in_], writes=[out])

    def memset(self, out, val, eng="pool"):
        return self.op(eng, lambda e: e.memset(out.ap, val), reads=[], writes=[out])

    def all_lanes(self):
        ls = list(self.lanes.values()) + [self.cclane]
        for name, (lanes, _) in self.rings.items():
            ls.extend(lanes)
        return ls

    def end_phase(self):
        for ename, e in self.eng.items():
            for l in self.all_lanes():
                if l.count > 0 and self.waited.get((ename, l.name), 0) < l.count:
                    self.waited[(ename, l.name)] = l.count
                    e.wait_ge(l.sem, l.count * l.inc)
                    self.ninst += 1
        self.stack.close()
        self.stack = contextlib.ExitStack()

    def allgather(self, dst, src, groups):
        return self.op("pool", lambda e: e.collective_compute("AllGather", ALU.bypass, replica_groups=groups,
                                                              ins=[src[:].ap.opt()], outs=[dst[:].ap.opt()]),
                       reads=[src], writes=[dst], lane=self.cclane)

    def dyn_dma(self, out, buf, row0, nrows, col_static, n, qscale, ename="sp", lane="ld"):
        e = self.eng[ename]
        key = (ename, qscale)
        if key not in self.pid:
            qreg = e.to_reg((e.partition_id() % 4) * qscale)
            self.pid[key] = (qreg, e.alloc_register("dynoff_%s_%d" % (ename, qscale)))
        qreg, r = self.pid[key]
        tens = buf.h.tensor
        rowlen = buf.h.shape[1]

        def fn(e_):
            e_.reg_add(r, qreg, row0 * rowlen + col_static)
            return e_.dma_start(out=out.ap, in_=bass.AP(tens, r, [[rowlen, nrows], [1, n]]))
        return self.op(ename, fn, reads=[buf], writes=[out], lane=lane)

    def dyn_dma3(self, out_ap, out_buf, buf, qscale, static_off, pattern, ename="sp", lane="dyn"):
        e = self.eng[ename]
        key = (ename, qscale)
        if key not in self.pid:
            qreg = e.to_reg((e.partition_id() % 4) * qscale)
            self.pid[key] = (qreg, e.alloc_register("dynoff_%s_%d" % (ename, qscale)))
        qreg, r = self.pid[key]
        tens = buf.h.tensor

        def fn(e_):
            e_.reg_add(r, qreg, static_off)
            return e_.dma_start(out=out_ap, in_=bass.AP(tens, r, pattern))
        return self.op(ename, fn, reads=[buf], writes=[out_buf], lane=lane)

    def wait_ring(self, ename, ring):
        if ring not in self.rings:
            return
        e = self.eng[ename]
        for l in self.rings[ring][0]:
            if l.count > 0 and self.waited.get((ename, l.name), 0) < l.count:
                self.waited[(ename, l.name)] = l.count
                e.wait_ge(l.sem, l.count * l.inc)
                self.ninst += 1

    def core_q(self, ename, mod):
        key = (ename, mod)
        if key not in self.pid:
            self.pid[key] = self.eng[ename].partition_id() % mod
        return self.pid[key]

    def finish(self, bufs):
        self.end_phase()

import math

D = 1024
EPS = 1e-6
TWO_PI = 2.0 * math.pi
MAGIC = 12582912.0


def cast_w(k, dst, src, ncols, nmix, stage, cnt=[0]):
    engs = ("act", "dve")
    piece = stage[0].h.shape[1]
    for kc in range(8):
        for c0 in range(0, ncols, piece):
            n = min(piece, ncols - c0)
            st = stage[cnt[0] % len(stage)]
            eng = engs[cnt[0] % len(engs)]
            cnt[0] += 1
            k.dma(st[:, 0:n], src[kc * 128:(kc + 1) * 128, c0:c0 + n], q="sp", lane="wld")
            if eng == "act":
                k.op("act", lambda e: e.mul(dst[kc][:, c0:c0 + n].ap, st[:, 0:n].ap, nmix[:, kc:kc + 1].ap), reads=[st, nmix], writes=[dst[kc]])
            else:
                k.ts(dst[kc][:, c0:c0 + n], st[:, 0:n], nmix[:, kc:kc + 1], ALU.mult, eng=eng)


def make_consts(k):
    ident = k.sb([128, 128], F32, "ident"); k.memset(ident[:], 0.0)
    k.op("pool", lambda e: e.affine_select(out=ident[:].ap, in_=ident[:].ap, compare_op=ALU.not_equal, fill=1.0, base=0,
                                           pattern=[[-1, 128]], channel_multiplier=1), reads=[ident], writes=[ident])
    identb = k.sb([128, 128], BF16, "identb"); k.copy(identb[:], ident[:])
    maskT = k.sb([128, 128], F32, "maskT"); k.memset(maskT[:], 1.0)
    k.op("pool", lambda e: e.affine_select(out=maskT[:].ap, in_=maskT[:].ap, compare_op=ALU.is_ge, fill=0.0, base=0,
                                           pattern=[[1, 128]], channel_multiplier=-1), reads=[maskT], writes=[maskT])
    ones = k.sb([128, 128], F32, "ones"); k.memset(ones[:], 1.0)
    onesb = k.sb([128, 128], BF16, "onesb"); k.memset(onesb[:], 1.0)
    return ident, identb, maskT, ones, onesb


def rsqrt_small(k, out, in_, mul, add):
    k.ts(out, in_, mul, ALU.mult, add, ALU.add)
    k.act(out, out, AF.Ln)
    k.act(out, out, AF.Exp, scale=-0.5)


def recip_1p(k, t):
    k.ts(t, t, 1.0, ALU.add)
    k.op("dve", lambda e: e.reciprocal(t.ap, t.ap), reads=[t], writes=[t])


def emit_A0(k, L, io, TB=512, PAD=4):
    skip = ()
    NB = L // TB
    xT = io["xT"]; w_fm = io["w_fm"]; w_tm = io["w_tm"]; nmix_d = io["nmix"]; wa2_d = io["wa2"]; nba_d = io["nba"]
    gn_d = io["gnorm"]; s5col_d = io["s5col"]; s5b_d = io["s5b"]; s5c_d = io["s5c"]; s5d_d = io["s5d"]
    P_o = io["P"]; PT = io["PT"]

    ident, identb, maskT, ones, onesb = make_consts(k)
    yaTa = [k.sb([128, TB], F32, "yaTa") for _ in range(2)]
    yaTb = [k.sb([64, TB], F32, "yaTb") for _ in range(2)]
    nmix = k.sb([128, 8]); k.dma(nmix[:], nmix_d[:], lane="wld")
    wa2 = k.sb([16, 128]); k.dma(wa2[:], wa2_d[:], lane="wld")
    nba = k.sb([128, 1]); k.dma(nba[:], nba_d[:], lane="wld")
    k.ts(nba[:], nba[:], -1.0, ALU.mult)
    gn = k.sb([128, 192]); k.dma(gn[:], gn_d[:], lane="wld")
    s5col = k.sb([128, 6]); k.dma(s5col[:], s5col_d[:], lane="wld")
    s5b = k.sb([128, 256]); k.dma(s5b[:], s5b_d[:], lane="wld")
    s5c = k.sb([128, 256]); k.dma(s5c[:], s5c_d[:], lane="wld")
    s5d = k.sb([64, 1]); k.dma(s5d[:], s5d_d[:], lane="wld")

    stage = [k.sb([128, 384], F32, "stage") for _ in range(3)]
    wfm = [k.sb([128, 336], BF16, "wfm") for _ in range(8)]
    wtm = [k.sb([128, 384], BF16, "wtm") for _ in range(8)]
    cast_w(k, wfm, w_fm, 336, nmix, stage)
    cast_w(k, wtm, w_tm, 384, nmix, stage)

    PS = [k.ps([128, 512], F32, "ps") for _ in range(7)]
    PSB = k.ps([128, 1024], BF16, "psb")
    psi = [0]

    psel = [None]
    psa = [0]; psb = [0]
    NA = len(PS) - 3

    def nps():
        if psel[0] == "A":
            p = PS[psa[0] % NA]; psa[0] += 1
        elif psel[0] == "B":
            p = PS[NA + psb[0] % 3]; psb[0] += 1
        else:
            p = PS[psi[0] % len(PS)]; psi[0] += 1
        return p

    cosT = []; sinT = []; rho = []; BbT = []; Cbd = []
    sm = k.sb([128, 32], F32, "s5small")
    for st in (range(2) if 's5setup' not in skip else []):
        lre = s5col[:, st * 3 + 0:st * 3 + 1]; lim = s5col[:, st * 3 + 1:st * 3 + 2]; ldt = s5col[:, st * 3 + 2:st * 3 + 3]
        c = lambda i: sm[:, i:i + 1]
        dt, a, ang, mag, nrd, angr, sn, cs_, tmp, lbr, lbi, nr, den, fre, fim, t2 = [c(i) for i in range(16)]
        k.act(dt, ldt, AF.Exp)
        k.tt(a, lre, dt, ALU.mult)
        k.tt(ang, lim, dt, ALU.mult)
        k.act(mag, a, AF.Exp)
        k.ts(nrd, ang, 1.0 / TWO_PI, ALU.mult)
        k.ts(nrd, nrd, MAGIC, ALU.add)
        k.ts(nrd, nrd, -MAGIC, ALU.add)
        k.stt(angr, nrd, -TWO_PI, ang, ALU.mult, ALU.add)
        k.ts(angr, angr, math.pi, ALU.min, -math.pi, ALU.max)
        k.act(sn, angr, AF.Sin)
        k.ts(tmp, angr, -1.0, ALU.mult); k.tt(tmp, tmp, angr, ALU.max)
        k.ts(tmp, tmp, -1.0, ALU.mult, math.pi / 2, ALU.add)
        k.act(cs_, tmp, AF.Sin)
        k.tt(lbr, mag, cs_, ALU.mult)
        k.tt(lbi, mag, sn, ALU.mult)
        k.ts(nr, lbr, -1.0, ALU.add)
        k.tt(den, lre, lre, ALU.mult)
        k.stt(den, lim, lim, den, ALU.mult, ALU.add)
        k.op("dve", lambda e: e.reciprocal(den.ap, den.ap), reads=[den], writes=[den])
        k.tt(fre, nr, lre, ALU.mult)
        k.stt(fre, lbi, lim, fre, ALU.mult, ALU.add)
        k.tt(fre, fre, den, ALU.mult)
        k.tt(fim, lbi, lre, ALU.mult)
        k.tt(t2, nr, lim, ALU.mult)
        k.tt(fim, fim, t2, ALU.subtract)
        k.tt(fim, fim, den, ALU.mult)
        r_ = k.sb([128, TB], F32, "rho"); k.ts(r_[:], ones[:, 0:1].bcast([128, TB]), mag, ALU.mult)
        rho.append(r_)
        ct = k.sb([128, TB], F32, "cosT"); stb = k.sb([128, TB], F32, "sinT")
        k.copy(ct[:, 0:1], cs_); k.copy(stb[:, 0:1], sn)
        n = 1
        tA = k.sb([128, TB // 2], F32, "tA")
        while n < TB:
            cr = ct[:, n - 1:n]; si = stb[:, n - 1:n]
            k.ts(tA[:, 0:n], stb[:, 0:n], si, ALU.mult)
            k.stt(ct[:, n:2 * n], ct[:, 0:n], cr, tA[:, 0:n], ALU.mult, ALU.subtract)
            k.ts(tA[:, 0:n], stb[:, 0:n], cr, ALU.mult)
            k.stt(stb[:, n:2 * n], ct[:, 0:n], si, tA[:, 0:n], ALU.mult, ALU.add)
            n *= 2
        cosT.append(ct); sinT.append(stb)
        bre = s5b[:, st * 128:st * 128 + 64]; bim = s5b[:, st * 128 + 64:st * 128 + 128]
        bb = k.sb([128, 128], F32, "bb")
        k.ts(bb[:, 0:64], bim, fim, ALU.mult)
        k.stt(bb[:, 0:64], bre, fre, bb[:, 0:64], ALU.mult, ALU.subtract)
        k.ts(bb[:, 64:128], bre, fim, ALU.mult)
        k.stt(bb[:, 64:128], bim, fre, bb[:, 64:128], ALU.mult, ALU.add)
        bt = k.sb([64, 256], F32, "BbT")
        for ri in range(2):
            p = nps()
            k.transpose(p[0:64, 0:128], bb[:, ri * 64:(ri + 1) * 64], ident[:])
            k.copy(bt[:, ri * 128:(ri + 1) * 128], p[0:64, 0:128])
        BbT.append(bt)
        cb_ = k.sb([128, 128], F32, "Cbd")
        k.copy(cb_[:, 0:64], s5c[:, st * 128:st * 128 + 64])
        k.ts(cb_[:, 64:128], s5c[:, st * 128 + 64:st * 128 + 128], -1.0, ALU.mult)
        Cbd.append(cb_)
    s5carry = [[k.sb([128, 1], F32, "s5carry") for _ in range(2)] for _ in range(2)]
    for st in range(2):
        for ri in range(2):
            k.memset(s5carry[st][ri][:], 0.0)

    S = k.sb([128, 192], F32, "S"); k.memset(S[:], 0.0)
    Sb = k.sb([128, 192], BF16, "Sb"); k.memset(Sb[:], 0.0)
    cmask = k.sb([128, TB], F32, "cmask"); k.memset(cmask[:], 1.0)
    for c in range(TB // 128):
        k.memset(cmask[:, c * 128:c * 128 + 1], 0.0)

    xs = [k.sb([128, TB], F32, "xs") for _ in range(8)]
    sqb = [k.sb([128, TB], BF16, "sqb") for _ in range(2)]
    hb = [k.sb([128, TB], BF16, "hb") for _ in range(8)]
    rstd = k.sb([128, TB], F32, "rstd")
    alo = k.sb([16, TB], F32, "alo")
    lsp = k.sb([128, TB], F32, "lsp")
    cs = k.sb([128, TB], F32, "cs")
    eb = k.sb([128, TB], F32, "eb")
    enb = k.sb([128, TB], F32, "enb")
    qt = k.sb([128, TB], BF16, "qt")
    kt = k.sb([128, TB], BF16, "kt")
    ksb = k.sb([128, TB], F32, "ksb")
    ncl = k.sb([128, 4], F32, "ncl")
    vb = [k.sb([128, 192], BF16, "vb") for _ in range(4)]
    gsl = [k.sb([128, 192], F32, "gsl") for _ in range(4)]
    ekh = [k.sb([128, 128], F32, "ekh") for _ in range(2)]
    khT = [k.sb([128, 128], BF16, "khT") for _ in range(2)]
    kh = [k.sb([128, 128], BF16, "kh") for _ in range(2)]
    att = [k.sb([128, 128], BF16, "att") for _ in range(2)]
    osq = k.sb([128, 192], F32, "osq")
    ssq = [k.sb([128, 1], F32, "ssq") for _ in range(2)]
    yo = [k.sb([128, 192], F32, "yo") for _ in range(2)]
    us = k.sb([64, TB], F32, "us")
    burs = [[k.sb([128, TB], F32, "bur") for _ in range(2)] for _ in range(2)]
    xrs = [[k.sb([128, TB], F32, "xr") for _ in range(2)] for _ in range(2)]
    t1p = [k.sb([128, TB], F32, "t1p") for _ in range(2)]
    t1d = [k.sb([128, TB], F32, "t1d") for _ in range(2)]
    wscs = [[k.sb([128, TB], F32, "wsc") for _ in range(2)] for _ in range(2)]
    sre = [[k.sb([128, TB], F32, "sre") for _ in range(2)] for _ in range(2)]
    yz = k.sb([64, TB], F32, "yz")
    gz = [k.sb([64, TB], F32, "gz") for _ in range(2)]
    zo = [k.sb([64, TB], F32, "zo") for _ in range(2)]

    for bi in range(NB):
        t0 = bi * TB
        pss = nps()
        for kc in range(8):
            k.dma(xs[kc][:], xT[kc * 128:(kc + 1) * 128, t0:t0 + TB], q="sp", lane="xld")
            k.act(sqb[kc % 2][:], xs[kc][:], AF.Square)
            k.mm(pss[:], onesb[:], sqb[kc % 2][:], start=(kc == 0), stop=(kc == 7))
        rsqrt_small(k, rstd[:], pss[:], 1.0 / D, EPS)
        for kc in range(8):
            k.tt(hb[kc][:], xs[kc][:], rstd[:], ALU.mult, eng=("pool" if kc % 2 else "dve"))
        pq = nps(); pk = nps(); pu = nps(); pa = nps()
        for (p, c0, m) in ((pq, 0, 128), (pk, 128, 128), (pu, 256, 64), (pa, 320, 16)):
            for kc in range(8):
                k.mm(p[0:m, :], wfm[kc][:, c0:c0 + m], hb[kc][:], start=(kc == 0), stop=(kc == 7))
        if 'gates' in skip:
            continue
        k.copy(alo[:], pa[0:16, :], eng="act")
        if 'g1' in skip:
            continue
        pl = nps()
        k.mm(pl[:], wa2[:], alo[:])
        k.act(lsp[:], pl[:], AF.Exp, scale=-1.0, bias=nba[:, 0:1])
        k.act(lsp[:], lsp[:], AF.Ln, bias=1.0)
        if 'g2' in skip:
            continue
        k.scan(cs[:], cmask[:], lsp[:], 0.0)
        k.act(eb[:], cs[:], AF.Exp, scale=-1.0 / 16.0)
        k.act(enb[:], cs[:], AF.Exp, scale=1.0 / 16.0)
        k.stt(qt[:], pq[:], 128.0 ** -0.5, eb[:], ALU.mult, ALU.mult)
        k.tt(kt[:], pk[:], enb[:], ALU.mult)
        k.copy(ksb[:], pk[:], eng="act")
        for c in range(4):
            k.ts(ncl[:, c:c + 1], cs[:, c * 128 + 127:c * 128 + 128], -1.0 / 16.0, ALU.mult)
        if 'g3' in skip:
            continue
        k.copy(us[:], pu[0:64, :], eng="act")
        if 'g4' in skip:
            continue
        for s in range(4):
            pv = nps()
            for kc in range(8):
                k.mm(pv[:, 0:384], hb[kc][:, s * 128:(s + 1) * 128], wtm[kc][:], start=(kc == 0), stop=(kc == 7))
            k.copy(vb[s][:], pv[:, 0:192], eng="act")
            k.act(gsl[s][:], pv[:, 192:384], AF.Exp, scale=-1.0)
            recip_1p(k, gsl[s][:])
            k.tt(gsl[s][:], gsl[s][:], gn[:], ALU.mult, eng="pool")
            k.tt(gsl[s][:], gsl[s][:], pv[:, 192:384], ALU.mult)
        recA = k.record(); psel[0] = "A"
        for c in (range(4) if 'gla' not in skip else []):
            sl = slice(c * 128, (c + 1) * 128)
            i2 = c % 2
            k.act(ekh[i2][:], cs[:, sl], AF.Exp, scale=1.0 / 16.0, bias=ncl[:, c:c + 1])
            k.tt(khT[i2][:], ksb[:, sl], ekh[i2][:], ALU.mult, eng="pool")
            k.transpose(PSB[:, i2 * 128:(i2 + 1) * 128], khT[i2][:], identb[:])
            k.copy(kh[i2][:], PSB[:, i2 * 128:(i2 + 1) * 128], eng="act")
            pa_ = nps()
            k.mm(pa_[:, 0:128], kt[:, sl], qt[:, sl])
            k.tt(att[i2][:], pa_[:, 0:128], maskT[:], ALU.mult)
            po = nps()
            k.mm(po[:, 0:192], att[i2][:], vb[c][:], start=True, stop=False)
            k.mm(po[:, 0:192], qt[:, sl], Sb[:], start=False, stop=True)
            pst = nps()
            k.mm(pst[:, 0:192], kh[i2][:], vb[c][:])
            k.stt(S[:], S[:], eb[:, c * 128 + 127:c * 128 + 128], pst[:, 0:192], ALU.mult, ALU.add)
            k.copy(Sb[:], S[:], eng="act")
            k.act(osq[:], po[:, 0:192], AF.Square, accum=ssq[i2][:])
            rsqrt_small(k, ssq[i2][:], ssq[i2][:], 1.0 / 192.0, EPS)
            k.stt(yo[i2][:], po[:, 0:192], ssq[i2][:, 0:1], gsl[c][:], ALU.mult, ALU.mult)
            pt1 = nps(); k.transpose(pt1[:, 0:128], yo[i2][:, 0:128], ident[:])
            k.copy(yaTa[bi % 2][:, sl], pt1[:, 0:128], eng="act")
            pt2 = nps(); k.transpose(pt2[0:64, 0:128], yo[i2][:, 128:192], ident[:])
            k.copy(yaTb[bi % 2][:, sl], pt2[0:64, 0:128], eng="act")
            if c == 3:
                pr0 = (t0 // PT) * 256; pc0 = t0 % PT
                k.dma(P_o[pr0:pr0 + 128, pc0:pc0 + TB], yaTa[bi % 2][:], q="pool", lane="st")
                k.dma(P_o[pr0 + 128:pr0 + 192, pc0:pc0 + TB], yaTb[bi % 2][:], q="pool", lane="st")
        recB = k.record(); psel[0] = "B"
        for st in range(2):
            pbr = nps(); pbi = nps()
            k.mm(pbr[:], BbT[st][:, 0:128], us[:])
            k.mm(pbi[:], BbT[st][:, 128:256], us[:])
            bur = burs[st]; xr = xrs[st]; wsc = wscs[st]
            k.copy(bur[0][:], pbr[:], eng="act")
            k.copy(bur[1][:], pbi[:], eng="act")
            k.tt(t1p[0][:], bur[0][:], cosT[st][:], ALU.mult, eng="pool")
            k.tt(t1p[1][:], bur[1][:], sinT[st][:], ALU.mult, eng="pool")
            k.tt(xr[0][:], t1p[0][:], t1p[1][:], ALU.add, eng="pool")
            k.tt(t1d[0][:], bur[1][:], cosT[st][:], ALU.mult)
            k.tt(t1d[1][:], bur[0][:], sinT[st][:], ALU.mult)
            k.tt(xr[1][:], t1d[0][:], t1d[1][:], ALU.subtract)
            for ri in range(2):
                k.scan(wsc[ri][:], rho[st][:], xr[ri][:], s5carry[st][ri][:, 0:1])
            k.tt(t1p[0][:], wsc[0][:], cosT[st][:], ALU.mult, eng="pool")
            k.tt(t1p[1][:], wsc[1][:], sinT[st][:], ALU.mult, eng="pool")
            k.tt(sre[st][0][:], t1p[0][:], t1p[1][:], ALU.subtract, eng="pool")
            k.tt(t1d[0][:], wsc[0][:], sinT[st][:], ALU.mult)
            k.tt(t1d[1][:], wsc[1][:], cosT[st][:], ALU.mult)
            k.tt(sre[st][1][:], t1d[0][:], t1d[1][:], ALU.add)
            for ri in range(2):
                k.copy(s5carry[st][ri][:], sre[st][ri][:, TB - 1:TB], eng="act")
        py = nps()
        for st in range(2):
            for ri in range(2):
                k.mm(py[0:64, :], Cbd[st][:, ri * 64:(ri + 1) * 64], sre[st][ri][:], start=(st == 0 and ri == 0), stop=(st == 1 and ri == 1))
        k.stt(yz[:], us[:], s5d[:, 0:1], py[0:64, :], ALU.mult, ALU.add)
        g0 = gz[0]; g1 = gz[1]
        k.act(g0[:], yz[:], AF.Square)
        k.ts(g0[:], g0[:], 0.044715 * 0.7978845608028654, ALU.mult, 0.7978845608028654, ALU.add)
        k.tt(g0[:], g0[:], yz[:], ALU.mult)
        k.act(g1[:], g0[:], AF.Exp, scale=-2.0)
        recip_1p(k, g1[:])
        zz = zo[bi % 2]
        k.tt(zz[:], g1[:], yz[:], ALU.mult)
        pr0 = (t0 // PT) * 256; pc0 = t0 % PT
        k.dma(P_o[pr0 + 192:pr0 + 256, pc0:pc0 + TB], zz[:], q="pool", lane="st")
        k.stop_record(); psel[0] = None
        k.emit_merged(recA, recB)
        io["hook"](bi)

import math

D = 1024
EPS = 1e-6
GN_EPS = 64e-5
CW = 64
EM05 = math.exp(-0.5)


def b3(v):
    return V(v.buf, v.ap.unsqueeze(1).to_broadcast([128, 2, 64]))


def r3(v):
    return v.rearrange("p (a b) -> p a b", a=2)


def emit_A1(k, L, io, Tc, TB=512, PAD=2):
    skip = ()
    NB = L // TB
    NFM = 8 * 128
    x1G = io["x1G"]; w_fm = io["w_fm"]; w_tm = io["w_tm"]; nmix_d = io["nmix"]; rwp_d = io["rwp"]; w2_d = io["w2p"]; a2_d = io["a2p"]
    g2_d = io["g2c"]; lnw_d = io["lnw"]; lnb_d = io["lnb"]; mcw_d = io["mcw"]; mcb_d = io["mcb"]; mh_d = io["mh"]
    P_o = io["P"]; PT = io["PT"]; XB = io["XB"]

    ident, identb, maskT, ones, onesb = make_consts(k)
    bmask = k.sb([128, 128], F32, "bmask"); k.memset(bmask[:], 0.0); k.memset(bmask[0:64, 0:64], 1.0); k.memset(bmask[64:128, 64:128], 1.0)
    mI = k.sb([128, 128], F32, "mI"); k.tt(mI[:], maskT[:], bmask[:], ALU.mult)
    mS = k.sb([128, 128], F32, "mS"); k.tt(mS[:], mI[:], ident[:], ALU.subtract)
    mSl = k.sb([128, 128], F32, "mSl")
    pt_ = k.ps([128, 512], F32, "ptmp")
    k.transpose(pt_[:, 0:128], mS[:], ident[:]); k.copy(mSl[:], pt_[:, 0:128])
    E = k.sb([128, 64], F32, "E"); k.memset(E[:], 0.0)
    k.op("pool", lambda e: e.affine_select(out=E[:].ap, in_=E[:].ap, compare_op=ALU.not_equal, fill=1.0, base=0, pattern=[[-1, 64]], channel_multiplier=1), reads=[E], writes=[E])
    k.op("pool", lambda e: e.affine_select(out=E[:].ap, in_=E[:].ap, compare_op=ALU.not_equal, fill=1.0, base=-64, pattern=[[-1, 64]], channel_multiplier=1), reads=[E], writes=[E])

    ycT = [k.sb([64, 2 * TB], F32, "ycT") for _ in range(2)]
    ydT = [k.sb([128, TB], F32, "ydT") for _ in range(2)]
    nmix = k.sb([128, 8]); k.dma(nmix[:], nmix_d[:], lane="wld")
    rwp = k.sb([128, 16]); k.dma(rwp[:], rwp_d[:], lane="wld")
    w2p = k.sb([128, 128]); k.dma(w2p[:], w2_d[:], lane="wld")
    a2p = k.sb([128, 128]); k.dma(a2p[:], a2_d[:], lane="wld")
    g2c = k.sb([128, 128]); k.dma(g2c[:], g2_d[:], lane="wld")
    lnw = k.sb([128, 64]); k.dma(lnw[:], lnw_d[:], lane="wld")
    lnb = k.sb([128, 64]); k.dma(lnb[:], lnb_d[:], lane="wld")
    mcw = k.sb([128, 12]); k.dma(mcw[:], mcw_d[:], lane="wld")
    mcb = k.sb([128, 3]); k.dma(mcb[:], mcb_d[:], lane="wld")
    mh = k.sb([128, 6]); k.dma(mh[:], mh_d[:], lane="wld")
    omka = k.sb([128, 1]); k.ts(omka[:], rwp[:, 8:9], -1.0, ALU.mult, 1.0, ALU.add)
    nrwp = k.sb([128, 16]); k.ts(nrwp[:], rwp[:], -1.0, ALU.mult)
    negA = k.sb([128, 2]); k.act(negA[:], mh[:, 2:4], AF.Exp); k.ts(negA[:], negA[:], -1.0, ALU.mult)

    stage = [k.sb([128, 512], F32, "stage") for _ in range(2)]
    wfm = [k.sb([128, NFM], BF16, "wfm") for _ in range(8)]
    wtm = [k.sb([128, 130], BF16, "wtm") for _ in range(8)]
    cast_w(k, wfm, w_fm, NFM, nmix, stage)
    cast_w(k, wtm, w_tm, 130, nmix, stage)

    PS = [pt_] + [k.ps([128, 512], F32, "ps") for _ in range(7)]
    psi = [0]

    psel = [None]
    psa = [0]; psb = [0]
    NA = len(PS) - 3

    def nps():
        if psel[0] == "A":
            p = PS[psa[0] % NA]; psa[0] += 1
        elif psel[0] == "B":
            p = PS[NA + psb[0] % 3]; psb[0] += 1
        else:
            p = PS[psi[0] % len(PS)]; psi[0] += 1
        return p

    def T(shape, name, dt=F32):
        return k.sb(shape, dt, name)

    Hpk = T([128, 64], "Hpk"); k.memset(Hpk[:], 0.0)
    HT = T([128, 128], "HT"); k.memset(HT[:], 0.0)
    cmask = T([128, TB], "cmask"); k.memset(cmask[:], 1.0)
    for c in range(TB // CW):
        k.memset(cmask[:, c * CW:c * CW + 1], 0.0)
    xs = [T([128, TB], "xs") for _ in range(8)]
    sqb = [T([128, TB], "sqb", BF16) for _ in range(2)]
    hb = [T([128, TB], "hb", BF16) for _ in range(8)]
    rstd = T([128, TB], "rstd")
    Psb = [T([128, TB + 1], "Psb") for _ in range(5)]
    for t_ in Psb:
        k.memset(t_[:, 0:1], 0.0)
    PM = [T([128, TB], "PM") for _ in range(5)]
    dtmp = T([128, TB], "dtmp")
    Xc = [T([128, TB + 3], "Xc") for _ in range(3)]
    for t_ in Xc:
        k.memset(t_[:, 0:3], 0.0)
    cacc = T([128, TB], "cacc")
    XBC = [T([128, TB], "XBC") for _ in range(3)]
    th = T([128, TB], "th"); nlw = T([128, TB], "nlw"); asig = T([128, TB], "asig"); sgx = T([128, TB], "sgx")
    sgxd = T([128, 2 * TB], "sgxd")
    kk = T([128, TB], "kk"); kkn = T([128, TB], "kkn"); kmod = T([128, TB], "kmod"); ftmp = T([128, TB], "ftmp")
    cumn = T([128, TB], "cumn"); Ep = T([128, TB], "Ep"); En = T([128, TB], "En"); Eex = T([128, TB], "Eex")
    rt = T([128, TB], "rt"); kt = T([128, TB], "kt"); bt = T([128, TB], "bt"); at = T([128, TB], "at"); prod = T([128, TB], "prod")
    ztm = [T([128, 128], "ztm") for _ in range(4)]
    dtv = [T([128, 2], "dtv") for _ in range(4)]

    def blk(out, src_v, eng="pool"):
        k.tt(r3(out[:]), b3(src_v), r3(bmask[:]), ALU.mult, eng=eng)

    for bi in range(NB):
        t0 = bi * TB
        pss = nps()
        for kc in range(8):
            rr = t0 // Tc; i0 = (t0 % Tc) // XB
            for pi in range(TB // XB):
                gr = (i0 + pi) * 4096 + rr * 1024 + kc * 128
                k.dma(xs[kc][:, pi * XB:(pi + 1) * XB], x1G[gr:gr + 128, :], q="sp", lane="xld")
            k.act(sqb[kc % 2][:], xs[kc][:], AF.Square)
            k.mm(pss[:], onesb[:], sqb[kc % 2][:], start=(kc == 0), stop=(kc == 7))
        rsqrt_small(k, rstd[:], pss[:], 1.0 / D, EPS)
        for kc in range(8):
            k.tt(hb[kc][:], xs[kc][:], rstd[:], ALU.mult, eng=("pool" if kc % 2 else "dve"))
        for ti in range(8):
            p = nps()
            for kc in range(8):
                k.mm(p[:], wfm[kc][:, ti * 128:(ti + 1) * 128], hb[kc][:], start=(kc == 0), stop=(kc == 7))
            if ti < 5:
                P = Psb[ti]
                k.copy(P[:, 1:TB + 1], p[:], eng="act")
                k.tt(dtmp[:], P[:, 0:TB], P[:, 1:TB + 1], ALU.subtract)
                k.stt(PM[ti][:], dtmp[:], rwp[:, ti:ti + 1], P[:, 1:TB + 1], ALU.mult, ALU.add)
                k.copy(P[:, 0:1], P[:, TB:TB + 1], eng="pool")
            else:
                mi = ti - 5
                X = Xc[mi]
                k.copy(X[:, 3:TB + 3], p[:], eng="act")
                k.act(cacc[:], X[:, 0:TB], AF.Identity, scale=mcw[:, mi * 4:mi * 4 + 1])
                for j in range(1, 4):
                    k.stt(cacc[:], X[:, j:TB + j], mcw[:, mi * 4 + j:mi * 4 + j + 1], cacc[:], ALU.mult, ALU.add)
                k.ts(cacc[:], cacc[:], mcb[:, mi:mi + 1], ALU.add)
                k.act(XBC[mi][:], cacc[:], AF.Exp, scale=-1.0)
                recip_1p(k, XBC[mi][:])
                k.tt(XBC[mi][:], XBC[mi][:], cacc[:], ALU.mult, eng="pool")
                k.copy(X[:, 0:3], X[:, TB:TB + 3], eng="pool")
        for s in range(4):
            pz = nps()
            for kc in range(8):
                k.mm(pz[:, 0:130], hb[kc][:, s * 128:(s + 1) * 128], wtm[kc][:], start=(kc == 0), stop=(kc == 7))
            k.act(ztm[s][:], pz[:, 0:128], AF.Exp, scale=-1.0)
            recip_1p(k, ztm[s][:])
            k.tt(ztm[s][:], ztm[s][:], pz[:, 0:128], ALU.mult)
            k.tt(dtv[s][:], pz[:, 128:130], mh[:, 0:2], ALU.add)
            k.act(dtv[s][:], dtv[s][:], AF.Exp)
            k.act(dtv[s][:], dtv[s][:], AF.Ln, bias=1.0)
        recA = k.record(); psel[0] = "A"
        if 'rwkv' not in skip:
            k.act(th[0:64, :], PM[4][0:64, :], AF.Exp, scale=-2.0)
            recip_1p(k, th[0:64, :])
            k.ts(th[0:64, :], th[0:64, :], 2.0, ALU.mult, -1.0, ALU.add)
            k.copy(th[64:128, :], PM[4][64:128, :], eng="act")
            pw = nps(); k.mm(pw[:], w2p[:], th[:])
            k.act(nlw[:], pw[:], AF.Exp, scale=-1.0, bias=nrwp[:, 5:6])
            recip_1p(k, nlw[:])
            k.ts(nlw[:], nlw[:], EM05, ALU.mult)
            pa = nps(); k.mm(pa[:], a2p[:], th[:])
            k.act(asig[:], pa[:], AF.Exp, scale=-1.0, bias=nrwp[:, 6:7])
            recip_1p(k, asig[:])
            k.act(sgx[:], PM[3][:], AF.Exp, scale=-1.0)
            recip_1p(k, sgx[:])
            k.copy(V(sgxd, sgxd[:].ap.rearrange("p (c a t) -> p c a t", a=2, t=CW)),
                   V(sgx, sgx[:].ap.rearrange("p (c t) -> p c t", t=CW).unsqueeze(2).to_broadcast([128, TB // CW, 2, CW])), eng="pool")
            k.ts(kk[:], PM[1][:], rwp[:, 7:8], ALU.mult)
            k.tt(ftmp[:], kk[:], kk[:], ALU.mult, eng="pool")
            pn = nps(); k.mm(pn[:], bmask[:], ftmp[:])
            k.ts(ftmp[:], pn[:], 1e-24, ALU.max)
            k.act(ftmp[:], ftmp[:], AF.Ln)
            k.act(ftmp[:], ftmp[:], AF.Exp, scale=-0.5)
            k.tt(kkn[:], kk[:], ftmp[:], ALU.mult)
            k.ts(ftmp[:], asig[:], rwp[:, 8:9], ALU.mult, omka[:, 0:1], ALU.add)
            k.tt(kmod[:], PM[1][:], ftmp[:], ALU.mult)
            k.scan(cumn[:], cmask[:], nlw[:], 0.0)
            k.act(Ep[:], cumn[:], AF.Exp, scale=-1.0)
            k.act(En[:], cumn[:], AF.Exp)
            k.tt(Eex[:], nlw[:], cumn[:], ALU.subtract, eng="pool")
            k.act(Eex[:], Eex[:], AF.Exp)
            k.tt(rt[:], PM[0][:], Ep[:], ALU.mult)
            k.tt(kt[:], kmod[:], En[:], ALU.mult, eng="pool")
            k.tt(bt[:], kkn[:], asig[:], ALU.mult, eng="pool")
            k.tt(bt[:], bt[:], En[:], ALU.mult, eng="pool")
            k.stt(at[:], kkn[:], -1.0, Eex[:], ALU.mult, ALU.mult)
            k.stt(prod[:], PM[0][:], rwp[:, 9:10], kmod[:], ALU.mult, ALU.mult)
            GR = 4
            if bi == 0:
                CB = []
                for _g in range(GR):
                    d = {n: T([128, 128], n) for n in ("rB", "kB", "bB", "aB", "bH", "kH", "vB", "pB", "AakT", "ArbT", "ArkT", "BH", "KH", "yf")}
                    d.update({n: T([128, 128], n, BF16) for n in ("M", "N", "X", "XT", "X2", "XT2", "R", "R2")})
                    d["Xpkb"] = T([128, 64], "Xpkb", BF16)
                    d.update({n: T([128, 64], n) for n in ("Vpk", "Xpk", "Upk", "yc", "ysq", "yn", "yo")})
                    d.update({n: T([128, 1], n) for n in ("sum", "ssq", "rk")})
                    CB.append(d)
            for g0 in range(0, TB // CW, GR):
                grp = [(g0 + u, CB[u]) for u in range(GR)]
                for c, b_ in grp:
                    sl = slice(c * CW, (c + 1) * CW)
                    gC = Ep[:, c * CW + CW - 1:c * CW + CW]
                    blk(b_["rB"], rt[:, sl]); blk(b_["kB"], kt[:, sl]); blk(b_["bB"], bt[:, sl], eng="dve"); blk(b_["aB"], at[:, sl], eng="dve")
                    blk(b_["vB"], PM[2][:, sl]); blk(b_["pB"], prod[:, sl])
                    k.stt(r3(b_["bH"][:]), b3(bt[:, sl]), gC, r3(bmask[:]), ALU.mult, ALU.mult)
                    k.stt(r3(b_["kH"][:]), b3(kt[:, sl]), gC, r3(bmask[:]), ALU.mult, ALU.mult)
                for (la, ra, dst, msk) in (("bB", "aB", "M", mS), ("aB", "bB", "N", mSl), ("kB", "aB", "AakT", mS),
                                           ("bB", "rB", "ArbT", mI), ("kB", "rB", "ArkT", mI)):
                    for c, b_ in grp:
                        p = nps(); k.mm(p[:, 0:128], b_[la][:], b_[ra][:]); k.tt(b_[dst][:], p[:, 0:128], msk[:], ALU.mult)
                cur = {}
                for c, b_ in grp:
                    k.tt(b_["R"][:], b_["M"][:], identb[:], ALU.add, eng="pool")
                    b_["Rcur"] = b_["R"]
                    cur[c] = (b_["M"], b_["N"])
                for lev in range(1, 6):
                    for c, b_ in grp:
                        Xc_, XTc = cur[c]
                        Xn, XTn = ((b_["X"], b_["XT"]), (b_["X2"], b_["XT2"]))[lev % 2]
                        p1 = nps(); k.mm(p1[:, 0:128], Xc_[:], XTc[:]); k.copy(XTn[:], p1[:, 0:128], eng="act")
                        if lev < 5:
                            p2 = nps(); k.mm(p2[:, 0:128], XTc[:], Xc_[:]); k.copy(Xn[:], p2[:, 0:128], eng="act")
                        cur[c] = (Xn, XTn)
                    for c, b_ in grp:
                        Rc = b_["Rcur"]; Rn = b_["R2"] if Rc is b_["R"] else b_["R"]
                        p3 = nps()
                        k.mm(p3[:, 0:128], identb[:], Rc[:], start=True, stop=False)
                        k.mm(p3[:, 0:128], cur[c][1][:], Rc[:], start=False, stop=True)
                        k.copy(Rn[:], p3[:, 0:128], eng="act")
                        b_["Rcur"] = Rn
                for c, b_ in grp:
                    p = nps(); k.mm(p[:, 0:64], b_["vB"][:], E[:]); k.copy(b_["Vpk"][:], p[:, 0:64], eng="act")
                for c, b_ in grp:
                    p = nps(); k.transpose(p[:, 0:128], b_["bH"][:], ident[:]); k.copy(b_["BH"][:], p[:, 0:128], eng="act")
                for c, b_ in grp:
                    p = nps(); k.transpose(p[:, 0:128], b_["kH"][:], ident[:]); k.copy(b_["KH"][:], p[:, 0:128], eng="act")
                pys = {}
                for c, b_ in grp:
                    gC = Ep[:, c * CW + CW - 1:c * CW + CW]
                    p = nps()
                    k.mm(p[:, 0:64], b_["aB"][:], Hpk[:], start=True, stop=False)
                    k.mm(p[:, 0:64], b_["AakT"][:], b_["Vpk"][:], start=False, stop=True)
                    k.copy(b_["Xpkb"][:], p[:, 0:64])
                    p = nps(); k.mm(p[:, 0:64], b_["Rcur"][:], b_["Xpkb"][:]); k.copy(b_["Upk"][:], p[:, 0:64])
                    py = nps()
                    k.mm(py[:, 0:64], b_["rB"][:], Hpk[:], start=True, stop=False)
                    k.mm(py[:, 0:64], b_["ArbT"][:], b_["Upk"][:], start=False, stop=False)
                    k.mm(py[:, 0:64], b_["ArkT"][:], b_["Vpk"][:], start=False, stop=True)
                    ph = nps()
                    k.mm(ph[:, 0:64], b_["BH"][:], b_["Upk"][:], start=True, stop=False)
                    k.mm(ph[:, 0:64], b_["KH"][:], b_["Vpk"][:], start=False, stop=True)
                    k.stt(Hpk[:], Hpk[:], gC, ph[:, 0:64], ALU.mult, ALU.add)
                    k.act(b_["yc"][:], py[:, 0:64], AF.Identity, accum=b_["sum"][:])
                for c, b_ in grp:
                    k.ts(b_["sum"][:], b_["sum"][:], -1.0 / 64.0, ALU.mult)
                for c, b_ in grp:
                    k.act(b_["yc"][:], b_["yc"][:], AF.Identity, bias=b_["sum"][:, 0:1])
                for c, b_ in grp:
                    k.act(b_["ysq"][:], b_["yc"][:], AF.Square, accum=b_["ssq"][:])
                for c, b_ in grp:
                    k.ts(b_["ssq"][:], b_["ssq"][:], 1.0 / 64.0, ALU.mult, GN_EPS, ALU.add)
                for c, b_ in grp:
                    k.act(b_["ssq"][:], b_["ssq"][:], AF.Ln)
                for c, b_ in grp:
                    k.act(b_["ssq"][:], b_["ssq"][:], AF.Exp, scale=-0.5)
                for c, b_ in grp:
                    k.stt(b_["yn"][:], b_["yc"][:], b_["ssq"][:, 0:1], lnw[:], ALU.mult, ALU.mult)
                for c, b_ in grp:
                    k.tt(b_["yn"][:], b_["yn"][:], lnb[:], ALU.add, eng="pool")
                for c, b_ in grp:
                    p = nps(); k.mm(p[:, 0:2], b_["pB"][:], ones[:, 0:2]); k.copy(b_["rk"][:], p[:, 0:1], eng="act")
                for c, b_ in grp:
                    k.stt(b_["yn"][:], b_["Vpk"][:], b_["rk"][:, 0:1], b_["yn"][:], ALU.mult, ALU.add)
                for c, b_ in grp:
                    pg = nps(); k.mm(pg[:, 0:128], sgxd[:, c * 128:(c + 1) * 128], g2c[:])
                    k.tt(r3(b_["yf"][:]), b3(b_["yn"][:]), r3(pg[:, 0:128]), ALU.mult)
                for c, b_ in grp:
                    k.tt(r3(b_["yf"][:]), r3(b_["yf"][:]), r3(bmask[:]), ALU.mult, eng="pool")
                for c, b_ in grp:
                    k.tt(b_["yo"][:], b_["yf"][:, 0:64], b_["yf"][:, 64:128], ALU.add, eng="pool")
                for c, b_ in grp:
                    pT = nps(); k.transpose(pT[0:64, 0:128], b_["yo"][:], ident[:])
                    k.copy(V(ycT[bi % 2], ycT[bi % 2][:].ap.rearrange("p (h t) -> p h t", h=2)[:, :, c * CW:(c + 1) * CW]),
                           V(pT, pT[0:64, 0:128].ap.rearrange("p (h t) -> p h t", h=2)), eng="act")
            for hh in range(2):
                pr0 = (t0 // PT) * 256; pc0 = t0 % PT
                k.dma(P_o[pr0 + hh * 64:pr0 + (hh + 1) * 64, pc0:pc0 + TB], ycT[bi % 2][:, hh * TB:(hh + 1) * TB], q="pool", lane="st")
        recB = k.record(); psel[0] = "B"
        for c in range(4):
            sl = slice(c * 128, (c + 1) * 128)
            if bi == 0 and c == 0:
                mb = {n: T([128, 128], n) for n in ("xtm", "xdt", "xdd", "Btm", "abc0", "abc1", "df", "GTm", "WT0", "WT1", "ydg", "yy")}
                mb.update({n: T([128, 2], n) for n in ("a", "acs", "nacs", "tot", "eacs", "dec", "etot")})
            a = mb["a"]
            k.tt(a[:], dtv[c][:], negA[:], ALU.mult)
            pc = nps()
            k.mm(pc[:, 0:2], maskT[:], a[:])
            k.mm(pc[:, 2:4], ones[:], a[:])
            k.copy(mb["acs"][:], pc[:, 0:2], eng="act")
            k.copy(mb["tot"][:], pc[:, 2:4], eng="act")
            k.ts(mb["nacs"][:], mb["acs"][:], -1.0, ALU.mult)
            k.act(mb["eacs"][:], mb["acs"][:], AF.Exp)
            k.act(mb["etot"][:], mb["tot"][:], AF.Exp)
            k.tt(mb["dec"][:], mb["tot"][:], mb["acs"][:], ALU.subtract)
            k.act(mb["dec"][:], mb["dec"][:], AF.Exp)
            p = nps(); k.transpose(p[:, 0:128], XBC[0][:, sl], ident[:]); k.copy(mb["xtm"][:], p[:, 0:128], eng="act")
            p = nps(); k.transpose(p[:, 0:128], XBC[1][:, sl], ident[:]); k.copy(mb["Btm"][:], p[:, 0:128], eng="act")
            for h in range(2):
                hs = slice(h * 64, (h + 1) * 64)
                k.ts(mb["xdt"][:, hs], mb["xtm"][:, hs], dtv[c][:, h:h + 1], ALU.mult)
                k.ts(mb["xdd"][:, hs], mb["xdt"][:, hs], mb["dec"][:, h:h + 1], ALU.mult)
            pg = nps(); k.mm(pg[:, 0:128], XBC[1][:, sl], XBC[2][:, sl]); k.tt(mb["GTm"][:], pg[:, 0:128], maskT[:], ALU.mult)
            pyd = nps()
            for h in range(2):
                hs = slice(h * 64, (h + 1) * 64)
                abc = mb["abc%d" % h]
                k.ts(abc[:], ones[:], a[:, h:h + 1], ALU.mult)
                pr = nps(); k.mm(pr[:, 0:128], abc[:], maskT[:])
                k.ts(mb["df"][:], pr[:, 0:128], mb["nacs"][:, h:h + 1], ALU.add, 0.0, ALU.min)
                k.act(mb["df"][:], mb["df"][:], AF.Exp)
                WT = mb["WT%d" % h]
                k.tt(WT[:], mb["df"][:], mb["GTm"][:], ALU.mult)
                k.mm(pyd[:, hs], WT[:], mb["xdt"][:, hs])
            k.copy(mb["ydg"][:], pyd[:, 0:128], eng="act")
            po = nps(); k.mm(po[:, 0:128], XBC[2][:, sl], HT[:])
            for h in range(2):
                hs = slice(h * 64, (h + 1) * 64)
                k.stt(mb["yy"][:, hs], po[:, hs], mb["eacs"][:, h:h + 1], mb["ydg"][:, hs], ALU.mult, ALU.add)
                k.stt(mb["yy"][:, hs], mb["xtm"][:, hs], mh[:, 4 + h:5 + h], mb["yy"][:, hs], ALU.mult, ALU.add)
            k.tt(mb["yy"][:], mb["yy"][:], ztm[c][:], ALU.mult, eng="pool")
            pT = nps(); k.transpose(pT[:, 0:128], mb["yy"][:], ident[:])
            k.copy(ydT[bi % 2][:, sl], pT[:, 0:128], eng="act")
            if c == 3:
                pr0 = (t0 // PT) * 256; pc0 = t0 % PT
                k.dma(P_o[pr0 + 128:pr0 + 256, pc0:pc0 + TB], ydT[bi % 2][:], q="pool", lane="st")
            pst = nps(); k.mm(pst[:, 0:128], mb["Btm"][:], mb["xdd"][:])
            for h in range(2):
                hs = slice(h * 64, (h + 1) * 64)
                k.stt(HT[:, hs], HT[:, hs], mb["etot"][:, h:h + 1], pst[:, hs], ALU.mult, ALU.add)
        k.stop_record(); psel[0] = None
        k.emit_merged(recA, recB)
        io["hook"](bi)


D = 1024
DFF = 2816
NJ = DFF // 128
EPS = 1e-6


def load_cast_weight(k, dst_chunks, src, rows_kc, ncols, scale_cols=None, piece=704, engs=("dve",), stage=None, cnt=[0]):
    for kc in range(rows_kc):
        for c0 in range(0, ncols, piece):
            n = min(piece, ncols - c0)
            st = stage[cnt[0] % len(stage)]
            k.dma(st[:, 0:n], src[kc * 128:(kc + 1) * 128, c0:c0 + n], q="sp", lane="wld")
            eng = engs[cnt[0] % len(engs)]
            cnt[0] += 1
            if scale_cols is not None:
                if eng == "act":
                    k.op("act", lambda e: e.mul(dst_chunks[kc][:, c0:c0 + n].ap, st[:, 0:n].ap, scale_cols[:, kc:kc + 1].ap),
                         reads=[st, scale_cols], writes=[dst_chunks[kc]])
                else:
                    k.ts(dst_chunks[kc][:, c0:c0 + n], st[:, 0:n], scale_cols[:, kc:kc + 1], ALU.mult, eng=eng)
            else:
                k.copy(dst_chunks[kc][:, c0:c0 + n], st[:, 0:n], eng=eng)


def emit_B(k, layer, Tc, final, io, TB=256):
    H = 4 if layer == 0 else 2
    w_out = io["w_out"]; w_up = io["w_up"]; w_dn = io["w_dn"]; cw_d = io["cw"]; cb_d = io["cb"]; nf_d = io["nffn"]
    if final:
        nfin_d = io["nfin"]
    yloc = io["yloc"]
    G = io["G"]; PT = io["PT"]; NPc = Tc // PT; W = 4 + Tc
    XB = TB
    if layer == 0:
        wglu_d = io["wglu"]; bglu_d = io["bglu"]
        parts = [(0, 192, 0, 192), (192, 64, 768, 64)]
    else:
        mbn_d = io["mbn"]
        parts = [(0, 128, 0, 128), (128, 128, 512, 128)]
    dq = "act"
    for i in range(-1, NPc):
        for (sr, nr, dr, dstride) in parts:
            if i < 0:
                soff = 0 * 1024 * PT + sr * PT + (PT - H); ncol = H; dcol = 0
            else:
                soff = (1 + i) * 1024 * PT + sr * PT; ncol = PT; dcol = H + i * PT
            dst_ap = bass.AP(yloc.h.tensor, dr * W + dcol, [[dstride * W, 4], [W, nr], [1, ncol]])
            k.dyn_dma3(dst_ap, yloc, G, NPc * 1024 * PT, soff, [[256 * PT, 4], [PT, nr], [1, ncol]], ename=dq, lane="yloc")

    cw = k.sb([128, NJ * 3]); k.dma(cw[:], cw_d[:], lane="wld")
    cb = k.sb([128, NJ]); k.dma(cb[:], cb_d[:], lane="wld")
    nf = k.sb([128, 8]); k.dma(nf[:], nf_d[:], lane="wld")
    if final:
        nfin = k.sb([128, 8]); k.dma(nfin[:], nfin_d[:], lane="wld")
    ones = k.sb([128, 128]); k.memset(ones[:], 1.0)
    if layer == 0:
        bglu = k.sb([128, 2]); k.dma(bglu[:], bglu_d[:], lane="wld")
        hbglu = k.sb([128, 2]); k.ts(hbglu[:], bglu[:], 0.5, ALU.mult)
    else:
        mbn = k.sb([128, 4]); k.dma(mbn[:], mbn_d[:], lane="wld")

    stage = [k.sb([128, 704], F32, "stage") for _ in range(3)]
    wo = [k.sb([128, D], BF16, "wo") for _ in range(8)]
    wu = [k.sb([128, 2 * DFF], BF16, "wu") for _ in range(8)]
    wd = [k.sb([128, D], BF16, "wd") for _ in range(NJ)]
    load_cast_weight(k, wo, w_out, 8, D, stage=stage, piece=512)
    if layer == 0:
        wg = [k.sb([128, 256], BF16, "wg") for _ in range(2)]
        load_cast_weight(k, wg, wglu_d, 2, 256, stage=stage, piece=256)
    load_cast_weight(k, wu, w_up, 8, 2 * DFF, scale_cols=nf, stage=stage)
    load_cast_weight(k, wd, w_dn, NJ, D, stage=stage, piece=512)

    xs = [[k.sb([128, TB], F32, "xs") for _ in range(8)] for _ in range(1)]
    ys = [k.sb([128, TB], F32, "ys") for _ in range(3)]
    yb = [k.sb([128, TB], BF16, "yb") for _ in range(8)]
    xnb = [k.sb([128, TB], BF16, "xnb") for _ in range(8)]
    actb = [k.sb([128, TB], BF16, "actb") for _ in range(NJ)]
    G = [k.sb([128, TB + 2], F32, "G") for _ in range(2)]
    cv = [k.sb([128, TB], F32, "cv") for _ in range(2)]
    sg = [k.sb([128, TB], F32, "sg") for _ in range(2)]
    carry = [k.sb([128, 2], F32, "carry") for _ in range(NJ)]
    sq = [k.sb([128, TB], F32, "sq") for _ in range(2)]
    rstd = k.sb([128, TB], F32, "rstd")
    zs = [k.sb([128, TB], F32, "zs") for _ in range(4)]
    PS = [k.ps([128, 512], F32, "ps") for _ in range(8)]
    psi = [0]

    def nps():
        p = PS[psi[0] % 8]
        psi[0] += 1
        return p

    def rstd_from_sumsq(ps, n, inv_n, eps):
        k.ts(rstd[:, 0:n], ps[:, 0:n], inv_n, ALU.mult, eps, ALU.add)
        k.act(rstd[:, 0:n], rstd[:, 0:n], AF.Sqrt)
        k.op("dve", lambda e: e.reciprocal(rstd[:, 0:n].ap, rstd[:, 0:n].ap), reads=[rstd], writes=[rstd])

    if layer == 0:
        blocks = [(0, 2, True), (2, 2, False)] + [(4 + i * TB, TB, False) for i in range(Tc // TB)]
    else:
        blocks = [(0, 2, True)] + [(2 + i * TB, TB, False) for i in range(Tc // TB)]
    for bi, (c0, n, halo) in enumerate(blocks):
        X = xs[0]
        for dc in range(8):
            if layer == 0:
                xsrc = io["xTb"][dc * 128:(dc + 1) * 128, c0:c0 + n]
            elif c0 < 2:
                xsrc = io["x1halo"][dc * 128:(dc + 1) * 128, c0:c0 + n]
            else:
                pi_ = (c0 - 2) // XB
                xsrc = io["x1p"][pi_ * 1024 + dc * 128:pi_ * 1024 + (dc + 1) * 128, :]
            k.dma(X[dc][:, 0:n], xsrc, q="sp", lane="xld")
        for dc in range(8):
            raw = (layer == 0 and dc >= 6) or (layer == 1 and dc >= 4)
            if raw:
                st = zs[dc - 4] if layer == 1 else zs[dc - 6]
            else:
                st = ys[dc % 3]
            k.dma(st[:, 0:n], yloc[dc * 128:(dc + 1) * 128, c0:c0 + n], q="act", lane="yld")
            if not raw:
                k.copy(yb[dc][:, 0:n], st[:, 0:n], eng=("act" if dc % 2 else "dve"))
        if layer == 0:
            zb = [xnb[0], xnb[1]]
            for i in range(2):
                k.copy(zb[i][:, 0:n], zs[i][:, 0:n], eng="pool")
            for co in range(2):
                p = nps()
                for ci in range(2):
                    k.mm(p[:, 0:n], wg[ci][:, co * 128:(co + 1) * 128], zb[ci][:, 0:n], start=(ci == 0), stop=(ci == 1))
                k.act(sg[0][:, 0:n], p[:, 0:n], AF.Tanh, scale=0.5, bias=hbglu[:, co:co + 1])
                k.ts(sg[0][:, 0:n], sg[0][:, 0:n], 0.5, ALU.mult, 0.5, ALU.add)
                k.tt(yb[6 + co][:, 0:n], zs[co][:, 0:n], sg[0][:, 0:n], ALU.mult)
        else:
            for g in range(2):
                p = nps()
                for i in range(2):
                    k.act(sq[i][:, 0:n], zs[2 * g + i][:, 0:n], AF.Square)
                    k.mm(p[:, 0:n], ones[:], sq[i][:, 0:n], start=(i == 0), stop=(i == 1))
                rstd_from_sumsq(p, n, 1.0 / 256.0, EPS)
                for i in range(2):
                    k.stt(yb[4 + 2 * g + i][:, 0:n], zs[2 * g + i][:, 0:n], mbn[:, 2 * g + i:2 * g + i + 1], rstd[:, 0:n], ALU.mult, ALU.mult)
        for dc in range(8):
            p = nps()
            for kc in range(8):
                k.mm(p[:, 0:n], wo[kc][:, dc * 128:(dc + 1) * 128], yb[kc][:, 0:n], start=(kc == 0), stop=(kc == 7))
            k.tt(X[dc][:, 0:n], X[dc][:, 0:n], p[:, 0:n], ALU.add)
        p = nps()
        for dc in range(8):
            k.act(sq[dc % 2][:, 0:n], X[dc][:, 0:n], AF.Square)
            k.mm(p[:, 0:n], ones[:], sq[dc % 2][:, 0:n], start=(dc == 0), stop=(dc == 7))
        rstd_from_sumsq(p, n, 1.0 / D, EPS)
        for dc in range(8):
            k.tt(xnb[dc][:, 0:n], X[dc][:, 0:n], rstd[:, 0:n], ALU.mult, eng=("pool" if dc % 2 else "dve"))
        def stage_a(j):
            pg = nps()
            pv = nps()
            for kc in range(8):
                k.mm(pg[:, 0:n], wu[kc][:, j * 128:(j + 1) * 128], xnb[kc][:, 0:n], start=(kc == 0), stop=(kc == 7))
            for kc in range(8):
                k.mm(pv[:, 0:n], wu[kc][:, DFF + j * 128:DFF + (j + 1) * 128], xnb[kc][:, 0:n], start=(kc == 0), stop=(kc == 7))
            Gj = G[j % 2]
            k.copy(Gj[:, 2:2 + n], pg[:, 0:n], eng="act")
            if halo:
                k.copy(carry[j][:, 0:2], Gj[:, 2:4], eng="pool")
            else:
                k.copy(Gj[:, 0:2], carry[j][:, 0:2], eng="pool")
            return (j, pv)

        def stage_b(j):
            Gj = G[j % 2]
            c = cv[j % 2]
            k.act(c[:, 0:n], Gj[:, 0:n], AF.Identity, scale=cw[:, 3 * j:3 * j + 1])
            k.stt(c[:, 0:n], Gj[:, 1:1 + n], cw[:, 3 * j + 1:3 * j + 2], c[:, 0:n], ALU.mult, ALU.add)
            k.stt(c[:, 0:n], Gj[:, 2:2 + n], cw[:, 3 * j + 2:3 * j + 3], c[:, 0:n], ALU.mult, ALU.add)
            k.copy(carry[j][:, 0:2], Gj[:, n:n + 2], eng="pool")

        def stage_c(jj, pvv):
            s_ = sg[jj % 2]
            k.act(s_[:, 0:n], cv[jj % 2][:, 0:n], AF.Silu, bias=cb[:, jj:jj + 1])
            k.tt(actb[jj][:, 0:n], s_[:, 0:n], pvv[:, 0:n], ALU.mult)

        cur_a = stage_a(0)
        prev_a = None
        for j in range(NJ):
            nxt_a = stage_a(j + 1) if j + 1 < NJ else None
            if not halo:
                stage_b(j)
                if prev_a is not None:
                    stage_c(*prev_a)
            prev_a = cur_a
            cur_a = nxt_a
        if not halo:
            stage_c(*prev_a)
        if halo:
            continue
        for dc in range(8):
            p = nps()
            for j in range(NJ):
                k.mm(p[:, 0:n], wd[j][:, dc * 128:(dc + 1) * 128], actb[j][:, 0:n], start=(j == 0), stop=(j == NJ - 1))
            k.tt(X[dc][:, 0:n], X[dc][:, 0:n], p[:, 0:n], ALU.add)
        if final:
            p = nps()
            for dc in range(8):
                k.act(sq[dc % 2][:, 0:n], X[dc][:, 0:n], AF.Square)
                k.mm(p[:, 0:n], ones[:], sq[dc % 2][:, 0:n], start=(dc == 0), stop=(dc == 7))
            rstd_from_sumsq(p, n, 1.0 / D, EPS)
            for dc in range(8):
                k.stt(X[dc][:, 0:n], X[dc][:, 0:n], nfin[:, dc:dc + 1], rstd[:, 0:n], ALU.mult, ALU.mult)
        for dc in range(8):
            if layer == 1:
                dst = io["outT"][dc * 128:(dc + 1) * 128, c0 - 2:c0 - 2 + n]
            elif c0 < 4:
                dst = io["x1halo"][dc * 128:(dc + 1) * 128, 0:2]
            else:
                pi_ = (c0 - 4) // XB
                dst = io["x1p"][pi_ * 1024 + dc * 128:pi_ * 1024 + (dc + 1) * 128, :]
            k.dma(dst, X[dc][:, 0:n], q="pool", lane="st")
        if layer == 0 and c0 >= 4:
            io["hook"]((c0 - 4) // XB)


def colmajor(v, nch):
    return np.ascontiguousarray(np.asarray(v, np.float32).reshape(nch, 128).T)


def colmajor(v, nch):
    return np.ascontiguousarray(np.asarray(v, np.float32).reshape(nch, 128).T)

def prep_A0(inp, xT_b, h):
    W = inp["e_w_in"][0]
    nk = 512; nv = 768
    q = W[:, h * 128:(h + 1) * 128]
    kk = W[:, nk + h * 128: nk + (h + 1) * 128]
    v = W[:, 2 * nk + h * 192: 2 * nk + (h + 1) * 192]
    g = W[:, 2 * nk + nv + h * 192: 2 * nk + nv + (h + 1) * 192]
    alo = W[:, 2 * nk + 2 * nv: 2 * nk + 2 * nv + 16]
    GC = 2 * nk + 2 * nv + 16
    u = W[:, GC + h * 64: GC + (h + 1) * 64]
    m = {"xT": xT_b,
         "w_fm": np.ascontiguousarray(np.concatenate([q, kk, u, alo], 1)),
         "w_tm": np.ascontiguousarray(np.concatenate([v, g], 1)),
         "nmix": colmajor(inp["norm_mix"][0], 8),
         "wa2": np.ascontiguousarray(inp["e_gla_wa2"][0][:, h * 128:(h + 1) * 128]),
         "nba": np.ascontiguousarray(inp["e_gla_ba"][0][h * 128:(h + 1) * 128].reshape(128, 1)),
         "gnorm": np.ascontiguousarray(np.broadcast_to(inp["e_gla_norm"][0][None, :], (128, 192))),
         "s5d": np.ascontiguousarray(inp["e_s5_d"][0][h * 64:(h + 1) * 64].reshape(64, 1))}
    s5col = np.zeros((128, 6), np.float32)
    s5b = np.zeros((128, 256), np.float32)
    s5c = np.zeros((128, 256), np.float32)
    for st in range(2):
        for gi in range(2):
            gl = 2 * st + gi
            g_ = 4 * h + gl
            rows = slice(gi * 64, (gi + 1) * 64)
            s5col[rows, st * 3 + 0] = inp["e_s5_lambda_re"][0][g_]
            s5col[rows, st * 3 + 1] = inp["e_s5_lambda_im"][0][g_]
            s5col[rows, st * 3 + 2] = inp["e_s5_log_dt"][0][g_]
            s5b[rows, st * 128 + gl * 16: st * 128 + gl * 16 + 16] = inp["e_s5_b_re"][0][g_]
            s5b[rows, st * 128 + 64 + gl * 16: st * 128 + 64 + gl * 16 + 16] = inp["e_s5_b_im"][0][g_]
            s5c[rows, st * 128 + gl * 16: st * 128 + gl * 16 + 16] = inp["e_s5_c_re"][0][g_].T
            s5c[rows, st * 128 + 64 + gl * 16: st * 128 + 64 + gl * 16 + 16] = inp["e_s5_c_im"][0][g_].T
    m["s5col"] = s5col; m["s5b"] = s5b; m["s5c"] = s5c
    return m


def prep_A1(inp, xT_b, j):
    W = inp["o_w_in"][0]
    RW = 512
    cs = slice(j * 128, (j + 1) * 128)
    def col(base):
        return W[:, base + j * 128: base + (j + 1) * 128]
    r = col(0); kk = col(RW); v = col(2 * RW)
    o = 3 * RW
    xw = W[:, o:o + 64]; xa = W[:, o + 64:o + 128]; xg = W[:, o + 128:o + 256]
    RC = 3 * RW + 256
    z = W[:, RC + j * 128: RC + (j + 1) * 128]
    xb = RC + 512
    x = W[:, xb + j * 128: xb + (j + 1) * 128]
    g = j // 2
    Bm = W[:, xb + 512 + g * 128: xb + 512 + (g + 1) * 128]
    Cm = W[:, xb + 768 + g * 128: xb + 768 + (g + 1) * 128]
    dt = W[:, xb + 1024 + 2 * j: xb + 1024 + 2 * j + 2]
    mu = inp["o_rw_mu"][0]
    rwp = np.zeros((128, 16), np.float32)
    rwp[:, 0] = mu[0 + j * 128: 0 + (j + 1) * 128]
    rwp[:, 1] = mu[RW + j * 128: RW + (j + 1) * 128]
    rwp[:, 2] = mu[2 * RW + j * 128: 2 * RW + (j + 1) * 128]
    rwp[:, 3] = mu[o + 128:o + 256]
    rwp[:, 4] = mu[o:o + 128]
    rwp[:, 5] = inp["o_rw_w0"][0][cs]
    rwp[:, 6] = inp["o_rw_a0"][0][cs]
    rwp[:, 7] = inp["o_rw_k_k"][0][cs]
    rwp[:, 8] = inp["o_rw_k_a"][0][cs]
    rwp[:, 9] = inp["o_rw_r_k"][0].reshape(-1)[cs]
    w2p = np.zeros((128, 128), np.float32); w2p[0:64] = inp["o_rw_w2"][0][:, cs]
    a2p = np.zeros((128, 128), np.float32); a2p[64:128] = inp["o_rw_a2"][0][:, cs]
    lnw = np.repeat(inp["o_rw_ln_w"][0][cs].reshape(2, 1, 64), 64, axis=1).reshape(128, 64)
    lnb = np.repeat(inp["o_rw_ln_b"][0][cs].reshape(2, 1, 64), 64, axis=1).reshape(128, 64)
    cw = inp["o_mb_conv_w"][0]; cb = inp["o_mb_conv_b"][0]
    mcw = np.zeros((128, 12), np.float32); mcb = np.zeros((128, 3), np.float32)
    for ti, sl in enumerate((slice(j * 128, (j + 1) * 128), slice(512 + g * 128, 512 + (g + 1) * 128), slice(768 + g * 128, 768 + (g + 1) * 128))):
        mcw[:, ti * 4:(ti + 1) * 4] = cw[:, sl].T
        mcb[:, ti] = cb[sl]
    mh = np.zeros((128, 6), np.float32)
    mh[:, 0:2] = inp["o_mb_dt_bias"][0][2 * j:2 * j + 2]
    mh[:, 2:4] = inp["o_mb_a_log"][0][2 * j:2 * j + 2]
    mh[:, 4:6] = inp["o_mb_d"][0][2 * j:2 * j + 2]
    return {"xT": xT_b,
            "w_fm": np.ascontiguousarray(np.concatenate([r, kk, v, xg, xw, xa, x, Bm, Cm], 1)),
            "w_tm": np.ascontiguousarray(np.concatenate([z, dt], 1)),
            "nmix": colmajor(inp["norm_mix"][1], 8), "rwp": rwp, "w2p": w2p, "a2p": a2p,
            "g2c": np.ascontiguousarray(inp["o_rw_g2"][0][:, cs]),
            "lnw": np.ascontiguousarray(lnw), "lnb": np.ascontiguousarray(lnb), "mcw": mcw, "mcb": mcb, "mh": mh}

NCORE = 8
GROUPS = [[0, 1, 2, 3], [4, 5, 6, 7]]

A0_IN = {"w_fm": [1024, 336], "w_tm": [1024, 384], "nmix": [128, 8], "wa2": [16, 128], "nba": [128, 1], "gnorm": [128, 192],
         "s5col": [128, 6], "s5b": [128, 256], "s5c": [128, 256], "s5d": [64, 1]}
A1_IN = {"w_fm": [1024, 1024], "w_tm": [1024, 130], "nmix": [128, 8], "rwp": [128, 16], "w2p": [128, 128], "a2p": [128, 128],
         "g2c": [128, 128], "lnw": [128, 64], "lnb": [128, 64], "mcw": [128, 12], "mcb": [128, 3], "mh": [128, 6]}
B_IN = {"w_out": [1024, 1024], "w_up": [1024, 5632], "w_dn": [2816, 1024], "cw": [128, 66], "cb": [128, 22], "nffn": [128, 8]}


def build_fused(L, Bsz=2):
    nc = bass.Bass("TRN2", target_bir_lowering=False)
    k = KB(nc)
    Tc = (Bsz * L) // NCORE
    PT = min(1024, Tc)
    NP = L // PT
    XB = 256
    NXB = Tc // XB

    def ext(prefix, spec):
        return {n: k.dram(prefix + n, shp, F32, kind="ExternalInput") for n, shp in spec.items()}
    a0 = ext("a0_", A0_IN)
    a0["xT"] = k.dram("a0_xT", [1024, L], F32, kind="ExternalInput")
    P0 = k.dram("P0", [NP * 256, PT]); G0 = k.dram("G0", [(NP + 1) * 1024, PT])
    P1 = k.dram("P1", [NP * 256, PT]); G1 = k.dram("G1", [(NP + 1) * 1024, PT])
    x1p = k.dram("x1p", [NXB * 1024, XB]); x1G = k.dram("x1G", [NXB * 4096, XB])
    x1halo = k.dram("x1halo", [1024, 2]); yloc = k.dram("yloc", [1024, 4 + Tc])
    b0 = ext("b0_", dict(B_IN, wglu=[256, 256], bglu=[128, 2]))
    b0["xTb"] = k.dram("b0_xTb", [1024, 4 + Tc], F32, kind="ExternalInput")
    a1 = ext("a1_", A1_IN)
    b1 = ext("b1_", dict(B_IN, nfin=[128, 8], mbn=[128, 4]))
    b1["outT"] = k.dram("outT", [1024, Tc], F32, kind="ExternalOutput")

    def piece_hook(P, G):
        def hook(bi):
            t1 = (bi + 1) * 512
            if t1 % PT == 0:
                p = t1 // PT - 1
                k.wait_ring("pool", "st")
                k.allgather(V(G, G.h[(p + 1) * 1024:(p + 2) * 1024, :]), V(P, P.h[p * 256:(p + 1) * 256, :]), GROUPS)
        return hook

    def zero_pad_piece(G):
        zt = k.sb([128, 4], F32, "zt"); k.memset(zt[:], 0.0)
        for r0 in range(0, 1024, 128):
            k.dma(G[r0:r0 + 128, PT - 4:PT], zt[:], q="pool", lane="st")

    zero_pad_piece(G0)
    a0.update(P=P0, PT=PT, hook=piece_hook(P0, G0))
    emit_A0(k, L, a0)
    k.end_phase()
    def x1_hook(i):
        k.wait_ring("pool", "st")
        k.allgather(V(x1G, x1G.h[i * 4096:(i + 1) * 4096, :]), V(x1p, x1p.h[i * 1024:(i + 1) * 1024, :]), GROUPS)
    b0.update(G=G0, PT=PT, yloc=yloc, x1halo=x1halo, x1p=x1p, hook=x1_hook)
    emit_B(k, 0, Tc, False, b0)
    k.end_phase()
    zero_pad_piece(G1)
    a1.update(x1G=x1G, P=P1, PT=PT, XB=XB, hook=piece_hook(P1, G1))
    emit_A1(k, L, a1, Tc)
    k.end_phase()
    b1.update(G=G1, PT=PT, yloc=yloc, x1halo=x1halo, x1p=x1p)
    emit_B(k, 1, Tc, True, b1)
    k.end_phase()
    print("fused program instructions:", k.ninst)
    return nc


def _b_params(inp, layer, final):
    i = layer
    m = {"w_out": inp["e_w_out"][0] if layer == 0 else inp["o_w_out"][0],
         "w_up": inp["ffn_w_up"][i], "w_dn": inp["ffn_w_down"][i],
         "cw": np.ascontiguousarray(inp["ffn_conv_w"][i].reshape(3, 22, 128).transpose(2, 1, 0).reshape(128, 66)),
         "cb": colmajor(inp["ffn_conv_b"][i], 22), "nffn": colmajor(inp["norm_ffn"][i], 8)}
    if final:
        m["nfin"] = colmajor(inp["norm_final"], 8)
    if layer == 0:
        m["wglu"] = inp["e_s5_w_glu"][0]
        m["bglu"] = colmajor(inp["e_s5_b_glu"][0], 2)
    else:
        m["mbn"] = colmajor(inp["o_mb_norm"][0], 4)
    return m


_NC_CACHE = {}


def kernel(**inputs):
    inp = {k_: np.ascontiguousarray(np.asarray(v, dtype=np.float32)) for k_, v in inputs.items()}
    x = inp["x"]
    Bsz, L = x.shape[0], x.shape[1]
    Tc = (Bsz * L) // NCORE
    per_b = NCORE // Bsz
    if L not in _NC_CACHE:
        _NC_CACHE[L] = build_fused(L, Bsz)
    nc = _NC_CACHE[L]
    xT = [np.ascontiguousarray(x[b].T) for b in range(Bsz)]
    pb0 = _b_params(inp, 0, False)
    pb1 = _b_params(inp, 1, True)
    maps = []
    for c in range(NCORE):
        b, q = c // per_b, c % per_b
        m = {}
        a0 = prep_A0(inp, xT[b], q)
        m.update({"a0_" + n: v for n, v in a0.items()})
        a1 = prep_A1(inp, None, q)
        m.update({"a1_" + n: v for n, v in a1.items() if n != "xT"})
        m.update({"b0_" + n: v for n, v in pb0.items()})
        m.update({"b1_" + n: v for n, v in pb1.items()})
        t0 = q * Tc
        if t0 == 0:
            xb = np.concatenate([np.zeros((1024, 4), np.float32), xT[b][:, 0:Tc]], 1)
        else:
            xb = xT[b][:, t0 - 4:t0 + Tc]
        m["b0_xTb"] = np.ascontiguousarray(xb)
        maps.append(m)
    res = run_bass_kernel_spmd(nc, maps, core_ids=list(range(NCORE))).results
    out = np.empty((Bsz, L, 1024), np.float32)
    for c in range(NCORE):
        b, q = c // per_b, c % per_b
        out[b, q * Tc:(q + 1) * Tc, :] = res[c]["outT"].T
    return out
```

```python
import contextlib
import numpy as np
import concourse.bass as bass
import concourse.mybir as mybir
from concourse.bass_utils import run_bass_kernel_spmd

F32 = mybir.dt.float32
BF16 = mybir.dt.bfloat16
ALU = mybir.AluOpType
AF = mybir.ActivationFunctionType
AX = mybir.AxisListType


class Lane:
    def __init__(self, nc, name, inc):
        self.sem = nc.alloc_semaphore(name)
        self.name = name
        self.inc = inc
        self.count = 0


class Buf:
    def __init__(self, h, name, psum=False):
        self.h = h
        self.name = name
        self.psum = psum
        self.last_w = None
        self.readers = {}

    def __getitem__(self, key):
        return V(self, self.h[key])


class V:
    def __init__(self, buf, ap):
        self.buf = buf
        self.ap = ap

    def __getitem__(self, key):
        return V(self.buf, self.ap[key])

    def rearrange(self, s, **kw):
        return V(self.buf, self.ap.rearrange(s, **kw))

    def bcast(self, shape):
        return V(self.buf, self.ap.to_broadcast(shape))

    def bitcast(self, dt):
        return V(self.buf, self.ap.bitcast(dt))


class KB:
    def __init__(self, nc):
        self.nc = nc
        self.eng = {"pe": nc.tensor, "act": nc.scalar, "dve": nc.vector, "pool": nc.gpsimd, "sp": nc.sync}
        self.lanes = {n: Lane(nc, "L" + n, 1) for n in ("pe", "act", "dve", "pool")}
        self.waited = {}
        self.rings = {}
        self.stack = contextlib.ExitStack()
        self.cclane = Lane(nc, "Lcc", 1)
        self.pid = {}
        self.rec = None
        self.nbuf = 0
        self.ninst = 0

    def sb(self, shape, dt=F32, name=None):
        self.nbuf += 1
        name = (name or "t") + "_%d" % self.nbuf
        return Buf(self.stack.enter_context(self.nc.sbuf_tensor(name, list(shape), dt)), name)

    def ps(self, shape, dt=F32, name=None):
        self.nbuf += 1
        name = (name or "p") + "_%d" % self.nbuf
        return Buf(self.stack.enter_context(self.nc.psum_tensor(name, list(shape), dt)), name, psum=True)

    def dram(self, name, shape, dt=F32, kind="Internal"):
        h = self.nc.dram_tensor(name, list(shape), dt, kind=kind)
        return Buf(h.ap(), name)

    RING = 8

    def dma_lane(self, name, ename):
        if name not in self.rings:
            self.rings[name] = [[Lane(self.nc, "D%s_%d" % (name, i), 16) for i in range(self.RING)], 0]
        ring = self.rings[name]
        lane = ring[0][ring[1] % self.RING]
        ring[1] += 1
        if lane.count > 0:
            wk = (ename, lane.name)
            if self.waited.get(wk, 0) < lane.count:
                self.waited[wk] = lane.count
                self.eng[ename].wait_ge(lane.sem, lane.count * lane.inc)
                self.ninst += 1
        return lane

    def _need(self, ename, deps, lane, cnt):
        if cnt <= 0:
            return
        key = lane.name
        if deps.get(key, (None, 0))[1] < cnt:
            deps[key] = (lane, cnt)

    def record(self):
        self.rec = []
        return self.rec

    def stop_record(self):
        self.rec = None

    def emit_merged(self, A, B):
        la, lb = len(A), len(B)
        ia = ib = 0
        while ia < la or ib < lb:
            if ib >= lb or (ia < la and ia * lb <= ib * la):
                self.op(*A[ia]); ia += 1
            else:
                self.op(*B[ib]); ib += 1

    def op(self, ename, fn, reads=(), writes=(), lane=None, pe_acc=False):
        if self.rec is not None:
            self.rec.append((ename, fn, list(reads), list(writes), lane, pe_acc))
            return None
        e = self.eng[ename]
        if lane is None:
            lane = self.lanes[ename]
        elif isinstance(lane, str):
            lane = self.dma_lane(lane, ename)
        deps = {}
        rb = []
        for r in reads:
            b = r.buf if isinstance(r, V) else r
            if b is None:
                continue
            rb.append(b)
            if b.last_w is not None:
                self._need(ename, deps, *b.last_w)
            if b.psum:
                for ln, (l, c) in b.readers.items():
                    if l is not lane:
                        self._need(ename, deps, l, c)
        wb = []
        for w in writes:
            b = w.buf if isinstance(w, V) else w
            wb.append(b)
            if b.last_w is not None:
                if not (pe_acc and b.last_w[0] is lane):
                    self._need(ename, deps, *b.last_w)
            for ln, (l, c) in b.readers.items():
                self._need(ename, deps, l, c)
        for key, (l, c) in deps.items():
            wk = (ename, key)
            if self.waited.get(wk, 0) >= c:
                continue
            self.waited[wk] = c
            e.wait_ge(l.sem, c * l.inc)
            self.ninst += 1
        ins = fn(e)
        lane.count += 1
        ins.then_inc(lane.sem, lane.inc)
        self.ninst += 1
        for b in wb:
            b.last_w = (lane, lane.count)
            b.readers = {}
        for b in rb:
            if b in wb:
                continue
            b.readers[lane.name] = (lane, lane.count)
        return ins

    def dma(self, out, in_, q="sp", lane="ld", **kw):
        return self.op(q, lambda e: e.dma_start(out=out.ap, in_=in_.ap, **kw), reads=[in_], writes=[out], lane=lane)

    def mm(self, out, lhsT, rhs, start=True, stop=True, **kw):
        return self.op("pe", lambda e: e.matmul(out.ap, lhsT.ap, rhs.ap, start=start, stop=stop, **kw),
                       reads=[lhsT, rhs], writes=[out], pe_acc=not start)

    def transpose(self, out, in_, ident):
        return self.op("pe", lambda e: e.transpose(out.ap, in_.ap, ident.ap), reads=[in_, ident], writes=[out])

    def act(self, out, in_, func, bias=None, scale=None, accum=None, eng="act"):
        kw = {}
        reads = [in_]
        if bias is not None:
            if isinstance(bias, V):
                kw["bias"] = bias.ap
                reads.append(bias)
            else:
                kw["bias"] = bias
        if scale is not None:
            if isinstance(scale, V):
                kw["scale"] = scale.ap
                reads.append(scale)
            else:
                kw["scale"] = scale
        writes = [out]
        if accum is not None:
            kw["accum_out"] = accum.ap
            writes.append(accum)
        return self.op(eng, lambda e: e.activation(out.ap, in_.ap, func, **kw), reads=reads, writes=writes)

    def tt(self, out, in0, in1, op, eng="dve"):
        return self.op(eng, lambda e: e.tensor_tensor(out.ap, in0.ap, in1.ap, op), reads=[in0, in1], writes=[out])

    def ts(self, out, in0, s1, op0, s2=None, op1=None, eng="dve", accum=None):
        reads = [in0]
        a1 = s1
        a2 = s2
        if isinstance(s1, V):
            reads.append(s1)
            a1 = s1.ap
        if isinstance(s2, V):
            reads.append(s2)
            a2 = s2.ap
        kw = {}
        writes = [out]
        if accum is not None:
            kw["accum_out"] = accum.ap
            writes.append(accum)
        if op1 is None:
            return self.op(eng, lambda e: e.tensor_scalar(out.ap, in0.ap, a1, None, op0, **kw), reads=reads, writes=writes)
        return self.op(eng, lambda e: e.tensor_scalar(out.ap, in0.ap, a1, a2, op0, op1, **kw), reads=reads, writes=writes)

    def stt(self, out, in0, scalar, in1, op0, op1, eng="dve"):
        reads = [in0, in1]
        a = scalar
        if isinstance(scalar, V):
            reads.append(scalar)
            a = scalar.ap
        return self.op(eng, lambda e: e.scalar_tensor_tensor(out.ap, in0.ap, a, in1.ap, op0, op1), reads=reads, writes=[out])

    def scan(self, out, d0, d1, init, op0=ALU.mult, op1=ALU.add, eng="dve"):
        reads = [d0, d1]
        a = init
        if isinstance(init, V):
            reads.append(init)
            a = init.ap
        return self.op(eng, lambda e: e.tensor_tensor_scan(out.ap, d0.ap, d1.ap, a, op0, op1), reads=reads, writes=[out])

    def copy(self, out, in_, eng="dve"):
        if eng == "act":
            return self.op("act", lambda e: e.copy(out.ap, in_.ap), reads=[in_], writes=[out])
        return self.op(eng, lambda e: e.tensor_copy(out.ap, in_.ap), reads=[in_], writes=[out])

    def memset(self, out, val, eng="pool"):
        return self.op(eng, lambda e: e.memset(out.ap, val), reads=[], writes=[out])

    def all_lanes(self):
        ls = list(self.lanes.values()) + [self.cclane]
        for name, (lanes, _) in self.rings.items():
            ls.extend(lanes)
        return ls

    def end_phase(self):
        for ename, e in self.eng.items():
            for l in self.all_lanes():
                if l.count > 0 and self.waited.get((ename, l.name), 0) < l.count:
                    self.waited[(ename, l.name)] = l.count
                    e.wait_ge(l.sem, l.count * l.inc)
                    self.ninst += 1
        self.stack.close()
        self.stack = contextlib.ExitStack()

    def allgather(self, dst, src, groups):
        return self.op("pool", lambda e: e.collective_compute("AllGather", ALU.bypass, replica_groups=groups,
                                                              ins=[src[:].ap.opt()], outs=[dst[:].ap.opt()]),
                       reads=[src], writes=[dst], lane=self.cclane)

    def dyn_dma(self, out, buf, row0, nrows, col_static, n, qscale, ename="sp", lane="ld"):
        e = self.eng[ename]
        key = (ename, qscale)
        if key not in self.pid:
            qreg = e.to_reg((e.partition_id() % 4) * qscale)
            self.pid[key] = (qreg, e.alloc_register("dynoff_%s_%d" % (ename, qscale)))
        qreg, r = self.pid[key]
        tens = buf.h.tensor
        rowlen = buf.h.shape[1]

        def fn(e_):
            e_.reg_add(r, qreg, row0 * rowlen + col_static)
            return e_.dma_start(out=out.ap, in_=bass.AP(tens, r, [[rowlen, nrows], [1, n]]))
        return self.op(ename, fn, reads=[buf], writes=[out], lane=lane)

    def dyn_dma3(self, out_ap, out_buf, buf, qscale, static_off, pattern, ename="sp", lane="dyn"):
        e = self.eng[ename]
        key = (ename, qscale)
        if key not in self.pid:
            qreg = e.to_reg((e.partition_id() % 4) * qscale)
            self.pid[key] = (qreg, e.alloc_register("dynoff_%s_%d" % (ename, qscale)))
        qreg, r = self.pid[key]
        tens = buf.h.tensor

        def fn(e_):
            e_.reg_add(r, qreg, static_off)
            return e_.dma_start(out=out_ap, in_=bass.AP(tens, r, pattern))
        return self.op(ename, fn, reads=[buf], writes=[out_buf], lane=lane)

    def wait_ring(self, ename, ring):
        if ring not in self.rings:
            return
        e = self.eng[ename]
        for l in self.rings[ring][0]:
            if l.count > 0 and self.waited.get((ename, l.name), 0) < l.count:
                self.waited[(ename, l.name)] = l.count
                e.wait_ge(l.sem, l.count * l.inc)
                self.ninst += 1

    def core_q(self, ename, mod):
        key = (ename, mod)
        if key not in self.pid:
            self.pid[key] = self.eng[ename].partition_id() % mod
        return self.pid[key]

    def finish(self, bufs):
        self.end_phase()

import math

D = 1024
EPS = 1e-6
TWO_PI = 2.0 * math.pi
MAGIC = 12582912.0


def cast_w(k, dst, src, ncols, nmix, stage, cnt=[0]):
    engs = ("act", "dve")
    piece = stage[0].h.shape[1]
    for kc in range(8):
        for c0 in range(0, ncols, piece):
            n = min(piece, ncols - c0)
            st = stage[cnt[0] % len(stage)]
            eng = engs[cnt[0] % len(engs)]
            cnt[0] += 1
            k.dma(st[:, 0:n], src[kc * 128:(kc + 1) * 128, c0:c0 + n], q="sp", lane="wld")
            if eng == "act":
                k.op("act", lambda e: e.mul(dst[kc][:, c0:c0 + n].ap, st[:, 0:n].ap, nmix[:, kc:kc + 1].ap), reads=[st, nmix], writes=[dst[kc]])
            else:
                k.ts(dst[kc][:, c0:c0 + n], st[:, 0:n], nmix[:, kc:kc + 1], ALU.mult, eng=eng)


def make_consts(k):
    ident = k.sb([128, 128], F32, "ident"); k.memset(ident[:], 0.0)
    k.op("pool", lambda e: e.affine_select(out=ident[:].ap, in_=ident[:].ap, compare_op=ALU.not_equal, fill=1.0, base=0,
                                           pattern=[[-1, 128]], channel_multiplier=1), reads=[ident], writes=[ident])
    identb = k.sb([128, 128], BF16, "identb"); k.copy(identb[:], ident[:])
    maskT = k.sb([128, 128], F32, "maskT"); k.memset(maskT[:], 1.0)
    k.op("pool", lambda e: e.affine_select(out=maskT[:].ap, in_=maskT[:].ap, compare_op=ALU.is_ge, fill=0.0, base=0,
                                           pattern=[[1, 128]], channel_multiplier=-1), reads=[maskT], writes=[maskT])
    ones = k.sb([128, 128], F32, "ones"); k.memset(ones[:], 1.0)
    onesb = k.sb([128, 128], BF16, "onesb"); k.memset(onesb[:], 1.0)
    return ident, identb, maskT, ones, onesb


def rsqrt_small(k, out, in_, mul, add):
    k.ts(out, in_, mul, ALU.mult, add, ALU.add)
    k.act(out, out, AF.Ln)
    k.act(out, out, AF.Exp, scale=-0.5)


def recip_1p(k, t):
    k.ts(t, t, 1.0, ALU.add)
    k.op("dve", lambda e: e.reciprocal(t.ap, t.ap), reads=[t], writes=[t])


def emit_A0(k, L, io, TB=512, PAD=4):
    skip = ()
    NB = L // TB
    xT = io["xT"]; w_fm = io["w_fm"]; w_tm = io["w_tm"]; nmix_d = io["nmix"]; wa2_d = io["wa2"]; nba_d = io["nba"]
    gn_d = io["gnorm"]; s5col_d = io["s5col"]; s5b_d = io["s5b"]; s5c_d = io["s5c"]; s5d_d = io["s5d"]
    P_o = io["P"]; PT = io["PT"]

    ident, identb, maskT, ones, onesb = make_consts(k)
    yaTa = [k.sb([128, TB], F32, "yaTa") for _ in range(2)]
    yaTb = [k.sb([64, TB], F32, "yaTb") for _ in range(2)]
    nmix = k.sb([128, 8]); k.dma(nmix[:], nmix_d[:], lane="wld")
    wa2 = k.sb([16, 128]); k.dma(wa2[:], wa2_d[:], lane="wld")
    nba = k.sb([128, 1]); k.dma(nba[:], nba_d[:], lane="wld")
    k.ts(nba[:], nba[:], -1.0, ALU.mult)
    gn = k.sb([128, 192]); k.dma(gn[:], gn_d[:], lane="wld")
    s5col = k.sb([128, 6]); k.dma(s5col[:], s5col_d[:], lane="wld")
    s5b = k.sb([128, 256]); k.dma(s5b[:], s5b_d[:], lane="wld")
    s5c = k.sb([128, 256]); k.dma(s5c[:], s5c_d[:], lane="wld")
    s5d = k.sb([64, 1]); k.dma(s5d[:], s5d_d[:], lane="wld")

    stage = [k.sb([128, 384], F32, "stage") for _ in range(3)]
    wfm = [k.sb([128, 336], BF16, "wfm") for _ in range(8)]
    wtm = [k.sb([128, 384], BF16, "wtm") for _ in range(8)]
    cast_w(k, wfm, w_fm, 336, nmix, stage)
    cast_w(k, wtm, w_tm, 384, nmix, stage)

    PS = [k.ps([128, 512], F32, "ps") for _ in range(7)]
    PSB = k.ps([128, 1024], BF16, "psb")
    psi = [0]

    psel = [None]
    psa = [0]; psb = [0]
    NA = len(PS) - 3

    def nps():
        if psel[0] == "A":
            p = PS[psa[0] % NA]; psa[0] += 1
        elif psel[0] == "B":
            p = PS[NA + psb[0] % 3]; psb[0] += 1
        else:
            p = PS[psi[0] % len(PS)]; psi[0] += 1
        return p

    cosT = []; sinT = []; rho = []; BbT = []; Cbd = []
    sm = k.sb([128, 32], F32, "s5small")
    for st in (range(2) if 's5setup' not in skip else []):
        lre = s5col[:, st * 3 + 0:st * 3 + 1]; lim = s5col[:, st * 3 + 1:st * 3 + 2]; ldt = s5col[:, st * 3 + 2:st * 3 + 3]
        c = lambda i: sm[:, i:i + 1]
        dt, a, ang, mag, nrd, angr, sn, cs_, tmp, lbr, lbi, nr, den, fre, fim, t2 = [c(i) for i in range(16)]
        k.act(dt, ldt, AF.Exp)
        k.tt(a, lre, dt, ALU.mult)
        k.tt(ang, lim, dt, ALU.mult)
        k.act(mag, a, AF.Exp)
        k.ts(nrd, ang, 1.0 / TWO_PI, ALU.mult)
        k.ts(nrd, nrd, MAGIC, ALU.add)
        k.ts(nrd, nrd, -MAGIC, ALU.add)
        k.stt(angr, nrd, -TWO_PI, ang, ALU.mult, ALU.add)
        k.ts(angr, angr, math.pi, ALU.min, -math.pi, ALU.max)
        k.act(sn, angr, AF.Sin)
        k.ts(tmp, angr, -1.0, ALU.mult); k.tt(tmp, tmp, angr, ALU.max)
        k.ts(tmp, tmp, -1.0, ALU.mult, math.pi / 2, ALU.add)
        k.act(cs_, tmp, AF.Sin)
        k.tt(lbr, mag, cs_, ALU.mult)
        k.tt(lbi, mag, sn, ALU.mult)
        k.ts(nr, lbr, -1.0, ALU.add)
        k.tt(den, lre, lre, ALU.mult)
        k.stt(den, lim, lim, den, ALU.mult, ALU.add)
        k.op("dve", lambda e: e.reciprocal(den.ap, den.ap), reads=[den], writes=[den])
        k.tt(fre, nr, lre, ALU.mult)
        k.stt(fre, lbi, lim, fre, ALU.mult, ALU.add)
        k.tt(fre, fre, den, ALU.mult)
        k.tt(fim, lbi, lre, ALU.mult)
        k.tt(t2, nr, lim, ALU.mult)
        k.tt(fim, fim, t2, ALU.subtract)
        k.tt(fim, fim, den, ALU.mult)
        r_ = k.sb([128, TB], F32, "rho"); k.ts(r_[:], ones[:, 0:1].bcast([128, TB]), mag, ALU.mult)
        rho.append(r_)
        ct = k.sb([128, TB], F32, "cosT"); stb = k.sb([128, TB], F32, "sinT")
        k.copy(ct[:, 0:1], cs_); k.copy(stb[:, 0:1], sn)
        n = 1
        tA = k.sb([128, TB // 2], F32, "tA")
        while n < TB:
            cr = ct[:, n - 1:n]; si = stb[:, n - 1:n]
            k.ts(tA[:, 0:n], stb[:, 0:n], si, ALU.mult)
            k.stt(ct[:, n:2 * n], ct[:, 0:n], cr, tA[:, 0:n], ALU.mult, ALU.subtract)
            k.ts(tA[:, 0:n], stb[:, 0:n], cr, ALU.mult)
            k.stt(stb[:, n:2 * n], ct[:, 0:n], si, tA[:, 0:n], ALU.mult, ALU.add)
            n *= 2
        cosT.append(ct); sinT.append(stb)
        bre = s5b[:, st * 128:st * 128 + 64]; bim = s5b[:, st * 128 + 64:st * 128 + 128]
        bb = k.sb([128, 128], F32, "bb")
        k.ts(bb[:, 0:64], bim, fim, ALU.mult)
        k.stt(bb[:, 0:64], bre, fre, bb[:, 0:64], ALU.mult, ALU.subtract)
        k.ts(bb[:, 64:128], bre, fim, ALU.mult)
        k.stt(bb[:, 64:128], bim, fre, bb[:, 64:128], ALU.mult, ALU.add)
        bt = k.sb([64, 256], F32, "BbT")
        for ri in range(2):
            p = nps()
            k.transpose(p[0:64, 0:128], bb[:, ri * 64:(ri + 1) * 64], ident[:])
            k.copy(bt[:, ri * 128:(ri + 1) * 128], p[0:64, 0:128])
        BbT.append(bt)
        cb_ = k.sb([128, 128], F32, "Cbd")
        k.copy(cb_[:, 0:64], s5c[:, st * 128:st * 128 + 64])
        k.ts(cb_[:, 64:128], s5c[:, st * 128 + 64:st * 128 + 128], -1.0, ALU.mult)
        Cbd.append(cb_)
    s5carry = [[k.sb([128, 1], F32, "s5carry") for _ in range(2)] for _ in range(2)]
    for st in range(2):
        for ri in range(2):
            k.memset(s5carry[st][ri][:], 0.0)

    S = k.sb([128, 192], F32, "S"); k.memset(S[:], 0.0)
    Sb = k.sb([128, 192], BF16, "Sb"); k.memset(Sb[:], 0.0)
    cmask = k.sb([128, TB], F32, "cmask"); k.memset(cmask[:], 1.0)
    for c in range(TB // 128):
        k.memset(cmask[:, c * 128:c * 128 + 1], 0.0)

    xs = [k.sb([128, TB], F32, "xs") for _ in range(8)]
    sqb = [k.sb([128, TB], BF16, "sqb") for _ in range(2)]
    hb = [k.sb([128, TB], BF16, "hb") for _ in range(8)]
    rstd = k.sb([128, TB], F32, "rstd")
    alo = k.sb([16, TB], F32, "alo")
    lsp = k.sb([128, TB], F32, "lsp")
    cs = k.sb([128, TB], F32, "cs")
    eb = k.sb([128, TB], F32, "eb")
    enb = k.sb([128, TB], F32, "enb")
    qt = k.sb([128, TB], BF16, "qt")
    kt = k.sb([128, TB], BF16, "kt")
    ksb = k.sb([128, TB], F32, "ksb")
    ncl = k.sb([128, 4], F32, "ncl")
    vb = [k.sb([128, 192], BF16, "vb") for _ in range(4)]
    gsl = [k.sb([128, 192], F32, "gsl") for _ in range(4)]
    ekh = [k.sb([128, 128], F32, "ekh") for _ in range(2)]
    khT = [k.sb([128, 128], BF16, "khT") for _ in range(2)]
    kh = [k.sb([128, 128], BF16, "kh") for _ in range(2)]
    att = [k.sb([128, 128], BF16, "att") for _ in range(2)]
    osq = k.sb([128, 192], F32, "osq")
    ssq = [k.sb([128, 1], F32, "ssq") for _ in range(2)]
    yo = [k.sb([128, 192], F32, "yo") for _ in range(2)]
    us = k.sb([64, TB], F32, "us")
    burs = [[k.sb([128, TB], F32, "bur") for _ in range(2)] for _ in range(2)]
    xrs = [[k.sb([128, TB], F32, "xr") for _ in range(2)] for _ in range(2)]
    t1p = [k.sb([128, TB], F32, "t1p") for _ in range(2)]
    t1d = [k.sb([128, TB], F32, "t1d") for _ in range(2)]
    wscs = [[k.sb([128, TB], F32, "wsc") for _ in range(2)] for _ in range(2)]
    sre = [[k.sb([128, TB], F32, "sre") for _ in range(2)] for _ in range(2)]
    yz = k.sb([64, TB], F32, "yz")
    gz = [k.sb([64, TB], F32, "gz") for _ in range(2)]
    zo = [k.sb([64, TB], F32, "zo") for _ in range(2)]

    for bi in range(NB):
        t0 = bi * TB
        pss = nps()
        for kc in range(8):
            k.dma(xs[kc][:], xT[kc * 128:(kc + 1) * 128, t0:t0 + TB], q="sp", lane="xld")
            k.act(sqb[kc % 2][:], xs[kc][:], AF.Square)
            k.mm(pss[:], onesb[:], sqb[kc % 2][:], start=(kc == 0), stop=(kc == 7))
        rsqrt_small(k, rstd[:], pss[:], 1.0 / D, EPS)
        for kc in range(8):
            k.tt(hb[kc][:], xs[kc][:], rstd[:], ALU.mult, eng=("pool" if kc % 4 == 3 else "dve"))
        pq = nps(); pk = nps(); pu = nps(); pa = nps()
        for (p, c0, m) in ((pq, 0, 128), (pk, 128, 128), (pu, 256, 64), (pa, 320, 16)):
            for kc in range(8):
                k.mm(p[0:m, :], wfm[kc][:, c0:c0 + m], hb[kc][:], start=(kc == 0), stop=(kc == 7))
        if 'gates' in skip:
            continue
        k.copy(alo[:], pa[0:16, :], eng="act")
        if 'g1' in skip:
            continue
        pl = nps()
        k.mm(pl[:], wa2[:], alo[:])
        k.act(lsp[:], pl[:], AF.Exp, scale=-1.0, bias=nba[:, 0:1])
        k.act(lsp[:], lsp[:], AF.Ln, bias=1.0)
        if 'g2' in skip:
            continue
        k.scan(cs[:], cmask[:], lsp[:], 0.0)
        k.act(eb[:], cs[:], AF.Exp, scale=-1.0 / 16.0)
        k.act(enb[:], cs[:], AF.Exp, scale=1.0 / 16.0)
        k.stt(qt[:], pq[:], 128.0 ** -0.5, eb[:], ALU.mult, ALU.mult)
        k.tt(kt[:], pk[:], enb[:], ALU.mult)
        k.copy(ksb[:], pk[:], eng="act")
        for c in range(4):
            k.ts(ncl[:, c:c + 1], cs[:, c * 128 + 127:c * 128 + 128], -1.0 / 16.0, ALU.mult)
        if 'g3' in skip:
            continue
        k.copy(us[:], pu[0:64, :], eng="act")
        if 'g4' in skip:
            continue
        for s in range(4):
            pv = nps()
            for kc in range(8):
                k.mm(pv[:, 0:384], hb[kc][:, s * 128:(s + 1) * 128], wtm[kc][:], start=(kc == 0), stop=(kc == 7))
            k.copy(vb[s][:], pv[:, 0:192], eng="act")
            k.act(gsl[s][:], pv[:, 192:384], AF.Exp, scale=-1.0)
            recip_1p(k, gsl[s][:])
            k.tt(gsl[s][:], gsl[s][:], gn[:], ALU.mult, eng="pool")
            k.tt(gsl[s][:], gsl[s][:], pv[:, 192:384], ALU.mult)
        recA = k.record(); psel[0] = "A"
        for c in (range(4) if 'gla' not in skip else []):
            sl = slice(c * 128, (c + 1) * 128)
            i2 = c % 2
            k.act(ekh[i2][:], cs[:, sl], AF.Exp, scale=1.0 / 16.0, bias=ncl[:, c:c + 1])
            k.tt(khT[i2][:], ksb[:, sl], ekh[i2][:], ALU.mult, eng="pool")
            k.transpose(PSB[:, i2 * 128:(i2 + 1) * 128], khT[i2][:], identb[:])
            k.copy(kh[i2][:], PSB[:, i2 * 128:(i2 + 1) * 128], eng="act")
            pa_ = nps()
            k.mm(pa_[:, 0:128], kt[:, sl], qt[:, sl])
            k.tt(att[i2][:], pa_[:, 0:128], maskT[:], ALU.mult)
            po = nps()
            k.mm(po[:, 0:192], att[i2][:], vb[c][:], start=True, stop=False)
            k.mm(po[:, 0:192], qt[:, sl], Sb[:], start=False, stop=True)
            pst = nps()
            k.mm(pst[:, 0:192], kh[i2][:], vb[c][:])
            k.stt(S[:], S[:], eb[:, c * 128 + 127:c * 128 + 128], pst[:, 0:192], ALU.mult, ALU.add)
            k.copy(Sb[:], S[:], eng="act")
            k.act(osq[:], po[:, 0:192], AF.Square, accum=ssq[i2][:])
            rsqrt_small(k, ssq[i2][:], ssq[i2][:], 1.0 / 192.0, EPS)
            k.stt(yo[i2][:], po[:, 0:192], ssq[i2][:, 0:1], gsl[c][:], ALU.mult, ALU.mult)
            pt1 = nps(); k.transpose(pt1[:, 0:128], yo[i2][:, 0:128], ident[:])
            k.copy(yaTa[bi % 2][:, sl], pt1[:, 0:128], eng="act")
            pt2 = nps(); k.transpose(pt2[0:64, 0:128], yo[i2][:, 128:192], ident[:])
            k.copy(yaTb[bi % 2][:, sl], pt2[0:64, 0:128], eng="act")
            if c == 3:
                pr0 = (t0 // PT) * 256; pc0 = t0 % PT
                k.dma(P_o[pr0:pr0 + 128, pc0:pc0 + TB], yaTa[bi % 2][:], q="pool", lane="st")
                k.dma(P_o[pr0 + 128:pr0 + 192, pc0:pc0 + TB], yaTb[bi % 2][:], q="pool", lane="st")
        recB = k.record(); psel[0] = "B"
        for st in range(2):
            pbr = nps(); pbi = nps()
            k.mm(pbr[:], BbT[st][:, 0:128], us[:])
            k.mm(pbi[:], BbT[st][:, 128:256], us[:])
            bur = burs[st]; xr = xrs[st]; wsc = wscs[st]
            k.copy(bur[0][:], pbr[:], eng="act")
            k.copy(bur[1][:], pbi[:], eng="act")
            k.tt(t1p[0][:], bur[0][:], cosT[st][:], ALU.mult, eng="pool")
            k.tt(t1p[1][:], bur[1][:], sinT[st][:], ALU.mult, eng="pool")
            k.tt(xr[0][:], t1p[0][:], t1p[1][:], ALU.add, eng="pool")
            k.tt(t1d[0][:], bur[1][:], cosT[st][:], ALU.mult)
            k.tt(t1d[1][:], bur[0][:], sinT[st][:], ALU.mult)
            k.tt(xr[1][:], t1d[0][:], t1d[1][:], ALU.subtract)
            for ri in range(2):
                k.scan(wsc[ri][:], rho[st][:], xr[ri][:], s5carry[st][ri][:, 0:1])
            k.tt(t1p[0][:], wsc[0][:], cosT[st][:], ALU.mult, eng="pool")
            k.tt(t1p[1][:], wsc[1][:], sinT[st][:], ALU.mult, eng="pool")
            k.tt(sre[st][0][:], t1p[0][:], t1p[1][:], ALU.subtract, eng="pool")
            k.tt(t1d[0][:], wsc[0][:], sinT[st][:], ALU.mult)
            k.tt(t1d[1][:], wsc[1][:], cosT[st][:], ALU.mult)
            k.tt(sre[st][1][:], t1d[0][:], t1d[1][:], ALU.add)
            for ri in range(2):
                k.copy(s5carry[st][ri][:], sre[st][ri][:, TB - 1:TB], eng="act")
        py = nps()
        for st in range(2):
            for ri in range(2):
                k.mm(py[0:64, :], Cbd[st][:, ri * 64:(ri + 1) * 64], sre[st][ri][:], start=(st == 0 and ri == 0), stop=(st == 1 and ri == 1))
        k.stt(yz[:], us[:], s5d[:, 0:1], py[0:64, :], ALU.mult, ALU.add)
        g0 = gz[0]; g1 = gz[1]
        k.act(g0[:], yz[:], AF.Square)
        k.ts(g0[:], g0[:], 0.044715 * 0.7978845608028654, ALU.mult, 0.7978845608028654, ALU.add)
        k.tt(g0[:], g0[:], yz[:], ALU.mult)
        k.act(g1[:], g0[:], AF.Exp, scale=-2.0)
        recip_1p(k, g1[:])
        zz = zo[bi % 2]
        k.tt(zz[:], g1[:], yz[:], ALU.mult)
        pr0 = (t0 // PT) * 256; pc0 = t0 % PT
        k.dma(P_o[pr0 + 192:pr0 + 256, pc0:pc0 + TB], zz[:], q="pool", lane="st")
        k.stop_record(); psel[0] = None
        k.emit_merged(recA, recB)
        io["hook"](bi)

import math

D = 1024
EPS = 1e-6
GN_EPS = 64e-5
CW = 64
EM05 = math.exp(-0.5)


def b3(v):
    return V(v.buf, v.ap.unsqueeze(1).to_broadcast([128, 2, 64]))


def r3(v):
    return v.rearrange("p (a b) -> p a b", a=2)


def emit_A1(k, L, io, Tc, TB=512, PAD=2):
    skip = ()
    NB = L // TB
    NFM = 8 * 128
    x1G = io["x1G"]; w_fm = io["w_fm"]; w_tm = io["w_tm"]; nmix_d = io["nmix"]; rwp_d = io["rwp"]; w2_d = io["w2p"]; a2_d = io["a2p"]
    g2_d = io["g2c"]; lnw_d = io["lnw"]; lnb_d = io["lnb"]; mcw_d = io["mcw"]; mcb_d = io["mcb"]; mh_d = io["mh"]
    P_o = io["P"]; PT = io["PT"]; XB = io["XB"]

    ident, identb, maskT, ones, onesb = make_consts(k)
    bmask = k.sb([128, 128], F32, "bmask"); k.memset(bmask[:], 0.0); k.memset(bmask[0:64, 0:64], 1.0); k.memset(bmask[64:128, 64:128], 1.0)
    mI = k.sb([128, 128], F32, "mI"); k.tt(mI[:], maskT[:], bmask[:], ALU.mult)
    mS = k.sb([128, 128], F32, "mS"); k.tt(mS[:], mI[:], ident[:], ALU.subtract)
    mSl = k.sb([128, 128], F32, "mSl")
    pt_ = k.ps([128, 512], F32, "ptmp")
    k.transpose(pt_[:, 0:128], mS[:], ident[:]); k.copy(mSl[:], pt_[:, 0:128])
    E = k.sb([128, 64], F32, "E"); k.memset(E[:], 0.0)
    k.op("pool", lambda e: e.affine_select(out=E[:].ap, in_=E[:].ap, compare_op=ALU.not_equal, fill=1.0, base=0, pattern=[[-1, 64]], channel_multiplier=1), reads=[E], writes=[E])
    k.op("pool", lambda e: e.affine_select(out=E[:].ap, in_=E[:].ap, compare_op=ALU.not_equal, fill=1.0, base=-64, pattern=[[-1, 64]], channel_multiplier=1), reads=[E], writes=[E])

    ycT = [k.sb([64, 2 * TB], F32, "ycT") for _ in range(2)]
    ydT = [k.sb([128, TB], F32, "ydT") for _ in range(2)]
    nmix = k.sb([128, 8]); k.dma(nmix[:], nmix_d[:], lane="wld")
    rwp = k.sb([128, 16]); k.dma(rwp[:], rwp_d[:], lane="wld")
    w2p = k.sb([128, 128]); k.dma(w2p[:], w2_d[:], lane="wld")
    a2p = k.sb([128, 128]); k.dma(a2p[:], a2_d[:], lane="wld")
    g2c = k.sb([128, 128]); k.dma(g2c[:], g2_d[:], lane="wld")
    lnw = k.sb([128, 64]); k.dma(lnw[:], lnw_d[:], lane="wld")
    lnb = k.sb([128, 64]); k.dma(lnb[:], lnb_d[:], lane="wld")
    mcw = k.sb([128, 12]); k.dma(mcw[:], mcw_d[:], lane="wld")
    mcb = k.sb([128, 3]); k.dma(mcb[:], mcb_d[:], lane="wld")
    mh = k.sb([128, 6]); k.dma(mh[:], mh_d[:], lane="wld")
    omka = k.sb([128, 1]); k.ts(omka[:], rwp[:, 8:9], -1.0, ALU.mult, 1.0, ALU.add)
    nrwp = k.sb([128, 16]); k.ts(nrwp[:], rwp[:], -1.0, ALU.mult)
    negA = k.sb([128, 2]); k.act(negA[:], mh[:, 2:4], AF.Exp); k.ts(negA[:], negA[:], -1.0, ALU.mult)

    stage = [k.sb([128, 512], F32, "stage") for _ in range(2)]
    wfm = [k.sb([128, NFM], BF16, "wfm") for _ in range(8)]
    wtm = [k.sb([128, 130], BF16, "wtm") for _ in range(8)]
    cast_w(k, wfm, w_fm, NFM, nmix, stage)
    cast_w(k, wtm, w_tm, 130, nmix, stage)

    PS = [pt_] + [k.ps([128, 512], F32, "ps") for _ in range(7)]
    psi = [0]

    psel = [None]
    psa = [0]; psb = [0]
    NA = len(PS) - 3

    def nps():
        if psel[0] == "A":
            p = PS[psa[0] % NA]; psa[0] += 1
        elif psel[0] == "B":
            p = PS[NA + psb[0] % 3]; psb[0] += 1
        else:
            p = PS[psi[0] % len(PS)]; psi[0] += 1
        return p

    def T(shape, name, dt=F32):
        return k.sb(shape, dt, name)

    Hpk = T([128, 64], "Hpk"); k.memset(Hpk[:], 0.0)
    HT = T([128, 128], "HT"); k.memset(HT[:], 0.0)
    cmask = T([128, TB], "cmask"); k.memset(cmask[:], 1.0)
    for c in range(TB // CW):
        k.memset(cmask[:, c * CW:c * CW + 1], 0.0)
    xs = [T([128, TB], "xs") for _ in range(8)]
    sqb = [T([128, TB], "sqb", BF16) for _ in range(2)]
    hb = [T([128, TB], "hb", BF16) for _ in range(8)]
    rstd = T([128, TB], "rstd")
    Psb = [T([128, TB + 1], "Psb") for _ in range(5)]
    for t_ in Psb:
        k.memset(t_[:, 0:1], 0.0)
    PM = [T([128, TB], "PM") for _ in range(5)]
    dtmp = T([128, TB], "dtmp")
    Xc = [T([128, TB + 3], "Xc") for _ in range(3)]
    for t_ in Xc:
        k.memset(t_[:, 0:3], 0.0)
    cacc = T([128, TB], "cacc")
    XBC = [T([128, TB], "XBC") for _ in range(3)]
    th = T([128, TB], "th"); nlw = T([128, TB], "nlw"); asig = T([128, TB], "asig"); sgx = T([128, TB], "sgx")
    sgxd = T([128, 2 * TB], "sgxd")
    kk = T([128, TB], "kk"); kkn = T([128, TB], "kkn"); kmod = T([128, TB], "kmod"); ftmp = T([128, TB], "ftmp")
    cumn = T([128, TB], "cumn"); Ep = T([128, TB], "Ep"); En = T([128, TB], "En"); Eex = T([128, TB], "Eex")
    rt = T([128, TB], "rt"); kt = T([128, TB], "kt"); bt = T([128, TB], "bt"); at = T([128, TB], "at"); prod = T([128, TB], "prod")
    ztm = [T([128, 128], "ztm") for _ in range(4)]
    dtv = [T([128, 2], "dtv") for _ in range(4)]

    def blk(out, src_v, eng="pool"):
        k.tt(r3(out[:]), b3(src_v), r3(bmask[:]), ALU.mult, eng=eng)

    for bi in range(NB):
        t0 = bi * TB
        pss = nps()
        for kc in range(8):
            rr = t0 // Tc; i0 = (t0 % Tc) // XB
            for pi in range(TB // XB):
                gr = (i0 + pi) * 4096 + rr * 1024 + kc * 128
                k.dma(xs[kc][:, pi * XB:(pi + 1) * XB], x1G[gr:gr + 128, :], q="sp", lane="xld")
            k.act(sqb[kc % 2][:], xs[kc][:], AF.Square)
            k.mm(pss[:], onesb[:], sqb[kc % 2][:], start=(kc == 0), stop=(kc == 7))
        rsqrt_small(k, rstd[:], pss[:], 1.0 / D, EPS)
        for kc in range(8):
            k.tt(hb[kc][:], xs[kc][:], rstd[:], ALU.mult, eng=("pool" if kc % 4 == 3 else "dve"))
        for ti in range(8):
            p = nps()
            for kc in range(8):
                k.mm(p[:], wfm[kc][:, ti * 128:(ti + 1) * 128], hb[kc][:], start=(kc == 0), stop=(kc == 7))
            if ti < 5:
                P = Psb[ti]
                k.copy(P[:, 1:TB + 1], p[:], eng="act")
                k.tt(dtmp[:], P[:, 0:TB], P[:, 1:TB + 1], ALU.subtract)
                k.stt(PM[ti][:], dtmp[:], rwp[:, ti:ti + 1], P[:, 1:TB + 1], ALU.mult, ALU.add)
                k.copy(P[:, 0:1], P[:, TB:TB + 1], eng="pool")
            else:
                mi = ti - 5
                X = Xc[mi]
                k.copy(X[:, 3:TB + 3], p[:], eng="act")
                k.act(cacc[:], X[:, 0:TB], AF.Identity, scale=mcw[:, mi * 4:mi * 4 + 1])
                for j in range(1, 4):
                    k.stt(cacc[:], X[:, j:TB + j], mcw[:, mi * 4 + j:mi * 4 + j + 1], cacc[:], ALU.mult, ALU.add)
                k.ts(cacc[:], cacc[:], mcb[:, mi:mi + 1], ALU.add)
                k.act(XBC[mi][:], cacc[:], AF.Exp, scale=-1.0)
                recip_1p(k, XBC[mi][:])
                k.tt(XBC[mi][:], XBC[mi][:], cacc[:], ALU.mult, eng="pool")
                k.copy(X[:, 0:3], X[:, TB:TB + 3], eng="pool")
        for s in range(4):
            pz = nps()
            for kc in range(8):
                k.mm(pz[:, 0:130], hb[kc][:, s * 128:(s + 1) * 128], wtm[kc][:], start=(kc == 0), stop=(kc == 7))
            k.act(ztm[s][:], pz[:, 0:128], AF.Exp, scale=-1.0)
            recip_1p(k, ztm[s][:])
            k.tt(ztm[s][:], ztm[s][:], pz[:, 0:128], ALU.mult)
            k.tt(dtv[s][:], pz[:, 128:130], mh[:, 0:2], ALU.add)
            k.act(dtv[s][:], dtv[s][:], AF.Exp)
            k.act(dtv[s][:], dtv[s][:], AF.Ln, bias=1.0)
        recA = k.record(); psel[0] = "A"
        if 'rwkv' not in skip:
            k.act(th[0:64, :], PM[4][0:64, :], AF.Exp, scale=-2.0)
            recip_1p(k, th[0:64, :])
            k.ts(th[0:64, :], th[0:64, :], 2.0, ALU.mult, -1.0, ALU.add)
            k.copy(th[64:128, :], PM[4][64:128, :], eng="act")
            pw = nps(); k.mm(pw[:], w2p[:], th[:])
            k.act(nlw[:], pw[:], AF.Exp, scale=-1.0, bias=nrwp[:, 5:6])
            recip_1p(k, nlw[:])
            k.ts(nlw[:], nlw[:], EM05, ALU.mult)
            pa = nps(); k.mm(pa[:], a2p[:], th[:])
            k.act(asig[:], pa[:], AF.Exp, scale=-1.0, bias=nrwp[:, 6:7])
            recip_1p(k, asig[:])
            k.act(sgx[:], PM[3][:], AF.Exp, scale=-1.0)
            recip_1p(k, sgx[:])
            k.copy(V(sgxd, sgxd[:].ap.rearrange("p (c a t) -> p c a t", a=2, t=CW)),
                   V(sgx, sgx[:].ap.rearrange("p (c t) -> p c t", t=CW).unsqueeze(2).to_broadcast([128, TB // CW, 2, CW])), eng="pool")
            k.ts(kk[:], PM[1][:], rwp[:, 7:8], ALU.mult)
            k.tt(ftmp[:], kk[:], kk[:], ALU.mult, eng="pool")
            pn = nps(); k.mm(pn[:], bmask[:], ftmp[:])
            k.ts(ftmp[:], pn[:], 1e-24, ALU.max)
            k.act(ftmp[:], ftmp[:], AF.Ln)
            k.act(ftmp[:], ftmp[:], AF.Exp, scale=-0.5)
            k.tt(kkn[:], kk[:], ftmp[:], ALU.mult)
            k.ts(ftmp[:], asig[:], rwp[:, 8:9], ALU.mult, omka[:, 0:1], ALU.add)
            k.tt(kmod[:], PM[1][:], ftmp[:], ALU.mult)
            k.scan(cumn[:], cmask[:], nlw[:], 0.0)
            k.act(Ep[:], cumn[:], AF.Exp, scale=-1.0)
            k.act(En[:], cumn[:], AF.Exp)
            k.tt(Eex[:], nlw[:], cumn[:], ALU.subtract, eng="pool")
            k.act(Eex[:], Eex[:], AF.Exp)
            k.tt(rt[:], PM[0][:], Ep[:], ALU.mult)
            k.tt(kt[:], kmod[:], En[:], ALU.mult, eng="pool")
            k.tt(bt[:], kkn[:], asig[:], ALU.mult, eng="pool")
            k.tt(bt[:], bt[:], En[:], ALU.mult, eng="pool")
            k.stt(at[:], kkn[:], -1.0, Eex[:], ALU.mult, ALU.mult)
            k.stt(prod[:], PM[0][:], rwp[:, 9:10], kmod[:], ALU.mult, ALU.mult)
            GR = 4
            if bi == 0:
                CB = []
                for _g in range(GR):
                    d = {n: T([128, 128], n) for n in ("rB", "kB", "bB", "aB", "bH", "kH", "vB", "pB", "AakT", "ArbT", "ArkT", "BH", "KH", "yf")}
                    d.update({n: T([128, 128], n, BF16) for n in ("M", "N", "X", "XT", "X2", "XT2", "R", "R2")})
                    d["Xpkb"] = T([128, 64], "Xpkb", BF16)
                    d.update({n: T([128, 64], n) for n in ("Vpk", "Xpk", "Upk", "yc", "ysq", "yn", "yo")})
                    d.update({n: T([128, 1], n) for n in ("sum", "ssq", "rk")})
                    CB.append(d)
            for g0 in range(0, TB // CW, GR):
                grp = [(g0 + u, CB[u]) for u in range(GR)]
                for c, b_ in grp:
                    sl = slice(c * CW, (c + 1) * CW)
                    gC = Ep[:, c * CW + CW - 1:c * CW + CW]
                    blk(b_["rB"], rt[:, sl]); blk(b_["kB"], kt[:, sl]); blk(b_["bB"], bt[:, sl], eng="dve"); blk(b_["aB"], at[:, sl], eng="dve")
                    blk(b_["vB"], PM[2][:, sl]); blk(b_["pB"], prod[:, sl])
                    k.stt(r3(b_["bH"][:]), b3(bt[:, sl]), gC, r3(bmask[:]), ALU.mult, ALU.mult)
                    k.stt(r3(b_["kH"][:]), b3(kt[:, sl]), gC, r3(bmask[:]), ALU.mult, ALU.mult)
                for (la, ra, dst, msk) in (("bB", "aB", "M", mS), ("aB", "bB", "N", mSl), ("kB", "aB", "AakT", mS),
                                           ("bB", "rB", "ArbT", mI), ("kB", "rB", "ArkT", mI)):
                    for c, b_ in grp:
                        p = nps(); k.mm(p[:, 0:128], b_[la][:], b_[ra][:]); k.tt(b_[dst][:], p[:, 0:128], msk[:], ALU.mult)
                cur = {}
                for c, b_ in grp:
                    k.tt(b_["R"][:], b_["M"][:], identb[:], ALU.add, eng="pool")
                    b_["Rcur"] = b_["R"]
                    cur[c] = (b_["M"], b_["N"])
                for lev in range(1, 6):
                    for c, b_ in grp:
                        Xc_, XTc = cur[c]
                        Xn, XTn = ((b_["X"], b_["XT"]), (b_["X2"], b_["XT2"]))[lev % 2]
                        p1 = nps(); k.mm(p1[:, 0:128], Xc_[:], XTc[:]); k.copy(XTn[:], p1[:, 0:128], eng="act")
                        if lev < 5:
                            p2 = nps(); k.mm(p2[:, 0:128], XTc[:], Xc_[:]); k.copy(Xn[:], p2[:, 0:128], eng="act")
                        cur[c] = (Xn, XTn)
                    for c, b_ in grp:
                        Rc = b_["Rcur"]; Rn = b_["R2"] if Rc is b_["R"] else b_["R"]
                        p3 = nps()
                        k.mm(p3[:, 0:128], identb[:], Rc[:], start=True, stop=False)
                        k.mm(p3[:, 0:128], cur[c][1][:], Rc[:], start=False, stop=True)
                        k.copy(Rn[:], p3[:, 0:128], eng="act")
                        b_["Rcur"] = Rn
                for c, b_ in grp:
                    p = nps(); k.mm(p[:, 0:64], b_["vB"][:], E[:]); k.copy(b_["Vpk"][:], p[:, 0:64], eng="act")
                for c, b_ in grp:
                    p = nps(); k.transpose(p[:, 0:128], b_["bH"][:], ident[:]); k.copy(b_["BH"][:], p[:, 0:128], eng="act")
                for c, b_ in grp:
                    p = nps(); k.transpose(p[:, 0:128], b_["kH"][:], ident[:]); k.copy(b_["KH"][:], p[:, 0:128], eng="act")
                pys = {}
                for c, b_ in grp:
                    gC = Ep[:, c * CW + CW - 1:c * CW + CW]
                    p = nps()
                    k.mm(p[:, 0:64], b_["aB"][:], Hpk[:], start=True, stop=False)
                    k.mm(p[:, 0:64], b_["AakT"][:], b_["Vpk"][:], start=False, stop=True)
                    k.copy(b_["Xpkb"][:], p[:, 0:64])
                    p = nps(); k.mm(p[:, 0:64], b_["Rcur"][:], b_["Xpkb"][:]); k.copy(b_["Upk"][:], p[:, 0:64])
                    py = nps()
                    k.mm(py[:, 0:64], b_["rB"][:], Hpk[:], start=True, stop=False)
                    k.mm(py[:, 0:64], b_["ArbT"][:], b_["Upk"][:], start=False, stop=False)
                    k.mm(py[:, 0:64], b_["ArkT"][:], b_["Vpk"][:], start=False, stop=True)
                    ph = nps()
                    k.mm(ph[:, 0:64], b_["BH"][:], b_["Upk"][:], start=True, stop=False)
                    k.mm(ph[:, 0:64], b_["KH"][:], b_["Vpk"][:], start=False, stop=True)
                    k.stt(Hpk[:], Hpk[:], gC, ph[:, 0:64], ALU.mult, ALU.add)
                    k.act(b_["yc"][:], py[:, 0:64], AF.Identity, accum=b_["sum"][:])
                for c, b_ in grp:
                    k.ts(b_["sum"][:], b_["sum"][:], -1.0 / 64.0, ALU.mult)
                for c, b_ in grp:
                    k.act(b_["yc"][:], b_["yc"][:], AF.Identity, bias=b_["sum"][:, 0:1])
                for c, b_ in grp:
                    k.act(b_["ysq"][:], b_["yc"][:], AF.Square, accum=b_["ssq"][:])
                for c, b_ in grp:
                    k.ts(b_["ssq"][:], b_["ssq"][:], 1.0 / 64.0, ALU.mult, GN_EPS, ALU.add)
                for c, b_ in grp:
                    k.act(b_["ssq"][:], b_["ssq"][:], AF.Ln)
                for c, b_ in grp:
                    k.act(b_["ssq"][:], b_["ssq"][:], AF.Exp, scale=-0.5)
                for c, b_ in grp:
                    k.stt(b_["yn"][:], b_["yc"][:], b_["ssq"][:, 0:1], lnw[:], ALU.mult, ALU.mult)
                for c, b_ in grp:
                    k.tt(b_["yn"][:], b_["yn"][:], lnb[:], ALU.add, eng="pool")
                for c, b_ in grp:
                    p = nps(); k.mm(p[:, 0:2], b_["pB"][:], ones[:, 0:2]); k.copy(b_["rk"][:], p[:, 0:1], eng="act")
                for c, b_ in grp:
                    k.stt(b_["yn"][:], b_["Vpk"][:], b_["rk"][:, 0:1], b_["yn"][:], ALU.mult, ALU.add)
                for c, b_ in grp:
                    pg = nps(); k.mm(pg[:, 0:128], sgxd[:, c * 128:(c + 1) * 128], g2c[:])
                    k.tt(r3(b_["yf"][:]), b3(b_["yn"][:]), r3(pg[:, 0:128]), ALU.mult)
                for c, b_ in grp:
                    k.tt(r3(b_["yf"][:]), r3(b_["yf"][:]), r3(bmask[:]), ALU.mult, eng="pool")
                for c, b_ in grp:
                    k.tt(b_["yo"][:], b_["yf"][:, 0:64], b_["yf"][:, 64:128], ALU.add, eng="pool")
                for c, b_ in grp:
                    pT = nps(); k.transpose(pT[0:64, 0:128], b_["yo"][:], ident[:])
                    k.copy(V(ycT[bi % 2], ycT[bi % 2][:].ap.rearrange("p (h t) -> p h t", h=2)[:, :, c * CW:(c + 1) * CW]),
                           V(pT, pT[0:64, 0:128].ap.rearrange("p (h t) -> p h t", h=2)), eng="act")
            for hh in range(2):
                pr0 = (t0 // PT) * 256; pc0 = t0 % PT
                k.dma(P_o[pr0 + hh * 64:pr0 + (hh + 1) * 64, pc0:pc0 + TB], ycT[bi % 2][:, hh * TB:(hh + 1) * TB], q="pool", lane="st")
        recB = k.record(); psel[0] = "B"
        for c in range(4):
            sl = slice(c * 128, (c + 1) * 128)
            if bi == 0 and c == 0:
                mb = {n: T([128, 128], n) for n in ("xtm", "xdt", "xdd", "Btm", "abc0", "abc1", "df", "GTm", "WT0", "WT1", "ydg", "yy")}
                mb.update({n: T([128, 2], n) for n in ("a", "acs", "nacs", "tot", "eacs", "dec", "etot")})
            a = mb["a"]
            k.tt(a[:], dtv[c][:], negA[:], ALU.mult)
            pc = nps()
            k.mm(pc[:, 0:2], maskT[:], a[:])
            k.mm(pc[:, 2:4], ones[:], a[:])
            k.copy(mb["acs"][:], pc[:, 0:2], eng="act")
            k.copy(mb["tot"][:], pc[:, 2:4], eng="act")
            k.ts(mb["nacs"][:], mb["acs"][:], -1.0, ALU.mult)
            k.act(mb["eacs"][:], mb["acs"][:], AF.Exp)
            k.act(mb["etot"][:], mb["tot"][:], AF.Exp)
            k.tt(mb["dec"][:], mb["tot"][:], mb["acs"][:], ALU.subtract)
            k.act(mb["dec"][:], mb["dec"][:], AF.Exp)
            p = nps(); k.transpose(p[:, 0:128], XBC[0][:, sl], ident[:]); k.copy(mb["xtm"][:], p[:, 0:128], eng="act")
            p = nps(); k.transpose(p[:, 0:128], XBC[1][:, sl], ident[:]); k.copy(mb["Btm"][:], p[:, 0:128], eng="act")
            for h in range(2):
                hs = slice(h * 64, (h + 1) * 64)
                k.ts(mb["xdt"][:, hs], mb["xtm"][:, hs], dtv[c][:, h:h + 1], ALU.mult)
                k.ts(mb["xdd"][:, hs], mb["xdt"][:, hs], mb["dec"][:, h:h + 1], ALU.mult)
            pg = nps(); k.mm(pg[:, 0:128], XBC[1][:, sl], XBC[2][:, sl]); k.tt(mb["GTm"][:], pg[:, 0:128], maskT[:], ALU.mult)
            pyd = nps()
            for h in range(2):
                hs = slice(h * 64, (h + 1) * 64)
                abc = mb["abc%d" % h]
                k.ts(abc[:], ones[:], a[:, h:h + 1], ALU.mult)
                pr = nps(); k.mm(pr[:, 0:128], abc[:], maskT[:])
                k.ts(mb["df"][:], pr[:, 0:128], mb["nacs"][:, h:h + 1], ALU.add, 0.0, ALU.min)
                k.act(mb["df"][:], mb["df"][:], AF.Exp)
                WT = mb["WT%d" % h]
                k.tt(WT[:], mb["df"][:], mb["GTm"][:], ALU.mult)
                k.mm(pyd[:, hs], WT[:], mb["xdt"][:, hs])
            k.copy(mb["ydg"][:], pyd[:, 0:128], eng="act")
            po = nps(); k.mm(po[:, 0:128], XBC[2][:, sl], HT[:])
            for h in range(2):
                hs = slice(h * 64, (h + 1) * 64)
                k.stt(mb["yy"][:, hs], po[:, hs], mb["eacs"][:, h:h + 1], mb["ydg"][:, hs], ALU.mult, ALU.add)
                k.stt(mb["yy"][:, hs], mb["xtm"][:, hs], mh[:, 4 + h:5 + h], mb["yy"][:, hs], ALU.mult, ALU.add)
            k.tt(mb["yy"][:], mb["yy"][:], ztm[c][:], ALU.mult, eng="pool")
            pT = nps(); k.transpose(pT[:, 0:128], mb["yy"][:], ident[:])
            k.copy(ydT[bi % 2][:, sl], pT[:, 0:128], eng="act")
            if c == 3:
                pr0 = (t0 // PT) * 256; pc0 = t0 % PT
                k.dma(P_o[pr0 + 128:pr0 + 256, pc0:pc0 + TB], ydT[bi % 2][:], q="pool", lane="st")
            pst = nps(); k.mm(pst[:, 0:128], mb["Btm"][:], mb["xdd"][:])
            for h in range(2):
                hs = slice(h * 64, (h + 1) * 64)
                k.stt(HT[:, hs], HT[:, hs], mb["etot"][:, h:h + 1], pst[:, hs], ALU.mult, ALU.add)
        k.stop_record(); psel[0] = None
        k.emit_merged(recA, recB)
        io["hook"](bi)


D = 1024
DFF = 2816
NJ = DFF // 128
EPS = 1e-6


def load_cast_weight(k, dst_chunks, src, rows_kc, ncols, scale_cols=None, piece=704, engs=("dve",), stage=None, cnt=[0]):
    for kc in range(rows_kc):
        for c0 in range(0, ncols, piece):
            n = min(piece, ncols - c0)
            st = stage[cnt[0] % len(stage)]
            k.dma(st[:, 0:n], src[kc * 128:(kc + 1) * 128, c0:c0 + n], q="sp", lane="wld")
            eng = engs[cnt[0] % len(engs)]
            cnt[0] += 1
            if scale_cols is not None:
                if eng == "act":
                    k.op("act", lambda e: e.mul(dst_chunks[kc][:, c0:c0 + n].ap, st[:, 0:n].ap, scale_cols[:, kc:kc + 1].ap),
                         reads=[st, scale_cols], writes=[dst_chunks[kc]])
                else:
                    k.ts(dst_chunks[kc][:, c0:c0 + n], st[:, 0:n], scale_cols[:, kc:kc + 1], ALU.mult, eng=eng)
            else:
                k.copy(dst_chunks[kc][:, c0:c0 + n], st[:, 0:n], eng=eng)


def emit_B(k, layer, Tc, final, io, TB=256):
    H = 4 if layer == 0 else 2
    w_out = io["w_out"]; w_up = io["w_up"]; w_dn = io["w_dn"]; cw_d = io["cw"]; cb_d = io["cb"]; nf_d = io["nffn"]
    if final:
        nfin_d = io["nfin"]
    yloc = io["yloc"]
    G = io["G"]; PT = io["PT"]; NPc = Tc // PT; W = 4 + Tc
    XB = TB
    if layer == 0:
        wglu_d = io["wglu"]; bglu_d = io["bglu"]
        parts = [(0, 192, 0, 192), (192, 64, 768, 64)]
    else:
        mbn_d = io["mbn"]
        parts = [(0, 128, 0, 128), (128, 128, 512, 128)]
    dq = "act"
    for i in range(-1, NPc):
        for (sr, nr, dr, dstride) in parts:
            if i < 0:
                soff = 0 * 1024 * PT + sr * PT + (PT - H); ncol = H; dcol = 0
            else:
                soff = (1 + i) * 1024 * PT + sr * PT; ncol = PT; dcol = H + i * PT
            dst_ap = bass.AP(yloc.h.tensor, dr * W + dcol, [[dstride * W, 4], [W, nr], [1, ncol]])
            k.dyn_dma3(dst_ap, yloc, G, NPc * 1024 * PT, soff, [[256 * PT, 4], [PT, nr], [1, ncol]], ename=dq, lane="yloc")

    cw = k.sb([128, NJ * 3]); k.dma(cw[:], cw_d[:], lane="wld")
    cb = k.sb([128, NJ]); k.dma(cb[:], cb_d[:], lane="wld")
    nf = k.sb([128, 8]); k.dma(nf[:], nf_d[:], lane="wld")
    if final:
        nfin = k.sb([128, 8]); k.dma(nfin[:], nfin_d[:], lane="wld")
    ones = k.sb([128, 128]); k.memset(ones[:], 1.0)
    if layer == 0:
        bglu = k.sb([128, 2]); k.dma(bglu[:], bglu_d[:], lane="wld")
        hbglu = k.sb([128, 2]); k.ts(hbglu[:], bglu[:], 0.5, ALU.mult)
    else:
        mbn = k.sb([128, 4]); k.dma(mbn[:], mbn_d[:], lane="wld")

    stage = [k.sb([128, 704], F32, "stage") for _ in range(3)]
    wo = [k.sb([128, D], BF16, "wo") for _ in range(8)]
    wu = [k.sb([128, 2 * DFF], BF16, "wu") for _ in range(8)]
    wd = [k.sb([128, D], BF16, "wd") for _ in range(NJ)]
    load_cast_weight(k, wo, w_out, 8, D, stage=stage, piece=512)
    if layer == 0:
        wg = [k.sb([128, 256], BF16, "wg") for _ in range(2)]
        load_cast_weight(k, wg, wglu_d, 2, 256, stage=stage, piece=256)
    load_cast_weight(k, wu, w_up, 8, 2 * DFF, scale_cols=nf, stage=stage)
    load_cast_weight(k, wd, w_dn, NJ, D, stage=stage, piece=512)

    xs = [[k.sb([128, TB], F32, "xs") for _ in range(8)] for _ in range(1)]
    ys = [k.sb([128, TB], F32, "ys") for _ in range(3)]
    yb = [k.sb([128, TB], BF16, "yb") for _ in range(8)]
    xnb = [k.sb([128, TB], BF16, "xnb") for _ in range(8)]
    actb = [k.sb([128, TB], BF16, "actb") for _ in range(NJ)]
    G = [k.sb([128, TB + 2], F32, "G") for _ in range(2)]
    cv = [k.sb([128, TB], F32, "cv") for _ in range(2)]
    sg = [k.sb([128, TB], F32, "sg") for _ in range(2)]
    carry = [k.sb([128, 2], F32, "carry") for _ in range(NJ)]
    sq = [k.sb([128, TB], F32, "sq") for _ in range(2)]
    rstd = k.sb([128, TB], F32, "rstd")
    zs = [k.sb([128, TB], F32, "zs") for _ in range(4)]
    PS = [k.ps([128, 512], F32, "ps") for _ in range(8)]
    psi = [0]

    def nps():
        p = PS[psi[0] % 8]
        psi[0] += 1
        return p

    def rstd_from_sumsq(ps, n, inv_n, eps):
        k.ts(rstd[:, 0:n], ps[:, 0:n], inv_n, ALU.mult, eps, ALU.add)
        k.act(rstd[:, 0:n], rstd[:, 0:n], AF.Sqrt)
        k.op("dve", lambda e: e.reciprocal(rstd[:, 0:n].ap, rstd[:, 0:n].ap), reads=[rstd], writes=[rstd])

    if layer == 0:
        blocks = [(0, 2, True), (2, 2, False)] + [(4 + i * TB, TB, False) for i in range(Tc // TB)]
    else:
        blocks = [(0, 2, True)] + [(2 + i * TB, TB, False) for i in range(Tc // TB)]
    for bi, (c0, n, halo) in enumerate(blocks):
        X = xs[0]
        for dc in range(8):
            if layer == 0:
                xsrc = io["xTb"][dc * 128:(dc + 1) * 128, c0:c0 + n]
            elif c0 < 2:
                xsrc = io["x1halo"][dc * 128:(dc + 1) * 128, c0:c0 + n]
            else:
                pi_ = (c0 - 2) // XB
                xsrc = io["x1p"][pi_ * 1024 + dc * 128:pi_ * 1024 + (dc + 1) * 128, :]
            k.dma(X[dc][:, 0:n], xsrc, q="sp", lane="xld")
        for dc in range(8):
            raw = (layer == 0 and dc >= 6) or (layer == 1 and dc >= 4)
            if raw:
                st = zs[dc - 4] if layer == 1 else zs[dc - 6]
            else:
                st = ys[dc % 3]
            k.dma(st[:, 0:n], yloc[dc * 128:(dc + 1) * 128, c0:c0 + n], q="act", lane="yld")
            if not raw:
                k.copy(yb[dc][:, 0:n], st[:, 0:n], eng=("act" if dc % 2 else "dve"))
        if layer == 0:
            zb = [xnb[0], xnb[1]]
            for i in range(2):
                k.copy(zb[i][:, 0:n], zs[i][:, 0:n], eng="pool")
            for co in range(2):
                p = nps()
                for ci in range(2):
                    k.mm(p[:, 0:n], wg[ci][:, co * 128:(co + 1) * 128], zb[ci][:, 0:n], start=(ci == 0), stop=(ci == 1))
                k.act(sg[0][:, 0:n], p[:, 0:n], AF.Tanh, scale=0.5, bias=hbglu[:, co:co + 1])
                k.ts(sg[0][:, 0:n], sg[0][:, 0:n], 0.5, ALU.mult, 0.5, ALU.add)
                k.tt(yb[6 + co][:, 0:n], zs[co][:, 0:n], sg[0][:, 0:n], ALU.mult)
        else:
            for g in range(2):
                p = nps()
                for i in range(2):
                    k.act(sq[i][:, 0:n], zs[2 * g + i][:, 0:n], AF.Square)
                    k.mm(p[:, 0:n], ones[:], sq[i][:, 0:n], start=(i == 0), stop=(i == 1))
                rstd_from_sumsq(p, n, 1.0 / 256.0, EPS)
                for i in range(2):
                    k.stt(yb[4 + 2 * g + i][:, 0:n], zs[2 * g + i][:, 0:n], mbn[:, 2 * g + i:2 * g + i + 1], rstd[:, 0:n], ALU.mult, ALU.mult)
        for dc in range(8):
            p = nps()
            for kc in range(8):
                k.mm(p[:, 0:n], wo[kc][:, dc * 128:(dc + 1) * 128], yb[kc][:, 0:n], start=(kc == 0), stop=(kc == 7))
            k.tt(X[dc][:, 0:n], X[dc][:, 0:n], p[:, 0:n], ALU.add)
        p = nps()
        for dc in range(8):
            k.act(sq[dc % 2][:, 0:n], X[dc][:, 0:n], AF.Square)
            k.mm(p[:, 0:n], ones[:], sq[dc % 2][:, 0:n], start=(dc == 0), stop=(dc == 7))
        rstd_from_sumsq(p, n, 1.0 / D, EPS)
        for dc in range(8):
            k.tt(xnb[dc][:, 0:n], X[dc][:, 0:n], rstd[:, 0:n], ALU.mult, eng=("pool" if dc % 4 == 3 else "dve"))
        def stage_a(j):
            pg = nps()
            pv = nps()
            for kc in range(8):
                k.mm(pg[:, 0:n], wu[kc][:, j * 128:(j + 1) * 128], xnb[kc][:, 0:n], start=(kc == 0), stop=(kc == 7))
            for kc in range(8):
                k.mm(pv[:, 0:n], wu[kc][:, DFF + j * 128:DFF + (j + 1) * 128], xnb[kc][:, 0:n], start=(kc == 0), stop=(kc == 7))
            Gj = G[j % 2]
            k.copy(Gj[:, 2:2 + n], pg[:, 0:n], eng="act")
            if halo:
                k.copy(carry[j][:, 0:2], Gj[:, 2:4], eng="pool")
            else:
                k.copy(Gj[:, 0:2], carry[j][:, 0:2], eng="pool")
            return (j, pv)

        def stage_b(j):
            Gj = G[j % 2]
            c = cv[j % 2]
            k.act(c[:, 0:n], Gj[:, 0:n], AF.Identity, scale=cw[:, 3 * j:3 * j + 1])
            k.stt(c[:, 0:n], Gj[:, 1:1 + n], cw[:, 3 * j + 1:3 * j + 2], c[:, 0:n], ALU.mult, ALU.add)
            k.stt(c[:, 0:n], Gj[:, 2:2 + n], cw[:, 3 * j + 2:3 * j + 3], c[:, 0:n], ALU.mult, ALU.add)
            k.copy(carry[j][:, 0:2], Gj[:, n:n + 2], eng="pool")

        def stage_c(jj, pvv):
            s_ = sg[jj % 2]
            k.act(s_[:, 0:n], cv[jj % 2][:, 0:n], AF.Silu, bias=cb[:, jj:jj + 1])
            k.tt(actb[jj][:, 0:n], s_[:, 0:n], pvv[:, 0:n], ALU.mult)

        cur_a = stage_a(0)
        prev_a = None
        for j in range(NJ):
            nxt_a = stage_a(j + 1) if j + 1 < NJ else None
            if not halo:
                stage_b(j)
                if prev_a is not None:
                    stage_c(*prev_a)
            prev_a = cur_a
            cur_a = nxt_a
        if not halo:
            stage_c(*prev_a)
        if halo:
            continue
        for dc in range(8):
            p = nps()
            for j in range(NJ):
                k.mm(p[:, 0:n], wd[j][:, dc * 128:(dc + 1) * 128], actb[j][:, 0:n], start=(j == 0), stop=(j == NJ - 1))
            k.tt(X[dc][:, 0:n], X[dc][:, 0:n], p[:, 0:n], ALU.add)
        if final:
            p = nps()
            for dc in range(8):
                k.act(sq[dc % 2][:, 0:n], X[dc][:, 0:n], AF.Square)
                k.mm(p[:, 0:n], ones[:], sq[dc % 2][:, 0:n], start=(dc == 0), stop=(dc == 7))
            rstd_from_sumsq(p, n, 1.0 / D, EPS)
            for dc in range(8):
                k.stt(X[dc][:, 0:n], X[dc][:, 0:n], nfin[:, dc:dc + 1], rstd[:, 0:n], ALU.mult, ALU.mult)
        for dc in range(8):
            if layer == 1:
                dst = io["outT"][dc * 128:(dc + 1) * 128, c0 - 2:c0 - 2 + n]
            elif c0 < 4:
                dst = io["x1halo"][dc * 128:(dc + 1) * 128, 0:2]
            else:
                pi_ = (c0 - 4) // XB
                dst = io["x1p"][pi_ * 1024 + dc * 128:pi_ * 1024 + (dc + 1) * 128, :]
            k.dma(dst, X[dc][:, 0:n], q="pool", lane="st")
        if layer == 0 and c0 >= 4:
            io["hook"]((c0 - 4) // XB)


def colmajor(v, nch):
    return np.ascontiguousarray(np.asarray(v, np.float32).reshape(nch, 128).T)


def colmajor(v, nch):
    return np.ascontiguousarray(np.asarray(v, np.float32).reshape(nch, 128).T)

def prep_A0(inp, xT_b, h):
    W = inp["e_w_in"][0]
    nk = 512; nv = 768
    q = W[:, h * 128:(h + 1) * 128]
    kk = W[:, nk + h * 128: nk + (h + 1) * 128]
    v = W[:, 2 * nk + h * 192: 2 * nk + (h + 1) * 192]
    g = W[:, 2 * nk + nv + h * 192: 2 * nk + nv + (h + 1) * 192]
    alo = W[:, 2 * nk + 2 * nv: 2 * nk + 2 * nv + 16]
    GC = 2 * nk + 2 * nv + 16
    u = W[:, GC + h * 64: GC + (h + 1) * 64]
    m = {"xT": xT_b,
         "w_fm": np.ascontiguousarray(np.concatenate([q, kk, u, alo], 1)),
         "w_tm": np.ascontiguousarray(np.concatenate([v, g], 1)),
         "nmix": colmajor(inp["norm_mix"][0], 8),
         "wa2": np.ascontiguousarray(inp["e_gla_wa2"][0][:, h * 128:(h + 1) * 128]),
         "nba": np.ascontiguousarray(inp["e_gla_ba"][0][h * 128:(h + 1) * 128].reshape(128, 1)),
         "gnorm": np.ascontiguousarray(np.broadcast_to(inp["e_gla_norm"][0][None, :], (128, 192))),
         "s5d": np.ascontiguousarray(inp["e_s5_d"][0][h * 64:(h + 1) * 64].reshape(64, 1))}
    s5col = np.zeros((128, 6), np.float32)
    s5b = np.zeros((128, 256), np.float32)
    s5c = np.zeros((128, 256), np.float32)
    for st in range(2):
        for gi in range(2):
            gl = 2 * st + gi
            g_ = 4 * h + gl
            rows = slice(gi * 64, (gi + 1) * 64)
            s5col[rows, st * 3 + 0] = inp["e_s5_lambda_re"][0][g_]
            s5col[rows, st * 3 + 1] = inp["e_s5_lambda_im"][0][g_]
            s5col[rows, st * 3 + 2] = inp["e_s5_log_dt"][0][g_]
            s5b[rows, st * 128 + gl * 16: st * 128 + gl * 16 + 16] = inp["e_s5_b_re"][0][g_]
            s5b[rows, st * 128 + 64 + gl * 16: st * 128 + 64 + gl * 16 + 16] = inp["e_s5_b_im"][0][g_]
            s5c[rows, st * 128 + gl * 16: st * 128 + gl * 16 + 16] = inp["e_s5_c_re"][0][g_].T
            s5c[rows, st * 128 + 64 + gl * 16: st * 128 + 64 + gl * 16 + 16] = inp["e_s5_c_im"][0][g_].T
    m["s5col"] = s5col; m["s5b"] = s5b; m["s5c"] = s5c
    return m


def prep_A1(inp, xT_b, j):
    W = inp["o_w_in"][0]
    RW = 512
    cs = slice(j * 128, (j + 1) * 128)
    def col(base):
        return W[:, base + j * 128: base + (j + 1) * 128]
    r = col(0); kk = col(RW); v = col(2 * RW)
    o = 3 * RW
    xw = W[:, o:o + 64]; xa = W[:, o + 64:o + 128]; xg = W[:, o + 128:o + 256]
    RC = 3 * RW + 256
    z = W[:, RC + j * 128: RC + (j + 1) * 128]
    xb = RC + 512
    x = W[:, xb + j * 128: xb + (j + 1) * 128]
    g = j // 2
    Bm = W[:, xb + 512 + g * 128: xb + 512 + (g + 1) * 128]
    Cm = W[:, xb + 768 + g * 128: xb + 768 + (g + 1) * 128]
    dt = W[:, xb + 1024 + 2 * j: xb + 1024 + 2 * j + 2]
    mu = inp["o_rw_mu"][0]
    rwp = np.zeros((128, 16), np.float32)
    rwp[:, 0] = mu[0 + j * 128: 0 + (j + 1) * 128]
    rwp[:, 1] = mu[RW + j * 128: RW + (j + 1) * 128]
    rwp[:, 2] = mu[2 * RW + j * 128: 2 * RW + (j + 1) * 128]
    rwp[:, 3] = mu[o + 128:o + 256]
    rwp[:, 4] = mu[o:o + 128]
    rwp[:, 5] = inp["o_rw_w0"][0][cs]
    rwp[:, 6] = inp["o_rw_a0"][0][cs]
    rwp[:, 7] = inp["o_rw_k_k"][0][cs]
    rwp[:, 8] = inp["o_rw_k_a"][0][cs]
    rwp[:, 9] = inp["o_rw_r_k"][0].reshape(-1)[cs]
    w2p = np.zeros((128, 128), np.float32); w2p[0:64] = inp["o_rw_w2"][0][:, cs]
    a2p = np.zeros((128, 128), np.float32); a2p[64:128] = inp["o_rw_a2"][0][:, cs]
    lnw = np.repeat(inp["o_rw_ln_w"][0][cs].reshape(2, 1, 64), 64, axis=1).reshape(128, 64)
    lnb = np.repeat(inp["o_rw_ln_b"][0][cs].reshape(2, 1, 64), 64, axis=1).reshape(128, 64)
    cw = inp["o_mb_conv_w"][0]; cb = inp["o_mb_conv_b"][0]
    mcw = np.zeros((128, 12), np.float32); mcb = np.zeros((128, 3), np.float32)
    for ti, sl in enumerate((slice(j * 128, (j + 1) * 128), slice(512 + g * 128, 512 + (g + 1) * 128), slice(768 + g * 128, 768 + (g + 1) * 128))):
        mcw[:, ti * 4:(ti + 1) * 4] = cw[:, sl].T
        mcb[:, ti] = cb[sl]
    mh = np.zeros((128, 6), np.float32)
    mh[:, 0:2] = inp["o_mb_dt_bias"][0][2 * j:2 * j + 2]
    mh[:, 2:4] = inp["o_mb_a_log"][0][2 * j:2 * j + 2]
    mh[:, 4:6] = inp["o_mb_d"][0][2 * j:2 * j + 2]
    return {"xT": xT_b,
            "w_fm": np.ascontiguousarray(np.concatenate([r, kk, v, xg, xw, xa, x, Bm, Cm], 1)),
            "w_tm": np.ascontiguousarray(np.concatenate([z, dt], 1)),
            "nmix": colmajor(inp["norm_mix"][1], 8), "rwp": rwp, "w2p": w2p, "a2p": a2p,
            "g2c": np.ascontiguousarray(inp["o_rw_g2"][0][:, cs]),
            "lnw": np.ascontiguousarray(lnw), "lnb": np.ascontiguousarray(lnb), "mcw": mcw, "mcb": mcb, "mh": mh}

NCORE = 8
GROUPS = [[0, 1, 2, 3], [4, 5, 6, 7]]

A0_IN = {"w_fm": [1024, 336], "w_tm": [1024, 384], "nmix": [128, 8], "wa2": [16, 128], "nba": [128, 1], "gnorm": [128, 192],
         "s5col": [128, 6], "s5b": [128, 256], "s5c": [128, 256], "s5d": [64, 1]}
A1_IN = {"w_fm": [1024, 1024], "w_tm": [1024, 130], "nmix": [128, 8], "rwp": [128, 16], "w2p": [128, 128], "a2p": [128, 128],
         "g2c": [128, 128], "lnw": [128, 64], "lnb": [128, 64], "mcw": [128, 12], "mcb": [128, 3], "mh": [128, 6]}
B_IN = {"w_out": [1024, 1024], "w_up": [1024, 5632], "w_dn": [2816, 1024], "cw": [128, 66], "cb": [128, 22], "nffn": [128, 8]}


def build_fused(L, Bsz=2):
    nc = bass.Bass("TRN2", target_bir_lowering=False)
    k = KB(nc)
    Tc = (Bsz * L) // NCORE
    PT = min(1024, Tc)
    NP = L // PT
    XB = 256
    NXB = Tc // XB

    def ext(prefix, spec):
        return {n: k.dram(prefix + n, shp, F32, kind="ExternalInput") for n, shp in spec.items()}
    a0 = ext("a0_", A0_IN)
    a0["xT"] = k.dram("a0_xT", [1024, L], F32, kind="ExternalInput")
    P0 = k.dram("P0", [NP * 256, PT]); G0 = k.dram("G0", [(NP + 1) * 1024, PT])
    P1 = k.dram("P1", [NP * 256, PT]); G1 = k.dram("G1", [(NP + 1) * 1024, PT])
    x1p = k.dram("x1p", [NXB * 1024, XB]); x1G = k.dram("x1G", [NXB * 4096, XB])
    x1halo = k.dram("x1halo", [1024, 2]); yloc = k.dram("yloc", [1024, 4 + Tc])
    b0 = ext("b0_", dict(B_IN, wglu=[256, 256], bglu=[128, 2]))
    b0["xTb"] = k.dram("b0_xTb", [1024, 4 + Tc], F32, kind="ExternalInput")
    a1 = ext("a1_", A1_IN)
    b1 = ext("b1_", dict(B_IN, nfin=[128, 8], mbn=[128, 4]))
    b1["outT"] = k.dram("outT", [1024, Tc], F32, kind="ExternalOutput")

    def piece_hook(P, G):
        def hook(bi):
            t1 = (bi + 1) * 512
            if t1 % PT == 0:
                p = t1 // PT - 1
                k.wait_ring("pool", "st")
                k.allgather(V(G, G.h[(p + 1) * 1024:(p + 2) * 1024, :]), V(P, P.h[p * 256:(p + 1) * 256, :]), GROUPS)
        return hook

    def zero_pad_piece(G):
        zt = k.sb([128, 4], F32, "zt"); k.memset(zt[:], 0.0)
        for r0 in range(0, 1024, 128):
            k.dma(G[r0:r0 + 128, PT - 4:PT], zt[:], q="pool", lane="st")

    zero_pad_piece(G0)
    a0.update(P=P0, PT=PT, hook=piece_hook(P0, G0))
    emit_A0(k, L, a0)
    k.end_phase()
    def x1_hook(i):
        k.wait_ring("pool", "st")
        k.allgather(V(x1G, x1G.h[i * 4096:(i + 1) * 4096, :]), V(x1p, x1p.h[i * 1024:(i + 1) * 1024, :]), GROUPS)
    b0.update(G=G0, PT=PT, yloc=yloc, x1halo=x1halo, x1p=x1p, hook=x1_hook)
    emit_B(k, 0, Tc, False, b0)
    k.end_phase()
    zero_pad_piece(G1)
    a1.update(x1G=x1G, P=P1, PT=PT, XB=XB, hook=piece_hook(P1, G1))
    emit_A1(k, L, a1, Tc)
    k.end_phase()
    b1.update(G=G1, PT=PT, yloc=yloc, x1halo=x1halo, x1p=x1p)
    emit_B(k, 1, Tc, True, b1)
    k.end_phase()
    print("fused program instructions:", k.ninst)
    return nc


def _b_params(inp, layer, final):
    i = layer
    m = {"w_out": inp["e_w_out"][0] if layer == 0 else inp["o_w_out"][0],
         "w_up": inp["ffn_w_up"][i], "w_dn": inp["ffn_w_down"][i],
         "cw": np.ascontiguousarray(inp["ffn_conv_w"][i].reshape(3, 22, 128).transpose(2, 1, 0).reshape(128, 66)),
         "cb": colmajor(inp["ffn_conv_b"][i], 22), "nffn": colmajor(inp["norm_ffn"][i], 8)}
    if final:
        m["nfin"] = colmajor(inp["norm_final"], 8)
    if layer == 0:
        m["wglu"] = inp["e_s5_w_glu"][0]
        m["bglu"] = colmajor(inp["e_s5_b_glu"][0], 2)
    else:
        m["mbn"] = colmajor(inp["o_mb_norm"][0], 4)
    return m


_NC_CACHE = {}


def kernel(**inputs):
    inp = {k_: np.ascontiguousarray(np.asarray(v, dtype=np.float32)) for k_, v in inputs.items()}
    x = inp["x"]
    Bsz, L = x.shape[0], x.shape[1]
    Tc = (Bsz * L) // NCORE
    per_b = NCORE // Bsz
    if L not in _NC_CACHE:
        _NC_CACHE[L] = build_fused(L, Bsz)
    nc = _NC_CACHE[L]
    xT = [np.ascontiguousarray(x[b].T) for b in range(Bsz)]
    pb0 = _b_params(inp, 0, False)
    pb1 = _b_params(inp, 1, True)
    maps = []
    for c in range(NCORE):
        b, q = c // per_b, c % per_b
        m = {}
        a0 = prep_A0(inp, xT[b], q)
        m.update({"a0_" + n: v for n, v in a0.items()})
        a1 = prep_A1(inp, None, q)
        m.update({"a1_" + n: v for n, v in a1.items() if n != "xT"})
        m.update({"b0_" + n: v for n, v in pb0.items()})
        m.update({"b1_" + n: v for n, v in pb1.items()})
        t0 = q * Tc
        if t0 == 0:
            xb = np.concatenate([np.zeros((1024, 4), np.float32), xT[b][:, 0:Tc]], 1)
        else:
            xb = xT[b][:, t0 - 4:t0 + Tc]
        m["b0_xTb"] = np.ascontiguousarray(xb)
        maps.append(m)
    res = run_bass_kernel_spmd(nc, maps, core_ids=list(range(NCORE))).results
    out = np.empty((Bsz, L, 1024), np.float32)
    for c in range(NCORE):
        b, q = c // per_b, c % per_b
        out[b, q * Tc:(q + 1) * Tc, :] = res[c]["outT"].T
    return out
```
